# Optimizing a Trainium2 kernel written in Bass

```python
import jax
import jax.numpy as jnp
from jax import lax
import numpy as np

D_MODEL = 1024
BATCH = 32
SEQ = 2048
DEPTH = 2

CHUNK = 64
N_MEM = 256
GROUP_WIDTH = D_MODEL // 4
N_GROUPS = 5
D_MIX = N_GROUPS * GROUP_WIDTH
N_HEADS = 4
HEAD_DIM = GROUP_WIDTH // N_HEADS
POOL_WINDOWS = (2, 4, 8, 16)
POOL_CH = GROUP_WIDTH // len(POOL_WINDOWS)
Q_BLOCK = 128
EPS = 1e-6
NEG_BIG = -1e30
LB_FLOOR = 1e-30
IN_SPLITS = (GROUP_WIDTH,) * 4 + (N_HEADS,) + (GROUP_WIDTH,) * 12
D_IN = sum(IN_SPLITS)

kernel_name = "hybrid_fox_stickbreak_hgrn2_pool_memory"

F32 = jnp.float32


def _rms(x, g):
    xf = x.astype(F32)
    y = xf * lax.rsqrt(jnp.mean(xf * xf, axis=-1, keepdims=True) + EPS)
    return (y * g.astype(F32)).astype(x.dtype)


def _split_heads(t):
    b, s, _ = t.shape
    return t.reshape(b, s, N_HEADS, HEAD_DIM).transpose(0, 2, 1, 3)


def _merge_heads(t):
    b, h, s, d = t.shape
    return t.transpose(0, 2, 1, 3).reshape(b, s, h * d)


def _forgetting_attention(q, k, v, log_f):
    s_len = q.shape[2]
    c = jnp.cumsum(log_f, axis=-1)
    scale = HEAD_DIM ** -0.5
    outs = []
    for i in range(s_len // Q_BLOCK):
        t0, t1 = i * Q_BLOCK, (i + 1) * Q_BLOCK
        logits = jnp.einsum('bhtd,bhsd->bhts', q[:, :, t0:t1], k[:, :, :t1]).astype(F32) * scale
        logits = logits + c[:, :, t0:t1, None] - c[:, :, None, :t1]
        mask = jnp.arange(t1)[None, :] <= jnp.arange(t0, t1)[:, None]
        probs = jax.nn.softmax(jnp.where(mask, logits, NEG_BIG), axis=-1)
        outs.append(jnp.einsum('bhts,bhsd->bhtd', probs.astype(v.dtype), v[:, :, :t1]))
    return jnp.concatenate(outs, axis=2)


def _stick_breaking_attention(q, k, v):
    s_len = q.shape[2]
    scale = HEAD_DIM ** -0.5
    outs = []
    for i in range(s_len // Q_BLOCK):
        t0, t1 = i * Q_BLOCK, (i + 1) * Q_BLOCK
        z = jnp.einsum('bhtd,bhsd->bhts', q[:, :, t0:t1], k[:, :, :t1]).astype(F32) * scale
        mask = jnp.arange(t1)[None, :] < jnp.arange(t0, t1)[:, None]
        log_one_minus = jnp.where(mask, jax.nn.log_sigmoid(-z), 0.0)
        log_between = lax.cumsum(log_one_minus, axis=3, reverse=True) - log_one_minus
        log_w = jnp.where(mask, jax.nn.log_sigmoid(z) + log_between, NEG_BIG)
        weights = jnp.where(mask, jnp.exp(log_w), 0.0)
        outs.append(jnp.einsum('bhts,bhsd->bhtd', weights.astype(v.dtype), v[:, :, :t1]))
    return jnp.concatenate(outs, axis=2)


def _hgrn2(q, k, v, log_f):
    b, h, s_len, dk = q.shape
    dv = v.shape[-1]
    n_chunks = s_len // CHUNK

    def chunks(t):
        return t.astype(F32).reshape(b, h, n_chunks, CHUNK, t.shape[-1]).transpose(2, 0, 1, 3, 4)

    causal = jnp.tril(jnp.ones((CHUNK, CHUNK), dtype=bool))[:, :, None]

    def step(state, inp):
        qc, kc, vc, gc = inp
        bcum = jnp.cumsum(gc, axis=2)
        o_inter = jnp.einsum('bhtd,bhde->bhte', qc * jnp.exp(bcum), state)
        diff = bcum[:, :, :, None, :] - bcum[:, :, None, :, :]
        decay = jnp.where(causal, jnp.exp(jnp.where(causal, diff, 0.0)), 0.0)
        scores = jnp.einsum('bhtd,bhsd,bhtsd->bhts', qc, kc, decay)
        o_intra = jnp.einsum('bhts,bhse->bhte', scores, vc)
        b_last = bcum[:, :, -1, :]
        state = jnp.exp(b_last)[..., None] * state + jnp.einsum(
            'bhsd,bhse->bhde', kc * jnp.exp(b_last[:, :, None, :] - bcum), vc)
        return state, o_inter + o_intra

    state0 = jnp.zeros((b, h, dk, dv), F32)
    _, o = lax.scan(step, state0, (chunks(q), chunks(k), chunks(v), chunks(log_f)))
    return o.transpose(1, 2, 0, 3, 4).reshape(b, h, s_len, dv)


def _pool_mixer(u, w, scale):
    b, s_len, _ = u.shape
    n_g = len(POOL_WINDOWS)
    uf = u.astype(F32).reshape(b, s_len, n_g, POOL_CH)
    cs = jnp.cumsum(uf, axis=1)
    cs = jnp.concatenate([jnp.zeros_like(cs[:, :1]), cs], axis=1)
    pos = jnp.arange(1, s_len + 1, dtype=F32)
    means = []
    for gi, win in enumerate(POOL_WINDOWS):
        c = cs[:, :, gi]
        hi = c[:, 1:]
        lo = jnp.pad(c[:, :s_len + 1 - win], ((0, 0), (win - 1, 0), (0, 0)))
        means.append((hi - lo) / jnp.minimum(pos, win)[None, :, None])
    pooled = jnp.stack(means, axis=2)
    y = jnp.einsum('bsgc,gcd->bsgd', pooled - uf, w.astype(F32))
    y = y * scale.astype(F32).reshape(n_g, POOL_CH)
    return y.reshape(b, s_len, GROUP_WIDTH)


def _memory_attention(q, mem, mem_norm_g, mem_w_kv, q_norm, k_norm):
    mn = _rms(mem, mem_norm_g)
    kv = jnp.einsum('bmd,dn->bmn', mn, mem_w_kv)
    k, v = jnp.split(kv, 2, axis=-1)
    qh = _rms(_split_heads(q), q_norm)
    kh = _rms(_split_heads(k), k_norm)
    vh = _split_heads(v)
    logits = jnp.einsum('bhtd,bhmd->bhtm', qh, kh).astype(F32) * (HEAD_DIM ** -0.5)
    probs = jax.nn.softmax(logits, axis=-1)
    return _merge_heads(jnp.einsum('bhtm,bhmd->bhtd', probs.astype(vh.dtype), vh))


def _hybrid_layer(x, mem, norm_g, w_in, fox_f_bias, fox_q_norm, fox_k_norm, lower_bound,
                  hgrn_out_norm, pool_w, pool_scale, mem_norm_g, mem_w_kv, mem_q_norm,
                  mem_k_norm, w_out):
    h = _rms(x, norm_g)
    proj = jnp.einsum('bsd,dn->bsn', h, w_in)
    split_points = np.cumsum(IN_SPLITS)[:-1].tolist()
    (fq, fk, fv, fg, ff, sq, sk, sv, sg, hq, hf, hi, hg, pv, pg, mq, mg) = jnp.split(
        proj, split_points, axis=-1)

    log_f_fox = jax.nn.log_sigmoid((ff + fox_f_bias).astype(F32)).transpose(0, 2, 1)
    qa = _rms(_split_heads(fq), fox_q_norm)
    ka = _rms(_split_heads(fk), fox_k_norm)
    out_a = _merge_heads(_forgetting_attention(qa, ka, _split_heads(fv), log_f_fox))

    out_b = _merge_heads(_stick_breaking_attention(_split_heads(sq), _split_heads(sk), _split_heads(sv)))

    lb = lower_bound.astype(F32)
    hf32 = hf.astype(F32)
    log_lb = jnp.log(jnp.maximum(lb, LB_FLOOR))
    log_f_h = jnp.logaddexp(log_lb, jnp.log1p(-lb) + jax.nn.log_sigmoid(hf32))
    k_h = (1.0 - lb) * jax.nn.sigmoid(-hf32)
    oc = _hgrn2(_split_heads(jax.nn.silu(hq)), _split_heads(k_h), _split_heads(hi), _split_heads(log_f_h))
    oc = _rms(oc, hgrn_out_norm.reshape(N_HEADS, 1, HEAD_DIM))
    out_c = _merge_heads(oc)

    out_d = _pool_mixer(pv, pool_w, pool_scale)

    out_e = _memory_attention(mq, mem, mem_norm_g, mem_w_kv, mem_q_norm, mem_k_norm)

    mixed = jnp.concatenate([
        out_a.astype(x.dtype) * jax.nn.silu(fg),
        out_b.astype(x.dtype) * jax.nn.silu(sg),
        out_c.astype(x.dtype) * jax.nn.silu(hg),
        out_d.astype(x.dtype) * jax.nn.silu(pg),
        out_e.astype(x.dtype) * jax.nn.silu(mg),
    ], axis=-1)
    return x + jnp.einsum('bsn,nd->bsd', mixed, w_out).astype(x.dtype)


def setup_inputs(seed: int = 0) -> dict:
    key = jax.random.key(seed)
    ks = jax.random.split(key, 16)

    def nrm(k, shape, scale):
        return scale * jax.random.normal(k, shape, F32)

    return {
        "x": nrm(ks[0], (BATCH, SEQ, D_MODEL), 1.0),
        "mem": nrm(ks[1], (BATCH, N_MEM, D_MODEL), 1.0),
        "norm_g": 1.0 + nrm(ks[2], (DEPTH, D_MODEL), 0.02),
        "w_in": nrm(ks[3], (DEPTH, D_MODEL, D_IN), D_MODEL ** -0.5),
        "fox_f_bias": nrm(ks[4], (DEPTH, N_HEADS), 0.1),
        "fox_q_norm": 1.0 + nrm(ks[5], (DEPTH, HEAD_DIM), 0.02),
        "fox_k_norm": 1.0 + nrm(ks[6], (DEPTH, HEAD_DIM), 0.02),
        "hgrn_lb_logits": nrm(ks[7], (DEPTH, GROUP_WIDTH), 0.5),
        "hgrn_out_norm": 1.0 + nrm(ks[8], (DEPTH, GROUP_WIDTH), 0.02),
        "pool_w": nrm(ks[9], (DEPTH, len(POOL_WINDOWS), POOL_CH, POOL_CH), POOL_CH ** -0.5),
        "pool_scale": 1.0 + nrm(ks[10], (DEPTH, GROUP_WIDTH), 0.1),
        "mem_norm_g": 1.0 + nrm(ks[11], (DEPTH, D_MODEL), 0.02),
        "mem_w_kv": nrm(ks[12], (DEPTH, D_MODEL, 2 * GROUP_WIDTH), D_MODEL ** -0.5),
        "mem_q_norm": 1.0 + nrm(ks[13], (DEPTH, HEAD_DIM), 0.02),
        "mem_k_norm": 1.0 + nrm(ks[14], (DEPTH, HEAD_DIM), 0.02),
        "w_out": nrm(ks[15], (DEPTH, D_MIX, D_MODEL), D_MIX ** -0.5),
    }


def reference(x, mem, norm_g, w_in, fox_f_bias, fox_q_norm, fox_k_norm, hgrn_lb_logits,
              hgrn_out_norm, pool_w, pool_scale, mem_norm_g, mem_w_kv, mem_q_norm,
              mem_k_norm, w_out):
    p = jax.nn.softmax(hgrn_lb_logits.astype(F32), axis=0)
    lower_bounds = jnp.clip(jnp.cumsum(p, axis=0) - p[0:1], 0.0, 1.0 - 1e-6)
    for l in range(DEPTH):
        x = _hybrid_layer(x, mem, norm_g[l], w_in[l], fox_f_bias[l], fox_q_norm[l], fox_k_norm[l],
                          lower_bounds[l], hgrn_out_norm[l], pool_w[l], pool_scale[l],
                          mem_norm_g[l], mem_w_kv[l], mem_q_norm[l], mem_k_norm[l], w_out[l])
    return x
```

```python
import numpy as np
from contextlib import ExitStack
import concourse.bass as bass
import concourse.mybir as mybir
from concourse.bass_utils import run_bass_kernel_spmd

F32 = mybir.dt.float32
BF16 = mybir.dt.bfloat16
AF = mybir.ActivationFunctionType
ALU = mybir.AluOpType

ENGS = ("pe", "act", "dve", "pool", "sp")


class Op:
    __slots__ = ("eng", "fn", "reads", "writes", "chan", "deps", "mile", "cnt", "waits", "idx")

    def __init__(self, eng, fn, reads, writes, chan):
        self.eng, self.fn, self.reads, self.writes, self.chan = eng, fn, tuple(reads), tuple(writes), chan
        self.deps = []
        self.mile = False
        self.cnt = 0
        self.waits = []


class Sched:
    def __init__(self, nc):
        self.nc = nc
        self.ops = []

    def add(self, eng, fn, reads=(), writes=(), chan=None):
        op = Op(eng, fn, reads, writes, chan)
        op.idx = len(self.ops)
        self.ops.append(op)
        return op

    def pe(self, fn, reads=(), writes=()):
        return self.add("pe", fn, reads, writes)

    def act(self, fn, reads=(), writes=()):
        return self.add("act", fn, reads, writes)

    def dve(self, fn, reads=(), writes=()):
        return self.add("dve", fn, reads, writes)

    def pool(self, fn, reads=(), writes=()):
        return self.add("pool", fn, reads, writes)

    def dma(self, q, chan, out, in_, reads=(), writes=(), **kw):
        return self.add(q, lambda e: e.dma_start(out=out, in_=in_, **kw), reads, writes, chan=chan)

    def barrier(self):
        return self.add("bar", None)

    def resolve(self):
        ops = self.ops
        last_w = {}
        readers = {}
        last_comp = {}
        pend = {e: [] for e in ENGS}
        for op in ops:
            if op.eng == "bar":
                L = list(last_comp.values())
                for e in ENGS:
                    pend[e] = list(L)
                continue
            deps = set(pend[op.eng])
            pend[op.eng] = []
            for r in op.reads:
                w = last_w.get(r)
                if w is not None:
                    deps.add(w)
            for r in op.writes:
                w = last_w.get(r)
                if w is not None:
                    deps.add(w)
                for rd in readers.get(r, ()):
                    deps.add(rd)
            deps.discard(op.idx)
            op.deps = sorted(deps)
            for r in op.writes:
                last_w[r] = op.idx
                readers[r] = []
            for r in op.reads:
                if r not in op.writes:
                    readers.setdefault(r, []).append(op.idx)
            if op.chan is None:
                last_comp[op.eng] = op.idx
        chan_cnt = {}
        need = []
        for op in ops:
            lst = []
            if op.eng != "bar":
                for d in op.deps:
                    dop = ops[d]
                    if dop.chan is not None:
                        lst.append(("chan", dop.chan, chan_cnt[dop.chan]))
                    else:
                        if dop.eng == op.eng and op.chan is None:
                            if op.eng == "pe":
                                continue
                        dop.mile = True
                        lst.append(("eng", dop.eng, d))
            need.append(lst)
            if op.chan is not None:
                chan_cnt[op.chan] = chan_cnt.get(op.chan, 0) + 1
        ecnt = {e: 0 for e in ENGS}
        for op in ops:
            if op.eng != "bar" and op.chan is None and op.mile:
                ecnt[op.eng] += 1
                op.cnt = ecnt[op.eng]
        waited = {e: {} for e in ENGS}
        for op, lst in zip(ops, need):
            if op.eng == "bar":
                continue
            w = {}
            for kind, key, v in lst:
                if kind == "chan":
                    k = ("chan", key)
                    val = 16 * v
                else:
                    k = ("eng", key)
                    val = ops[v].cnt
                if val > w.get(k, 0):
                    w[k] = val
            out = []
            for k, val in w.items():
                if waited[op.eng].get(k, 0) >= val:
                    continue
                waited[op.eng][k] = val
                out.append((k, val))
            op.waits = out
        self.chans = sorted({op.chan for op in ops if op.chan is not None})

    def emit(self, stack):
        nc = self.nc
        self.resolve()
        sems = {}
        for e in ENGS:
            sems[("eng", e)] = stack.enter_context(nc.semaphore("s_" + e))
        for c in self.chans:
            sems[("chan", c)] = stack.enter_context(nc.semaphore("c_" + str(c)))
        per = {e: [op for op in self.ops if op.eng == e] for e in ENGS}
        block = stack.enter_context(nc.Block())

        def run(eng_handle, lst):
            for op in lst:
                for k, val in op.waits:
                    eng_handle.wait_ge(sems[k], val)
                if op.fn is None:
                    continue
                inst = op.fn(eng_handle)
                if op.chan is not None:
                    inst.then_inc(sems[("chan", op.chan)], 16)
                elif op.mile:
                    inst.then_inc(sems[("eng", op.eng)], 1)

        @block.tensor
        def _(e):
            run(e, per["pe"])

        @block.scalar
        def _(e):
            run(e, per["act"])

        @block.vector
        def _(e):
            run(e, per["dve"])

        @block.gpsimd
        def _(e):
            run(e, per["pool"])

        @block.sync
        def _(e):
            run(e, per["sp"])


import os
CSTOP = int(os.environ.get('CSTOP', '99'))
T = 2048
D = 1024
DIN = 4100
NB = 16
NT = 4
EPS = 1e-6
POOL_WINDOWS = (2, 4, 8, 16)

C_IDENT, C_NEGID, C_TRI, C_ONES, C_NEGM, C_MLT, C_NUINCL, C_TBD, C_TBDU, C_BONES, C_NONES = range(11)
C_MASK64 = 11 * 128
C_RC = C_MASK64 + 64
C_INVW = C_RC + 32
C_ZERO = C_INVW + 2
NCF = C_ZERO + 64


def make_cpack():
    p = np.arange(128)[:, None]
    c = np.arange(128)[None, :]
    cp = np.zeros((128, NCF), np.float32)

    def put(i, m):
        cp[:, i * 128:(i + 1) * 128] = m.astype(np.float32)

    put(C_IDENT, p == c)
    put(C_NEGID, -(p == c).astype(np.float32))
    put(C_TRI, p <= c)
    put(C_ONES, np.ones((128, 128)))
    put(C_NEGM, np.where(p > c, -1e30, 0.0))
    put(C_MLT, p < c)
    put(C_NUINCL, -(p >= c).astype(np.float32))
    same = (p // 64) == (c // 64)
    put(C_TBD, same & (p <= c))
    put(C_TBDU, same & (p > c))
    put(C_BONES, same)
    put(C_NONES, -np.ones((128, 128)))
    cp[:, C_MASK64:C_MASK64 + 64] = ((p % 64) <= np.arange(64)[None, :]).astype(np.float32)
    for ch in range(2):
        for half in range(2):
            win = POOL_WINDOWS[2 * ch + half]
            pos = np.arange(1, 17, dtype=np.float32)
            cp[half * 64:(half + 1) * 64, C_RC + ch * 16:C_RC + (ch + 1) * 16] = 1.0 / np.minimum(pos, win)[None, :]
            cp[half * 64:(half + 1) * 64, C_INVW + ch] = 1.0 / win
    return cp


class Arena:
    def __init__(self, ap, nbytes):
        self.ap, self.cap, self.off = ap, nbytes, 0
        self.peak = 0

    def alloc(self, shape, dtype):
        isz = 4 if dtype == F32 else 2
        n = int(np.prod(shape[1:]))
        nb = n * isz
        off = (self.off + 63) // 64 * 64
        assert off + nb <= self.cap, ("arena overflow", off + nb, self.cap)
        a = self.ap[:, off // 2:(off + nb) // 2]
        if dtype == F32:
            a = a.bitcast(F32)
        if len(shape) == 3:
            a = a.rearrange("p (c n) -> p c n", c=shape[1])
        elif len(shape) == 4:
            a = a.rearrange("p (a b n) -> p a b n", a=shape[1], b=shape[2])
        self.off = off + nb
        self.peak = max(self.peak, self.off)
        return a


def build_nc(NSEQ=4, NLAYER=2, dbg=False, groups="ABCDE"):
    nc = bass.Bass("TRN2", target_bir_lowering=False)

    def din(name, shape):
        return nc.dram_tensor(name, list(shape), F32, kind="ExternalInput").ap()

    x = din("x", [NSEQ, T, D])
    mem = din("mem", [NSEQ, 256, D])
    norm_g = din("norm_g", [2, D])
    w_in = din("w_in", [2, D, DIN])
    fox_f_bias = din("fox_f_bias", [2, 4])
    fox_q_norm = din("fox_q_norm", [2, 64])
    fox_k_norm = din("fox_k_norm", [2, 64])
    hgrn_lb_logits = din("hgrn_lb_logits", [2, 256])
    hgrn_out_norm = din("hgrn_out_norm", [2, 256])
    pool_w = din("pool_w", [2, 4, 64, 64])
    pool_scale = din("pool_scale", [2, 256])
    mem_norm_g = din("mem_norm_g", [2, D])
    mem_w_kv = din("mem_w_kv", [2, D, 512])
    mem_q_norm = din("mem_q_norm", [2, 64])
    mem_k_norm = din("mem_k_norm", [2, 64])
    w_out = din("w_out", [2, 1280, D])
    cpack = din("cpack", [128, NCF])
    y = nc.dram_tensor("y", [NSEQ, T, D], F32, kind="ExternalOutput").ap()
    xs = nc.dram_tensor("xs_scratch", [T, D], F32).ap()
    if dbg:
        dbg_out = nc.dram_tensor("dbg", [128, 10, T], BF16, kind="ExternalOutput").ap()

    st = ExitStack()
    S = Sched(nc)

    def sb(name, shape, dtype):
        return st.enter_context(nc.sbuf_tensor(name, list(shape), dtype))

    hT = sb("hT", [128, 8, T], BF16)
    mixT = sb("mixT", [128, 10, T], BF16)
    Wo = sb("Wo", [128, 10, D], BF16)
    ws = sb("ws", [128, 5, 8, 256], BF16)
    cf = sb("cf", [128, NCF], F32)
    cb = sb("cb", [128, NCF], BF16)
    g_b = sb("g_b", [128, D], F32)
    mg_b = sb("mg_b", [128, D], F32)
    xst = sb("xst", [128, 2, D], F32)
    hbuf = sb("hbuf", [128, 2, D], BF16)
    sqj = sb("sqj", [128, D], BF16)
    ssb = sb("ssb", [128, 32], F32)
    pv = sb("pv", [128, 8], F32)
    hn = sb("hn", [128, 2], F32)
    psc = sb("psc", [128, 2], F32)
    lbl = sb("lbl", [128, 2, 2], F32)
    lblb = sb("lblb", [128, 2, 256], F32)
    lbp = sb("lbp", [128, 4], F32)
    lb_b = sb("lb_b", [128, 256], F32)
    oml_b = sb("oml_b", [128, 256], F32)
    fb_b = sb("fb_b", [128, 4], F32)
    pwf = sb("pwf", [128, 2, 128], F32)
    pwb = sb("pwb", [128, 2, 128], BF16)
    kmT = sb("kmT", [128, 2, 256], BF16)
    Vm = sb("Vm", [128, 2, 256], BF16)
    ARENA_BYTES = 54 * 1024
    arena_t = sb("arena", [128, ARENA_BYTES // 2], BF16)
    ar = Arena(arena_t, ARENA_BYTES)

    PD = [st.enter_context(nc.psum_tensor(f"pd{i}", [128, 1024], F32)) for i in range(3)]
    PX = st.enter_context(nc.psum_tensor("px", [128, 512], F32))
    PTt = st.enter_context(nc.psum_tensor("ptt", [128, 1024], BF16))

    def bank(i):
        if i < 6:
            return PD[i // 2][:, (i % 2) * 512:(i % 2 + 1) * 512]
        return PX[:, :]

    def bres(i):
        return f"b{i}"

    def MM(out, lhsT, rhs, start, stop, r, w):
        S.pe(lambda e: e.matmul(out, lhsT=lhsT, rhs=rhs, start=start, stop=stop, skip_group_check=True), r, w)

    def ACT(out, in_, func, r, w, **kw):
        S.act(lambda e: e.activation(out=out, in_=in_, func=func, **kw), r, w)

    def TT(eng, out, in0, in1, op, r, w):
        S.add(eng, lambda e: e.tensor_tensor(out=out, in0=in0, in1=in1, op=op), r, w)

    def TS(eng, out, in0, s1, s2, op0, op1, r, w):
        if s2 is None:
            S.add(eng, lambda e: e.tensor_scalar(out=out, in0=in0, scalar1=s1, scalar2=None, op0=op0), r, w)
        else:
            S.add(eng, lambda e: e.tensor_scalar(out=out, in0=in0, scalar1=s1, scalar2=s2, op0=op0, op1=op1), r, w)

    def STT(eng, out, in0, scalar, in1, op0, op1, r, w):
        S.add(eng, lambda e: e.scalar_tensor_tensor(out=out, in0=in0, scalar=scalar, in1=in1, op0=op0, op1=op1), r, w)

    def COPY(eng, out, in_, r, w):
        S.add(eng, lambda e: e.tensor_copy(out=out, in_=in_), r, w)

    def RECIP(out, in_, r, w):
        S.dve(lambda e: e.reciprocal(out=out, in_=in_), r, w)

    def MEMSET(eng, ap, val, w):
        S.add(eng, lambda e: e.memset(ap, val), (), w)

    def cbm(i, n=128):
        return cb[:, i * 128:i * 128 + n]

    def cfm(i, n=128):
        return cf[:, i * 128:i * 128 + n]

    S.dma("sp", "cst", cf[:], cpack, writes=["cst"])
    COPY("dve", cb[:], cf[:], ["cst"], ["cst"])

    slab_free = [True] * 5

    class Slab:
        pass

    def slab_load(src3d, ncols):
        slot = slab_free.index(True)
        slab_free[slot] = False
        s = Slab()
        s.slot, s.res, s.ap = slot, f"ws{slot}", ws[:, slot]
        S.dma("pool", f"ws{slot}", ws[:, slot, :, 0:ncols], src3d, writes=[s.res])
        return s

    def slab_release(s):
        slab_free[s.slot] = True

    def win_cols(l, c0, n):
        return w_in[l].rearrange("(c p) n -> p c n", p=128)[:, :, c0:c0 + n]

    def wkv_cols(l, c0, n):
        return mem_w_kv[l].rearrange("(c p) n -> p c n", p=128)[:, :, c0:c0 + n]

    def proj_fm(s, col0, t, bk):
        for kc in range(8):
            MM(bank(bk), s.ap[:, kc, col0:col0 + 128], hT[:, kc, t * 512:(t + 1) * 512], kc == 0, kc == 7,
               [s.res, f"hT{t}"], [bres(bk)])

    def proj_tm(s, col0, ncols, blk, out_ap, ores):
        for kc in range(8):
            MM(out_ap, hT[:, kc, blk * 128:(blk + 1) * 128], s.ap[:, kc, col0:col0 + ncols], kc == 0, kc == 7,
               [s.res, f"hT{blk // 4}"], [ores])

    rot = {}

    def nxt(name="g"):
        rot[name] = rot.get(name, 0) + 1
        return rot[name]

    def headnorm(src_bk, N, gain_col, out_ap, out_res, stat_bk, hb):
        k = nxt() % 2
        sq, lnv = hb
        src = bank(src_bk)[:, 0:N]
        ACT(sq[:, k, 0:N], src, AF.Square, [bres(src_bk)], [f"hnsq{k}"])
        MM(bank(stat_bk)[:, 0:N], cbm(C_BONES), sq[:, k, 0:N], True, True, [f"hnsq{k}", "cst"], [bres(stat_bk)])
        ACT(lnv[:, k, 0:N], bank(stat_bk)[:, 0:N], AF.Ln, [bres(stat_bk)], [f"hnln{k}"], bias=EPS, scale=1.0 / 64)
        ACT(lnv[:, k, 0:N], lnv[:, k, 0:N], AF.Exp, [f"hnln{k}"], [f"hnln{k}"], scale=-0.5)
        STT("dve", out_ap, src, gain_col, lnv[:, k, 0:N], ALU.mult, ALU.mult,
            [bres(src_bk), f"hnln{k}", "pv"], [out_res])

    def load_params(l):
        S.dma("sp", "g_b", g_b[:], norm_g[l].partition_broadcast(128), writes=["g_b"])
        S.dma("sp", "mg_b", mg_b[:], mem_norm_g[l].partition_broadcast(128), writes=["mg_b"])
        for i, src in enumerate((fox_q_norm, fox_k_norm, mem_q_norm, mem_k_norm)):
            col = src[l].rearrange("(p o) -> p o", o=1)
            S.dma("sp", "pv", pv[0:64, i:i + 1], col, writes=["pv"])
            S.dma("sp", "pv", pv[64:128, i:i + 1], col, writes=["pv"])
        S.dma("sp", "hn", hn[:], hgrn_out_norm[l].rearrange("(c p) -> p c", p=128), writes=["hn"],
              allow_slow_non_contiguous=True)
        S.dma("sp", "psc", psc[:], pool_scale[l].rearrange("(c p) -> p c", p=128), writes=["psc"],
              allow_slow_non_contiguous=True)
        S.dma("sp", "fb_b", fb_b[:], fox_f_bias[l].partition_broadcast(128), writes=["fb_b"])
        MEMSET("dve", pwf[:], 0.0, ["pwf"])
        for g in range(4):
            hs = slice((g % 2) * 64, (g % 2) * 64 + 64)
            S.dma("sp", "pwf", pwf[hs, g // 2, (g % 2) * 64:(g % 2) * 64 + 64], pool_w[l, g], writes=["pwf"])
        COPY("dve", pwb[:], pwf[:], ["pwf"], ["pwb"])
        if l == 0:
            MEMSET("dve", lbp[:, 0:2], 0.0, ["lb"])
            MEMSET("dve", lbp[:, 2:4], 1.0, ["lb"])
            MEMSET("dve", lb_b[:], 0.0, ["lb"])
            MEMSET("dve", oml_b[:], 1.0, ["lb"])
        else:
            S.dma("sp", "lbl", lbl[:], hgrn_lb_logits.rearrange("l (c p) -> p l c", p=128), writes=["lbl"],
                  allow_slow_non_contiguous=True)
            S.dma("sp", "lblb", lblb[:], hgrn_lb_logits.partition_broadcast(128), writes=["lblb"])
            for (dst_lb, dst_oml, a0, a1) in ((lbp[:, 0:2], lbp[:, 2:4], lbl[:, 0, :], lbl[:, 1, :]),
                                             (lb_b[:], oml_b[:], lblb[:, 0, :], lblb[:, 1, :])):
                TT("dve", dst_oml, a1, a0, ALU.subtract, ["lbl", "lblb", "lb"], ["lb"])
                ACT(dst_oml, dst_oml, AF.Exp, ["lb"], ["lb"], scale=-1.0)
                TS("dve", dst_oml, dst_oml, 1.0, None, ALU.add, None, ["lb"], ["lb"])
                RECIP(dst_lb, dst_oml, ["lb"], ["lb"])
                TS("dve", dst_lb, dst_lb, 1.0 - 1e-6, None, ALU.min, None, ["lb"], ["lb"])
                TS("dve", dst_oml, dst_lb, -1.0, 1.0, ALU.mult, ALU.add, ["lb"], ["lb"])

    def rms_phase(b, l):
        src = x[b] if l == 0 else xs
        for blk in range(NB):
            k = nxt("xst") % 2
            xr = f"xst{k}"
            rd = [] if l == 0 else [f"xs{blk}"]
            S.dma("sp", xr, xst[:, k, :], src[blk * 128:(blk + 1) * 128, :], reads=rd, writes=[xr])
            sc = ssb[:, blk:blk + 1]
            rc_ = ssb[:, 16 + blk:17 + blk]
            ACT(sqj[:], xst[:, k, :], AF.Square, [xr], ["sqj", f"ss{blk}"], accum_out=sc)
            ACT(rc_, sc, AF.Ln, [f"ss{blk}"], [f"rs{blk}"], bias=EPS, scale=1.0 / D)
            ACT(rc_, rc_, AF.Exp, [f"rs{blk}"], [f"rs{blk}"], scale=-0.5)
            hk = blk % 2
            STT("dve", hbuf[:, hk, :], xst[:, k, :], rc_, g_b[:], ALU.mult, ALU.mult,
                [xr, f"rs{blk}", "g_b"], [f"hb{hk}"])
            for c in range(8):
                S.pe(lambda e, c=c, hk=hk: e.transpose(out=PTt[:, c * 128:(c + 1) * 128],
                                                       in_=hbuf[:, hk, c * 128:(c + 1) * 128], identity=cbm(C_IDENT)),
                     [f"hb{hk}", "cst"], ["b7"])
            ACT(hT[:, :, blk * 128:(blk + 1) * 128], PTt[:, :].rearrange("p (c n) -> p c n", c=8), AF.Copy,
                ["b7"], [f"hT{blk // 4}"])

    def attn_finish(h, c, qt, sg, mixc, with_den):
        pb = (h % 2) * 64
        hs = slice(pb, pb + 64)
        num = bank(4)[hs, :]
        mres = f"mx{mixc}_{qt}"
        if with_den:
            rec = ar_rec
            RECIP(rec[hs, :], bank(5)[hs, :], [bres(5)], ["rec"])
            TT("dve", rec[hs, :], num, rec[hs, :], ALU.mult, [bres(4), "rec"], ["rec"])
            TT("dve", mixT[hs, mixc, qt * 512:(qt + 1) * 512], rec[hs, :], sg[hs, c, qt * 512:(qt + 1) * 512], ALU.mult,
               ["rec", "sg"], [mres])
        else:
            TT("dve", mixT[hs, mixc, qt * 512:(qt + 1) * 512], num, sg[hs, c, qt * 512:(qt + 1) * 512], ALU.mult,
               [bres(4), "sg"], [mres])

    def gates_full(s, sg):
        for c in range(2):
            for t in range(NT):
                bk = 6
                proj_fm(s, c * 128, t, bk)
                ACT(sg[:, c, t * 512:(t + 1) * 512], bank(bk), AF.Silu, [bres(bk)], ["sg"])

    def group_A(b, l, sl):
        nonlocal ar_rec
        S.barrier()
        ar.off = 0
        qT = ar.alloc([128, 2, T], BF16)
        kT = ar.alloc([128, 2, T], BF16)
        V = ar.alloc([128, NB, 256], BF16)
        sg = ar.alloc([128, 2, T], BF16)
        csp = ar.alloc([128, NB, 4], F32)
        mark = ar.off
        hb = (ar.alloc([128, 2, 512], BF16), ar.alloc([128, 2, 512], F32))
        tmpf = ar.alloc([128, 64], F32)
        tot = ar.alloc([128, 64], F32)
        inc = ar.alloc([128, 64], F32)
        for (s, dst, gcol, nm) in ((sl["Aq"], qT, 0, "q"), (sl["Ak"], kT, 1, "k")):
            for c in range(2):
                for t in range(NT):
                    bk = nxt() % 2
                    proj_fm(s, c * 128, t, bk)
                    headnorm(bk, 512, pv[:, gcol:gcol + 1], dst[:, c, t * 512:(t + 1) * 512], f"A{nm}{c}_{t}", 2 + bk, hb)
            slab_release(s)
        for blk in range(NB):
            bk = 4 + nxt() % 2
            proj_tm(sl["Av"], 0, 256, blk, bank(bk)[:, 0:256], bres(bk))
            COPY("dve", V[:, blk, :], bank(bk)[:, 0:256], [bres(bk)], [f"AV{blk}"])
        slab_release(sl["Av"])
        for blk in range(NB):
            proj_tm(sl["Aff"], 0, 4, blk, bank(6)[:, blk * 4:(blk + 1) * 4], bres(6))
        slab_release(sl["Aff"])
        TT("dve", tmpf[:].rearrange("p (b h) -> p b h", h=4), bank(6)[:, 0:64].rearrange("p (b h) -> p b h", h=4),
           fb_b[:].unsqueeze(1).to_broadcast([128, NB, 4]), ALU.add, [bres(6), "fb_b"], ["tmpf"])
        gates_full(sl["Ag"], sg)
        slab_release(sl["Ag"])
        ACT(tmpf[:], tmpf[:], AF.Exp, ["tmpf"], ["tmpf"], scale=-1.0)
        ACT(tmpf[:], tmpf[:], AF.Ln, ["tmpf"], ["tmpf"], bias=1.0)
        MM(bank(0)[:, 0:64], cfm(C_TRI), tmpf[:], True, True, ["tmpf", "cst"], [bres(0)])
        MM(bank(1)[:, 0:64], cfm(C_ONES), tmpf[:], True, True, ["tmpf", "cst"], [bres(1)])
        COPY("dve", tot[:], bank(1)[:, 0:64], [bres(1)], ["tot"])
        for h in range(4):
            S.dve(lambda e, h=h: e.tensor_tensor_scan(
                out=inc[:].rearrange("p (b h) -> p b h", h=4)[:, :, h], data0=cfm(C_ONES, 16),
                data1=tot[:].rearrange("p (b h) -> p b h", h=4)[:, :, h], initial=0.0, op0=ALU.mult, op1=ALU.add),
                ["tot", "cst"], ["inc"])
        TT("dve", inc[:], inc[:], tot[:], ALU.subtract, ["inc", "tot"], ["inc"])
        TT("dve", csp[:].rearrange("p b h -> p (b h)"), bank(0)[:, 0:64], inc[:], ALU.add, [bres(0), "inc"], ["csp"])
        S.barrier()
        ar.off = mark
        CTn = ar.alloc([128, T], F32)
        argb = ar.alloc([128, 3, 512], F32)
        ptb = ar.alloc([128, 3, 512], BF16)
        ar_rec = ar.alloc([128, 512], F32)
        scale = 64 ** -0.5
        for h in range(4):
            c, pb = h // 2, (h % 2) * 64
            hs = slice(pb, pb + 64)
            for q4 in range(4):
                bk = 4 + q4 % 2
                for i in range(4):
                    blk = q4 * 4 + i
                    MM(bank(bk)[:, i * 128:(i + 1) * 128], csp[:, blk, h:h + 1].to_broadcast([128, 128]), cfm(C_NEGID),
                       True, True, ["csp", "cst"], [bres(bk)])
                ACT(CTn[:, q4 * 512:(q4 + 1) * 512], bank(bk), AF.Copy, [bres(bk)], ["CTn"])
            for qt in range(NT):
                nk = 4 * qt + 4
                for kb in range(nk):
                    off = max(0, kb - 4 * qt) * 128
                    sbk = nxt("sbk") % 4
                    k3 = nxt("k3") % 3
                    MM(bank(sbk)[:, off:512], kT[hs, c, kb * 128:(kb + 1) * 128], qT[hs, c, qt * 512 + off:(qt + 1) * 512],
                       True, True, [f"Ak{c}_{kb // 4}", f"Aq{c}_{qt}"], [bres(sbk)])
                    STT("dve", argb[:, k3, off:512], bank(sbk)[:, off:512], scale, CTn[:, qt * 512 + off:(qt + 1) * 512],
                        ALU.mult, ALU.add, [bres(sbk), "CTn"], [f"arg{k3}"])
                    if kb >= 4 * qt:
                        TT("dve", argb[:, k3, off:off + 128], argb[:, k3, off:off + 128], cfm(C_NEGM), ALU.add,
                           [f"arg{k3}", "cst"], [f"arg{k3}"])
                    ACT(ptb[:, k3, off:512], argb[:, k3, off:512], AF.Exp, [f"arg{k3}", "csp"], [f"pt{k3}"],
                        bias=csp[:, kb, h:h + 1])
                    MM(bank(4)[hs, off:512], V[:, kb, h * 64:(h + 1) * 64], ptb[:, k3, off:512], kb == 0, kb == nk - 1,
                       [f"AV{kb}", f"pt{k3}"], [bres(4)])
                    MM(bank(5)[hs, off:512], cbm(C_ONES, 64), ptb[:, k3, off:512], kb == 0, kb == nk - 1,
                       [f"pt{k3}", "cst"], [bres(5)])
                attn_finish(h, c, qt, sg, c, True)

    def group_B(b, l, sl):
        S.barrier()
        ar.off = 0
        qT = ar.alloc([128, 2, T], BF16)
        kT = ar.alloc([128, 2, T], BF16)
        V = ar.alloc([128, NB, 256], BF16)
        sg = ar.alloc([128, 2, T], BF16)
        eb = ar.alloc([128, 2, 512], F32)
        spb = ar.alloc([128, 2, 512], BF16)
        atb = ar.alloc([128, 2, 512], BF16)
        acc = ar.alloc([128, 512], BF16)
        z64 = ar.alloc([128, 64], BF16)
        MEMSET("dve", z64[:], 0.0, ["z64"])
        scale = 64 ** -0.5
        for (s, dst, nm, scl) in ((sl["Bq"], qT, "q", scale), (sl["Bk"], kT, "k", 1.0)):
            for c in range(2):
                for t in range(NT):
                    bk = nxt() % 2
                    proj_fm(s, c * 128, t, bk)
                    ACT(dst[:, c, t * 512:(t + 1) * 512], bank(bk), AF.Copy, [bres(bk)], [f"B{nm}{c}_{t}"], scale=scl)
            slab_release(s)
        for blk in range(NB):
            bk = 4 + nxt() % 2
            proj_tm(sl["Bv"], 0, 256, blk, bank(bk)[:, 0:256], bres(bk))
            COPY("dve", V[:, blk, :], bank(bk)[:, 0:256], [bres(bk)], [f"BV{blk}"])
        slab_release(sl["Bv"])
        gates_full(sl["Bg"], sg)
        slab_release(sl["Bg"])
        for h in range(4):
            c, pb = h // 2, (h % 2) * 64
            hs = slice(pb, pb + 64)
            for qt in range(NT):
                MEMSET("pool", acc[:], 0.0, ["acc"])
                MM(bank(4)[hs, :], z64[:], qT[:, c, qt * 512:(qt + 1) * 512], True, False, ["z64", f"Bq{c}_{qt}"], [bres(4)])
                for kb in range(4 * qt + 3, -1, -1):
                    off = max(0, kb - 4 * qt) * 128
                    diag = kb >= 4 * qt
                    k2 = nxt("k2") % 2
                    za = k2
                    zb = 2 + k2
                    kq = [f"Bk{c}_{kb // 4}", f"Bq{c}_{qt}"]
                    ksl = kT[hs, c, kb * 128:(kb + 1) * 128]
                    qsl = qT[hs, c, qt * 512 + off:(qt + 1) * 512]
                    MM(bank(za)[:, off:512], ksl, qsl, True, True, kq, [bres(za)])
                    ACT(eb[:, k2, off:512], bank(za)[:, off:512], AF.Exp, [bres(za)], [f"e{k2}"])
                    ACT(spb[:, k2, off:512], eb[:, k2, off:512], AF.Ln, [f"e{k2}"], [f"sp{k2}"], bias=1.0)
                    if diag:
                        TT("pool", spb[:, k2, off:off + 128], spb[:, k2, off:off + 128], cbm(C_MLT), ALU.mult,
                           [f"sp{k2}", "cst"], [f"sp{k2}"])
                    MM(bank(zb)[:, off:512], cbm(C_NUINCL), spb[:, k2, off:512], True, False, [f"sp{k2}", "cst"], [bres(zb)])
                    MM(bank(zb)[:, off:512], cbm(C_NONES), acc[:, off:512], False, False, ["acc", "cst"], [bres(zb)])
                    MM(bank(zb)[:, off:512], ksl, qsl, False, True, kq, [bres(zb)])
                    ACT(atb[:, k2, off:512], bank(zb)[:, off:512], AF.Exp, [bres(zb)], [f"at{k2}"])
                    if diag:
                        TT("pool", atb[:, k2, off:off + 128], atb[:, k2, off:off + 128], cbm(C_MLT), ALU.mult,
                           [f"at{k2}", "cst"], [f"at{k2}"])
                    if kb > 0:
                        TT("dve", acc[:, off:512], acc[:, off:512], spb[:, k2, off:512], ALU.add, ["acc", f"sp{k2}"], ["acc"])
                    MM(bank(4)[hs, off:512], V[:, kb, h * 64:(h + 1) * 64], atb[:, k2, off:512], False, kb == 0,
                       [f"BV{kb}", f"at{k2}"], [bres(4)])
                attn_finish(h, c, qt, sg, 2 + c, False)

    def group_C(b, l, sl):
        S.barrier()
        ar.off = 0
        Sall = ar.alloc([128, 9, 2, 64], F32)
        MEMSET("dve", Sall[:, 0], 0.0, ["S0"])
        mark = ar.off
        for t in range(NT):
            S.barrier()
            ar.off = mark
            e_s = ar.alloc([128, 1024], F32)
            f_s = ar.alloc([128, 1024], F32)
            k_s = ar.alloc([128, 1024], F32)
            lf_hi = ar.alloc([128, 1024], BF16)
            lf_lo = ar.alloc([128, 1024], BF16)
            khat = ar.alloc([128, 4, 256], BF16)
            Eq = ar.alloc([128, 2, 512], F32)
            Ek = ar.alloc([128, 2, 512], F32)
            sq_s = ar.alloc([128, 2, 512], F32)
            qt_ = ar.alloc([128, 2, 512], BF16)
            kt_ = ar.alloc([128, 2, 512], BF16)
            Vc = ar.alloc([128, 4, 256], BF16)
            scm = ar.alloc([128, 16, 64], BF16)
            Sb = ar.alloc([128, 8, 2, 64], BF16)
            sqo = ar.alloc([128, 1024], BF16)
            sgc = ar.alloc([128, 512], F32)
            P0, P1, P2 = PD[0], PD[1], PD[2]
            b01, b23, b45 = [bres(0), bres(1)], [bres(2), bres(3)], [bres(4), bres(5)]
            for i in range(4):
                proj_tm(sl["Cf"], 0, 256, 4 * t + i, P0[:, i * 256:(i + 1) * 256], bres(i // 2))
            ACT(e_s[:], P0[:, :], AF.Exp, b01, ["e_s"], scale=-1.0)
            TS("dve", e_s[:], e_s[:], 1.0, None, ALU.add, None, ["e_s"], ["e_s"])
            RECIP(e_s[:], e_s[:], ["e_s"], ["e_s"])
            omb = oml_b[:].unsqueeze(1).to_broadcast([128, 4, 256])
            lbb = lb_b[:].unsqueeze(1).to_broadcast([128, 4, 256])
            v3 = lambda a: a.rearrange("p (i n) -> p i n", i=4)
            TT("dve", v3(e_s[:]), v3(e_s[:]), omb, ALU.mult, ["e_s", "lb"], ["e_s"])
            TT("dve", v3(f_s[:]), v3(e_s[:]), lbb, ALU.add, ["e_s", "lb"], ["f_s"])
            TT("dve", v3(k_s[:]), omb, v3(e_s[:]), ALU.subtract, ["e_s", "lb"], ["k_s"])
            ACT(f_s[:], f_s[:], AF.Ln, ["f_s"], ["f_s"])
            COPY("dve", lf_hi[:], f_s[:], ["f_s"], ["lf_hi"])
            TT("dve", lf_lo[:], f_s[:], lf_hi[:], ALU.subtract, ["f_s", "lf_hi"], ["lf_lo"])
            if CSTOP < 2:
                continue
            for i in range(4):
                o = P1[:, i * 256:(i + 1) * 256]
                MM(o, cbm(C_TBDU), lf_hi[:, i * 256:(i + 1) * 256], True, False, ["lf_hi", "cst"], [bres(2 + i // 2)])
                MM(o, cbm(C_TBDU), lf_lo[:, i * 256:(i + 1) * 256], False, True, ["lf_lo", "cst"], [bres(2 + i // 2)])
            ACT(e_s[:], P1[:, :], AF.Exp, b23, ["e_s"])
            TT("dve", khat[:].rearrange("p i n -> p (i n)"), k_s[:], e_s[:], ALU.mult, ["k_s", "e_s"], ["khat"])
            for c in range(2):
                for i in range(4):
                    o = P2[:, c * 512 + i * 128:c * 512 + (i + 1) * 128]
                    MM(o, lf_hi[:, i * 256 + c * 128:i * 256 + (c + 1) * 128], cbm(C_TBD), True, False,
                       ["lf_hi", "cst"], [bres(4 + c)])
                    MM(o, lf_lo[:, i * 256 + c * 128:i * 256 + (c + 1) * 128], cbm(C_TBD), False, True,
                       ["lf_lo", "cst"], [bres(4 + c)])
            ACT(Eq[:].rearrange("p c n -> p (c n)"), P2[:, :], AF.Exp, b45, ["Eq"])
            ACT(Ek[:].rearrange("p c n -> p (c n)"), P2[:, :], AF.Exp, b45, ["Ek"], scale=-1.0)
            if CSTOP < 3:
                continue
            for c in range(2):
                bk = 6
                proj_fm(sl["Cq"], c * 128, t, bk)
                ACT(sq_s[:, c, :], bank(bk), AF.Silu, [bres(bk)], [f"sq_s{c}"])
                TT("dve", qt_[:, c, :], sq_s[:, c, :], Eq[:, c, :], ALU.mult, [f"sq_s{c}", "Eq"], ["qt_"])
            for c in range(2):
                bk = 6
                proj_fm(sl["Cf"], c * 128, t, bk)
                ACT(sq_s[:, c, :], bank(bk), AF.Exp, [bres(bk)], [f"sq_s{c}"])
                TS("dve", sq_s[:, c, :], sq_s[:, c, :], 1.0, None, ALU.add, None, [f"sq_s{c}"], [f"sq_s{c}"])
                RECIP(sq_s[:, c, :], sq_s[:, c, :], [f"sq_s{c}"], [f"sq_s{c}"])
                STT("dve", kt_[:, c, :], sq_s[:, c, :], lbp[:, 2 + c:3 + c], Ek[:, c, :], ALU.mult, ALU.mult,
                    [f"sq_s{c}", "Ek", "lb"], ["kt_"])
            if CSTOP < 4:
                continue
            for i in range(4):
                proj_tm(sl["Ci"], 0, 256, 4 * t + i, P0[:, i * 256:(i + 1) * 256], bres(i // 2))
            COPY("dve", Vc[:].rearrange("p i n -> p (i n)"), P0[:, :], b01, ["Vc"])
            if CSTOP < 5:
                continue
            for hp in range(2):
                pb = hp * 64
                for ch in range(8):
                    i, half = ch // 2, ch % 2
                    for j in range(2):
                        slot = i * 2 + j
                        MM(P1[half * 64:(half + 1) * 64, hp * 512 + slot * 64:hp * 512 + (slot + 1) * 64],
                           kt_[pb:pb + 64, j, ch * 64:(ch + 1) * 64], qt_[pb:pb + 64, j, ch * 64:(ch + 1) * 64],
                           True, True, ["kt_", "qt_"], [bres(2 + hp)])
            TT("dve", scm[:], P1[:, :].rearrange("p (s n) -> p s n", n=64),
               cb[:, C_MASK64:C_MASK64 + 64].unsqueeze(1).to_broadcast([128, 16, 64]), ALU.mult, b23 + ["cst"], ["scm"])
            if CSTOP < 6:
                continue
            def uslot(ch, j):
                return (ch % 2) * 512 + ((ch // 2) * 2 + j) * 64
            for half in range(2):
                for ch in range(half, 8, 2):
                    i = ch // 2
                    for h in range(4):
                        j = h // 2
                        o0 = uslot(ch, j)
                        MM(P0[(h % 2) * 64:(h % 2) * 64 + 64, o0:o0 + 64],
                           khat[half * 64:(half + 1) * 64, i, h * 64:(h + 1) * 64],
                           Vc[half * 64:(half + 1) * 64, i, h * 64:(h + 1) * 64], True, True, ["khat", "Vc"], [bres(half)])
            if CSTOP < 7:
                continue
            for ch in range(8):
                for j in range(2):
                    o0 = uslot(ch, j)
                    STT("dve", Sall[:, ch + 1, j, :], Sall[:, ch, j, :], Eq[:, j, ch * 64 + 63:ch * 64 + 64],
                        P0[:, o0:o0 + 64], ALU.mult, ALU.add, [f"S{ch}", "Eq", bres(ch % 2)], [f"S{ch + 1}"])
            ACT(Sb[:].rearrange("p a b n -> p (a b n)"), Sall[:, 0:8].rearrange("p a b n -> p (a b n)"), AF.Copy,
                [f"S{k}" for k in range(8)], ["Sb"])
            COPY("dve", Sall[:, 0].rearrange("p b n -> p (b n)"), Sall[:, 8].rearrange("p b n -> p (b n)"), ["S8"], ["S0"])
            if CSTOP < 8:
                continue
            for hp in range(2):
                pb = hp * 64
                for j in range(2):
                    for ch in range(8):
                        MM(P2[pb:pb + 64, j * 512 + ch * 64:j * 512 + (ch + 1) * 64], Sb[pb:pb + 64, ch, j, :],
                           qt_[pb:pb + 64, j, ch * 64:(ch + 1) * 64], True, True, ["Sb", "qt_"], [bres(4 + j)])
            for half in range(2):
                for ch in range(half, 8, 2):
                    i = ch // 2
                    for h in range(4):
                        j, hp = h // 2, h % 2
                        o0 = half * 512 + (j * 4 + i) * 64
                        MM(P0[hp * 64:(hp + 1) * 64, o0:o0 + 64],
                           Vc[half * 64:(half + 1) * 64, i, h * 64:(h + 1) * 64],
                           scm[half * 64:(half + 1) * 64, hp * 8 + i * 2 + j, :], True, True, ["Vc", "scm"], [bres(half)])
            for half in range(2):
                COPY("dve", k_s[:].rearrange("p (j i h t) -> p j i h t", j=2, i=4, h=2)[:, :, :, half, :],
                     P0[:, half * 512:(half + 1) * 512].rearrange("p (j i t) -> p j i t", j=2, i=4), [bres(half)], ["k_s"])
            TT("dve", f_s[:], P2[:, :], k_s[:], ALU.add, b45 + ["k_s"], ["f_s"])
            if CSTOP < 9:
                continue
            ACT(sqo[:], f_s[:], AF.Square, ["f_s"], ["sqo"])
            for j in range(2):
                MM(bank(2 + j), cbm(C_BONES), sqo[:, j * 512:(j + 1) * 512], True, True, ["sqo", "cst"], [bres(2 + j)])
            ACT(e_s[:], P1[:, :], AF.Ln, b23, ["e_s"], bias=EPS, scale=1.0 / 64)
            ACT(e_s[:], e_s[:], AF.Exp, ["e_s"], ["e_s"], scale=-0.5)
            for j in range(2):
                STT("dve", f_s[:, j * 512:(j + 1) * 512], f_s[:, j * 512:(j + 1) * 512], hn[:, j:j + 1],
                    e_s[:, j * 512:(j + 1) * 512], ALU.mult, ALU.mult, ["f_s", "e_s", "hn"], ["f_s"])
                proj_fm(sl["Cg"], j * 128, t, 6)
                ACT(sgc[:], bank(6), AF.Silu, [bres(6)], ["sgc"])
                TT("dve", mixT[:, 4 + j, t * 512:(t + 1) * 512], f_s[:, j * 512:(j + 1) * 512], sgc[:], ALU.mult,
                   ["f_s", "sgc"], [f"mx{4 + j}_{t}"])
        for k in ("Cq", "Cf", "Ci", "Cg"):
            slab_release(sl[k])

    def group_D(b, l, sl):
        S.barrier()
        ar.off = 0
        PADW = 16 + T
        sgd = ar.alloc([128, 512], F32)
        for c in range(2):
            S.barrier()
            ar.off = 2048
            u = ar.alloc([128, PADW], F32)
            A1 = ar.alloc([128, PADW], F32)
            A2 = ar.alloc([128, PADW], F32)
            dfb = ar.alloc([128, T], BF16)
            tmp = ar.alloc([128, 16], F32)
            for a in (u, A1, A2):
                MEMSET("pool", a[:, 0:16], 0.0, ["dpad"])
            for t in range(NT):
                bk = nxt() % 2
                proj_fm(sl["Dv"], c * 128, t, bk)
                ACT(u[:, 16 + t * 512:16 + (t + 1) * 512], bank(bk), AF.Copy, [bres(bk)], ["u"])
            lo, hi = slice(0, 64), slice(64, 128)
            TT("pool", A1[:, 16:], u[:, 16:], u[:, 15:PADW - 1], ALU.add, ["u", "dpad"], ["A1"])
            if c == 0:
                TT("pool", A2[hi, 16:], A1[hi, 16:], A1[hi, 14:PADW - 2], ALU.add, ["A1", "dpad"], ["A2"])
                fins = ((lo, A1), (hi, A2))
            else:
                TT("pool", A2[:, 16:], A1[:, 16:], A1[:, 14:PADW - 2], ALU.add, ["A1", "dpad"], ["A2"])
                TT("pool", A1[:, 16:], A2[:, 16:], A2[:, 12:PADW - 4], ALU.add, ["A2", "dpad"], ["A1"])
                TT("pool", A2[hi, 16:], A1[hi, 16:], A1[hi, 8:PADW - 8], ALU.add, ["A1", "dpad"], ["A2"])
                fins = ((lo, A1), (hi, A2))
            for (hs, fin) in fins:
                STT("dve", dfb[hs, :], fin[hs, 16:], cf[hs, C_INVW + c:C_INVW + c + 1], u[hs, 16:], ALU.mult, ALU.subtract,
                    ["A1", "A2", "u", "cst"], ["dfb"])
                TT("dve", tmp[hs, :], fin[hs, 16:32], cf[hs, C_RC + c * 16:C_RC + (c + 1) * 16], ALU.mult,
                   ["A1", "A2", "cst"], ["dtmp"])
                TT("dve", dfb[hs, 0:16], tmp[hs, :], u[hs, 16:32], ALU.subtract, ["dtmp", "u", "dfb"], ["dfb"])
            for t in range(NT):
                bk = 2 + nxt() % 2
                MM(bank(bk), pwb[:, c, :], dfb[:, t * 512:(t + 1) * 512], True, True, ["dfb", "pwb"], [bres(bk)])
                proj_fm(sl["Dg"], c * 128, t, 6)
                ACT(sgd[:], bank(6), AF.Silu, [bres(6)], ["sgd"])
                STT("dve", mixT[:, 6 + c, t * 512:(t + 1) * 512], bank(bk), psc[:, c:c + 1], sgd[:], ALU.mult, ALU.mult,
                    [bres(bk), "sgd", "psc"], [f"mx{6 + c}_{t}"])
        slab_release(sl["Dv"])
        slab_release(sl["Dg"])

    def group_E(b, l, sl):
        nonlocal ar_rec
        S.barrier()
        ar.off = 0
        qm = ar.alloc([128, 2, T], BF16)
        hb = (ar.alloc([128, 2, 512], BF16), ar.alloc([128, 2, 512], F32))
        mt = ar.alloc([128, 2, D], F32)
        mnb = ar.alloc([128, D], BF16)
        mnT = ar.alloc([128, 8, 256], BF16)
        mss = ar.alloc([128, 4], F32)
        ptb = ar.alloc([128, 3, 512], BF16)
        ar_rec = ar.alloc([128, 512], F32)
        sge = ar.alloc([128, 512], F32)
        for mb in range(2):
            S.dma("sp", f"memld{mb}", mt[:, mb, :], mem[b, mb * 128:(mb + 1) * 128, :], writes=[f"mt{mb}"])
            sc, rc_ = mss[:, mb:mb + 1], mss[:, 2 + mb:3 + mb]
            ACT(sqj[:], mt[:, mb, :], AF.Square, [f"mt{mb}"], ["sqj", "mss"], accum_out=sc)
            ACT(rc_, sc, AF.Ln, ["mss"], ["mrs"], bias=EPS, scale=1.0 / D)
            ACT(rc_, rc_, AF.Exp, ["mrs"], ["mrs"], scale=-0.5)
            STT("dve", mnb[:], mt[:, mb, :], rc_, mg_b[:], ALU.mult, ALU.mult, [f"mt{mb}", "mrs", "mg_b"], ["mnb"])
            for c in range(8):
                S.pe(lambda e, c=c: e.transpose(out=PTt[:, c * 128:(c + 1) * 128], in_=mnb[:, c * 128:(c + 1) * 128],
                                                identity=cbm(C_IDENT)), ["mnb", "cst"], ["b7"])
            ACT(mnT[:, :, mb * 128:(mb + 1) * 128], PTt[:, :].rearrange("p (c n) -> p c n", c=8), AF.Copy, ["b7"], ["mnT"])
        for c in range(2):
            bk = nxt() % 2
            for kc in range(8):
                MM(bank(bk)[:, 0:256], sl["mK"].ap[:, kc, c * 128:(c + 1) * 128], mnT[:, kc, :], kc == 0, kc == 7,
                   [sl["mK"].res, "mnT"], [bres(bk)])
            headnorm(bk, 256, pv[:, 3:4], kmT[:, c, :], "kmT", 2 + bk, hb)
        slab_release(sl["mK"])
        for mb in range(2):
            bk = 4 + mb
            for kc in range(8):
                MM(bank(bk)[:, 0:256], mnT[:, kc, mb * 128:(mb + 1) * 128], sl["mV"].ap[:, kc, 0:256], kc == 0, kc == 7,
                   [sl["mV"].res, "mnT"], [bres(bk)])
            COPY("dve", Vm[:, mb, :], bank(bk)[:, 0:256], [bres(bk)], ["Vm"])
        slab_release(sl["mV"])
        for c in range(2):
            for t in range(NT):
                bk = nxt() % 2
                proj_fm(sl["Eq"], c * 128, t, bk)
                headnorm(bk, 512, pv[:, 2:3], qm[:, c, t * 512:(t + 1) * 512], f"Eq{c}_{t}", 2 + bk, hb)
        slab_release(sl["Eq"])
        scale = 64 ** -0.5
        for c in range(2):
            for t in range(NT):
                for hh in range(2):
                    h = 2 * c + hh
                    pb = hh * 64
                    hs = slice(pb, pb + 64)
                    for mb in range(2):
                        sbk = nxt("sbk") % 4
                        k3 = nxt("k3") % 3
                        MM(bank(sbk), kmT[hs, c, mb * 128:(mb + 1) * 128], qm[hs, c, t * 512:(t + 1) * 512], True, True,
                           ["kmT", f"Eq{c}_{t}"], [bres(sbk)])
                        ACT(ptb[:, k3, :], bank(sbk), AF.Exp, [bres(sbk)], [f"pt{k3}"], scale=scale)
                        MM(bank(4)[hs, :], Vm[:, mb, h * 64:(h + 1) * 64], ptb[:, k3, :], mb == 0, mb == 1,
                           ["Vm", f"pt{k3}"], [bres(4)])
                        MM(bank(5)[hs, :], cbm(C_ONES, 64), ptb[:, k3, :], mb == 0, mb == 1, [f"pt{k3}", "cst"], [bres(5)])
                RECIP(ar_rec[:], bank(5), [bres(5)], ["rec"])
                TT("dve", ar_rec[:], bank(4), ar_rec[:], ALU.mult, [bres(4), "rec"], ["rec"])
                proj_fm(sl["Eg"], c * 128, t, 6)
                ACT(sge[:], bank(6), AF.Silu, [bres(6)], ["sge"])
                TT("dve", mixT[:, 8 + c, t * 512:(t + 1) * 512], ar_rec[:], sge[:], ALU.mult, ["rec", "sge"], [f"mx{8 + c}_{t}"])
        slab_release(sl["Eg"])

    def wout_phase(b, l):
        S.barrier()
        src = x[b] if l == 0 else xs
        if l < NLAYER - 1:
            dst = xs
        else:
            dst = y[b]
        for blk in range(NB):
            k = nxt("xst") % 2
            xr = f"xst{k}"
            rd = [] if l == 0 else [f"xs{blk}"]
            S.dma("sp", xr, xst[:, k, :], src[blk * 128:(blk + 1) * 128, :], reads=rd, writes=[xr])
            for dh in range(2):
                bk = nxt() % 4
                for kc in range(10):
                    MM(bank(bk), mixT[:, kc, blk * 128:(blk + 1) * 128], Wo[:, kc, dh * 512:(dh + 1) * 512], kc == 0, kc == 9,
                       [f"mx{kc}_{blk // 4}", "Wo"], [bres(bk)])
                TT("dve", xst[:, k, dh * 512:(dh + 1) * 512], xst[:, k, dh * 512:(dh + 1) * 512], bank(bk), ALU.add,
                   [xr, bres(bk)], [xr])
            if l < NLAYER - 1:
                S.dma("sp", "xout", dst[blk * 128:(blk + 1) * 128, :], xst[:, k, :], reads=[xr], writes=[f"xs{blk}"])
            else:
                S.dma("sp", "xout", dst[blk * 128:(blk + 1) * 128, :], xst[:, k, :], reads=[xr], writes=["yout"])

    ar_rec = None
    if dbg:
        MEMSET("dve", mixT[:], 0.0, [f"mx{c}_{t}" for c in range(10) for t in range(NT)])
    for b in range(NSEQ):
        for l in range(NLAYER):
            load_params(l)
            sl = {}
            for (k, c0, n) in (("Aq", 0, 256), ("Ak", 256, 256), ("Av", 512, 256), ("Aff", 1024, 4), ("Ag", 768, 256)):
                sl[k] = slab_load(win_cols(l, c0, n), n)
            rms_phase(b, l)
            if "A" in groups:
                group_A(b, l, sl)
            else:
                for k in ("Aq", "Ak", "Av", "Aff", "Ag"):
                    slab_release(sl[k])
            for (k, c0, n) in (("Bq", 1028, 256), ("Bk", 1284, 256), ("Bv", 1540, 256), ("Bg", 1796, 256)):
                sl[k] = slab_load(win_cols(l, c0, n), n)
            if "B" in groups:
                group_B(b, l, sl)
            else:
                for k in ("Bq", "Bk", "Bv", "Bg"):
                    slab_release(sl[k])
            for (k, c0, n) in (("Cq", 2052, 256), ("Cf", 2308, 256), ("Ci", 2564, 256), ("Cg", 2820, 256)):
                sl[k] = slab_load(win_cols(l, c0, n), n)
            for kc in range(10):
                S.dma("pool", "Wo", Wo[:, kc, :], w_out[l, kc * 128:(kc + 1) * 128, :], writes=["Wo"])
            if "C" in groups:
                group_C(b, l, sl)
            else:
                for k in ("Cq", "Cf", "Ci", "Cg"):
                    slab_release(sl[k])
            for (k, c0, n) in (("Dv", 3076, 256), ("Dg", 3332, 256)):
                sl[k] = slab_load(win_cols(l, c0, n), n)
            sl["mK"] = slab_load(wkv_cols(l, 0, 256), 256)
            sl["mV"] = slab_load(wkv_cols(l, 256, 256), 256)
            if "D" in groups:
                group_D(b, l, sl)
            else:
                slab_release(sl["Dv"])
                slab_release(sl["Dg"])
            for (k, c0, n) in (("Eq", 3588, 256), ("Eg", 3844, 256)):
                sl[k] = slab_load(win_cols(l, c0, n), n)
            if "E" in groups:
                group_E(b, l, sl)
            else:
                for k in ("mK", "mV", "Eq", "Eg"):
                    slab_release(sl[k])
            wout_phase(b, l)
    fin_reads = ["yout"]
    if dbg:
        S.barrier()
        S.dma("sp", "dbg", dbg_out, mixT[:], reads=[f"mx{c}_{t}" for c in range(10) for t in range(NT)], writes=["dbgo"])
        fin_reads.append("dbgo")
    S.add("sp", None, reads=fin_reads)
    S.emit(st)
    st.close()
    nc._arena_peak = ar.peak
    return nc


_CACHE = {}


def kernel(x, mem, norm_g, w_in, fox_f_bias, fox_q_norm, fox_k_norm, hgrn_lb_logits, hgrn_out_norm, pool_w,
           pool_scale, mem_norm_g, mem_w_kv, mem_q_norm, mem_k_norm, w_out):
    ncores = 8
    f = lambda a: np.ascontiguousarray(np.asarray(a, dtype=np.float32))
    x = f(x)
    mem = f(mem)
    B = x.shape[0]
    per = B // ncores
    if "nc" not in _CACHE:
        _CACHE["nc"] = build_nc(per, 2)
    nc = _CACHE["nc"]
    shared = dict(norm_g=f(norm_g), w_in=f(w_in), fox_f_bias=f(fox_f_bias), fox_q_norm=f(fox_q_norm),
                  fox_k_norm=f(fox_k_norm), hgrn_lb_logits=f(hgrn_lb_logits), hgrn_out_norm=f(hgrn_out_norm),
                  pool_w=f(pool_w), pool_scale=f(pool_scale), mem_norm_g=f(mem_norm_g), mem_w_kv=f(mem_w_kv),
                  mem_q_norm=f(mem_q_norm), mem_k_norm=f(mem_k_norm), w_out=f(w_out), cpack=make_cpack())
    in_maps = []
    for i in range(ncores):
        d = dict(shared)
        d["x"] = np.ascontiguousarray(x[i * per:(i + 1) * per])
        d["mem"] = np.ascontiguousarray(mem[i * per:(i + 1) * per])
        in_maps.append(d)
    res = run_bass_kernel_spmd(nc, in_maps, core_ids=list(range(ncores)))
    return np.concatenate([np.asarray(r["y"], dtype=np.float32) for r in res.results], axis=0)
```

```python
import numpy as np
from contextlib import ExitStack
import concourse.bass as bass
import concourse.mybir as mybir
from concourse.bass_utils import run_bass_kernel_spmd

F32 = mybir.dt.float32
BF16 = mybir.dt.bfloat16
AF = mybir.ActivationFunctionType
ALU = mybir.AluOpType

ENGS = ("pe", "act", "dve", "pool", "sp")


class Op:
    __slots__ = ("eng", "fn", "reads", "writes", "chan", "deps", "mile", "cnt", "waits", "idx")

    def __init__(self, eng, fn, reads, writes, chan):
        self.eng, self.fn, self.reads, self.writes, self.chan = eng, fn, tuple(reads), tuple(writes), chan
        self.deps = []
        self.mile = False
        self.cnt = 0
        self.waits = []


class Sched:
    def __init__(self, nc):
        self.nc = nc
        self.ops = []

    def add(self, eng, fn, reads=(), writes=(), chan=None):
        op = Op(eng, fn, reads, writes, chan)
        op.idx = len(self.ops)
        self.ops.append(op)
        return op

    def pe(self, fn, reads=(), writes=()):
        return self.add("pe", fn, reads, writes)

    def act(self, fn, reads=(), writes=()):
        return self.add("act", fn, reads, writes)

    def dve(self, fn, reads=(), writes=()):
        return self.add("dve", fn, reads, writes)

    def pool(self, fn, reads=(), writes=()):
        return self.add("pool", fn, reads, writes)

    def dma(self, q, chan, out, in_, reads=(), writes=(), **kw):
        return self.add(q, lambda e: e.dma_start(out=out, in_=in_, **kw), reads, writes, chan=chan)

    def barrier(self):
        return self.add("bar", None)

    def resolve(self):
        ops = self.ops
        last_w = {}
        readers = {}
        last_comp = {}
        pend = {e: [] for e in ENGS}
        for op in ops:
            if op.eng == "bar":
                L = list(last_comp.values())
                for e in ENGS:
                    pend[e] = list(L)
                continue
            deps = set(pend[op.eng])
            pend[op.eng] = []
            for r in op.reads:
                w = last_w.get(r)
                if w is not None:
                    deps.add(w)
            for r in op.writes:
                w = last_w.get(r)
                if w is not None:
                    deps.add(w)
                for rd in readers.get(r, ()):
                    deps.add(rd)
            deps.discard(op.idx)
            op.deps = sorted(deps)
            for r in op.writes:
                last_w[r] = op.idx
                readers[r] = []
            for r in op.reads:
                if r not in op.writes:
                    readers.setdefault(r, []).append(op.idx)
            if op.chan is None:
                last_comp[op.eng] = op.idx
        chan_cnt = {}
        need = []
        for op in ops:
            lst = []
            if op.eng != "bar":
                for d in op.deps:
                    dop = ops[d]
                    if dop.chan is not None:
                        lst.append(("chan", dop.chan, chan_cnt[dop.chan]))
                    else:
                        if dop.eng == op.eng and op.chan is None:
                            if op.eng == "pe":
                                continue
                        dop.mile = True
                        lst.append(("eng", dop.eng, d))
            need.append(lst)
            if op.chan is not None:
                chan_cnt[op.chan] = chan_cnt.get(op.chan, 0) + 1
        ecnt = {e: 0 for e in ENGS}
        for op in ops:
            if op.eng != "bar" and op.chan is None and op.mile:
                ecnt[op.eng] += 1
                op.cnt = ecnt[op.eng]
        waited = {e: {} for e in ENGS}
        for op, lst in zip(ops, need):
            if op.eng == "bar":
                continue
            w = {}
            for kind, key, v in lst:
                if kind == "chan":
                    k = ("chan", key)
                    val = 16 * v
                else:
                    k = ("eng", key)
                    val = ops[v].cnt
                if val > w.get(k, 0):
                    w[k] = val
            out = []
            for k, val in w.items():
                if waited[op.eng].get(k, 0) >= val:
                    continue
                waited[op.eng][k] = val
                out.append((k, val))
            op.waits = out
        self.chans = sorted({op.chan for op in ops if op.chan is not None})

    def emit(self, stack):
        nc = self.nc
        self.resolve()
        sems = {}
        for e in ENGS:
            sems[("eng", e)] = stack.enter_context(nc.semaphore("s_" + e))
        for c in self.chans:
            sems[("chan", c)] = stack.enter_context(nc.semaphore("c_" + str(c)))
        per = {e: [op for op in self.ops if op.eng == e] for e in ENGS}
        block = stack.enter_context(nc.Block())

        def run(eng_handle, lst):
            for op in lst:
                for k, val in op.waits:
                    eng_handle.wait_ge(sems[k], val)
                if op.fn is None:
                    continue
                inst = op.fn(eng_handle)
                if op.chan is not None:
                    inst.then_inc(sems[("chan", op.chan)], 16)
                elif op.mile:
                    inst.then_inc(sems[("eng", op.eng)], 1)

        @block.tensor
        def _(e):
            run(e, per["pe"])

        @block.scalar
        def _(e):
            run(e, per["act"])

        @block.vector
        def _(e):
            run(e, per["dve"])

        @block.gpsimd
        def _(e):
            run(e, per["pool"])

        @block.sync
        def _(e):
            run(e, per["sp"])


import os
CSTOP = int(os.environ.get('CSTOP', '99'))
T = 2048
D = 1024
DIN = 4100
NB = 16
NT = 4
EPS = 1e-6
POOL_WINDOWS = (2, 4, 8, 16)

C_IDENT, C_NEGID, C_TRI, C_ONES, C_NEGM, C_MLT, C_NUINCL, C_TBD, C_TBDU, C_BONES, C_NONES = range(11)
C_MASK64 = 11 * 128
C_RC = C_MASK64 + 64
C_INVW = C_RC + 32
C_ZERO = C_INVW + 2
NCF = C_ZERO + 64


def make_cpack():
    p = np.arange(128)[:, None]
    c = np.arange(128)[None, :]
    cp = np.zeros((128, NCF), np.float32)

    def put(i, m):
        cp[:, i * 128:(i + 1) * 128] = m.astype(np.float32)

    put(C_IDENT, p == c)
    put(C_NEGID, -(p == c).astype(np.float32))
    put(C_TRI, p <= c)
    put(C_ONES, np.ones((128, 128)))
    put(C_NEGM, np.where(p > c, -1e30, 0.0))
    put(C_MLT, p < c)
    put(C_NUINCL, -(p >= c).astype(np.float32))
    same = (p // 64) == (c // 64)
    put(C_TBD, same & (p <= c))
    put(C_TBDU, same & (p > c))
    put(C_BONES, same)
    put(C_NONES, -np.ones((128, 128)))
    cp[:, C_MASK64:C_MASK64 + 64] = ((p % 64) <= np.arange(64)[None, :]).astype(np.float32)
    for ch in range(2):
        for half in range(2):
            win = POOL_WINDOWS[2 * ch + half]
            pos = np.arange(1, 17, dtype=np.float32)
            cp[half * 64:(half + 1) * 64, C_RC + ch * 16:C_RC + (ch + 1) * 16] = 1.0 / np.minimum(pos, win)[None, :]
            cp[half * 64:(half + 1) * 64, C_INVW + ch] = 1.0 / win
    return cp


class Arena:
    def __init__(self, ap, nbytes):
        self.ap, self.cap, self.off = ap, nbytes, 0
        self.peak = 0

    def alloc(self, shape, dtype):
        isz = 4 if dtype == F32 else 2
        n = int(np.prod(shape[1:]))
        nb = n * isz
        off = (self.off + 63) // 64 * 64
        assert off + nb <= self.cap, ("arena overflow", off + nb, self.cap)
        a = self.ap[:, off // 2:(off + nb) // 2]
        if dtype == F32:
            a = a.bitcast(F32)
        if len(shape) == 3:
            a = a.rearrange("p (c n) -> p c n", c=shape[1])
        elif len(shape) == 4:
            a = a.rearrange("p (a b n) -> p a b n", a=shape[1], b=shape[2])
        self.off = off + nb
        self.peak = max(self.peak, self.off)
        return a


def build_nc(NSEQ=4, NLAYER=2, dbg=False, groups="ABCDE"):
    nc = bass.Bass("TRN2", target_bir_lowering=False)

    def din(name, shape):
        return nc.dram_tensor(name, list(shape), F32, kind="ExternalInput").ap()

    x = din("x", [NSEQ, T, D])
    mem = din("mem", [NSEQ, 256, D])
    norm_g = din("norm_g", [2, D])
    w_in = din("w_in", [2, D, DIN])
    fox_f_bias = din("fox_f_bias", [2, 4])
    fox_q_norm = din("fox_q_norm", [2, 64])
    fox_k_norm = din("fox_k_norm", [2, 64])
    hgrn_lb_logits = din("hgrn_lb_logits", [2, 256])
    hgrn_out_norm = din("hgrn_out_norm", [2, 256])
    pool_w = din("pool_w", [2, 4, 64, 64])
    pool_scale = din("pool_scale", [2, 256])
    mem_norm_g = din("mem_norm_g", [2, D])
    mem_w_kv = din("mem_w_kv", [2, D, 512])
    mem_q_norm = din("mem_q_norm", [2, 64])
    mem_k_norm = din("mem_k_norm", [2, 64])
    w_out = din("w_out", [2, 1280, D])
    cpack = din("cpack", [128, NCF])
    y = nc.dram_tensor("y", [NSEQ, T, D], F32, kind="ExternalOutput").ap()
    xs = nc.dram_tensor("xs_scratch", [T, D], F32).ap()
    if dbg:
        dbg_out = nc.dram_tensor("dbg", [128, 10, T], BF16, kind="ExternalOutput").ap()

    st = ExitStack()
    S = Sched(nc)

    def sb(name, shape, dtype):
        return st.enter_context(nc.sbuf_tensor(name, list(shape), dtype))

    hT = sb("hT", [128, 8, T], BF16)
    mixT = sb("mixT", [128, 10, T], BF16)
    Wo = sb("Wo", [128, 10, D], BF16)
    ws = sb("ws", [128, 5, 8, 256], BF16)
    cf = sb("cf", [128, NCF], F32)
    cb = sb("cb", [128, NCF], BF16)
    g_b = sb("g_b", [128, D], F32)
    mg_b = sb("mg_b", [128, D], F32)
    xst = sb("xst", [128, 2, D], F32)
    hbuf = sb("hbuf", [128, 2, D], BF16)
    sqj = sb("sqj", [128, D], BF16)
    ssb = sb("ssb", [128, 32], F32)
    pv = sb("pv", [128, 8], F32)
    hn = sb("hn", [128, 2], F32)
    psc = sb("psc", [128, 2], F32)
    lbl = sb("lbl", [128, 2, 2], F32)
    lblb = sb("lblb", [128, 2, 256], F32)
    lbp = sb("lbp", [128, 4], F32)
    lb_b = sb("lb_b", [128, 256], F32)
    oml_b = sb("oml_b", [128, 256], F32)
    fb_b = sb("fb_b", [128, 4], F32)
    pwf = sb("pwf", [128, 2, 128], F32)
    pwb = sb("pwb", [128, 2, 128], BF16)
    kmT = sb("kmT", [128, 2, 256], BF16)
    Vm = sb("Vm", [128, 2, 256], BF16)
    ARENA_BYTES = 54 * 1024
    arena_t = sb("arena", [128, ARENA_BYTES // 2], BF16)
    ar = Arena(arena_t, ARENA_BYTES)

    PD = [st.enter_context(nc.psum_tensor(f"pd{i}", [128, 1024], F32)) for i in range(3)]
    PX = st.enter_context(nc.psum_tensor("px", [128, 512], F32))
    PTt = st.enter_context(nc.psum_tensor("ptt", [128, 1024], BF16))

    def bank(i):
        if i < 6:
            return PD[i // 2][:, (i % 2) * 512:(i % 2 + 1) * 512]
        return PX[:, :]

    def bres(i):
        return f"b{i}"

    def MM(out, lhsT, rhs, start, stop, r, w):
        S.pe(lambda e: e.matmul(out, lhsT=lhsT, rhs=rhs, start=start, stop=stop, skip_group_check=True), r, w)

    def ACT(out, in_, func, r, w, **kw):
        S.act(lambda e: e.activation(out=out, in_=in_, func=func, **kw), r, w)

    def TT(eng, out, in0, in1, op, r, w):
        S.add(eng, lambda e: e.tensor_tensor(out=out, in0=in0, in1=in1, op=op), r, w)

    def TS(eng, out, in0, s1, s2, op0, op1, r, w):
        if s2 is None:
            S.add(eng, lambda e: e.tensor_scalar(out=out, in0=in0, scalar1=s1, scalar2=None, op0=op0), r, w)
        else:
            S.add(eng, lambda e: e.tensor_scalar(out=out, in0=in0, scalar1=s1, scalar2=s2, op0=op0, op1=op1), r, w)

    def STT(eng, out, in0, scalar, in1, op0, op1, r, w):
        S.add(eng, lambda e: e.scalar_tensor_tensor(out=out, in0=in0, scalar=scalar, in1=in1, op0=op0, op1=op1), r, w)

    def COPY(eng, out, in_, r, w):
        S.add(eng, lambda e: e.tensor_copy(out=out, in_=in_), r, w)

    def RECIP(out, in_, r, w):
        S.dve(lambda e: e.reciprocal(out=out, in_=in_), r, w)

    def MEMSET(eng, ap, val, w):
        S.add(eng, lambda e: e.memset(ap, val), (), w)

    def cbm(i, n=128):
        return cb[:, i * 128:i * 128 + n]

    def cfm(i, n=128):
        return cf[:, i * 128:i * 128 + n]

    S.dma("sp", "cst", cf[:], cpack, writes=["cst"])
    COPY("dve", cb[:], cf[:], ["cst"], ["cst"])

    slab_free = [True] * 5

    class Slab:
        pass

    def slab_load(src3d, ncols):
        slot = slab_free.index(True)
        slab_free[slot] = False
        s = Slab()
        s.slot, s.res, s.ap = slot, f"ws{slot}", ws[:, slot]
        S.dma("pool", f"ws{slot}", ws[:, slot, :, 0:ncols], src3d, writes=[s.res])
        return s

    def slab_release(s):
        slab_free[s.slot] = True

    def win_cols(l, c0, n):
        return w_in[l].rearrange("(c p) n -> p c n", p=128)[:, :, c0:c0 + n]

    def wkv_cols(l, c0, n):
        return mem_w_kv[l].rearrange("(c p) n -> p c n", p=128)[:, :, c0:c0 + n]

    def proj_fm(s, col0, t, bk):
        for kc in range(8):
            MM(bank(bk), s.ap[:, kc, col0:col0 + 128], hT[:, kc, t * 512:(t + 1) * 512], kc == 0, kc == 7,
               [s.res, f"hT{t}"], [bres(bk)])

    def proj_tm(s, col0, ncols, blk, out_ap, ores):
        for kc in range(8):
            MM(out_ap, hT[:, kc, blk * 128:(blk + 1) * 128], s.ap[:, kc, col0:col0 + ncols], kc == 0, kc == 7,
               [s.res, f"hT{blk // 4}"], [ores])

    rot = {}

    def nxt(name="g"):
        rot[name] = rot.get(name, 0) + 1
        return rot[name]

    def headnorm(src_bk, N, gain_col, out_ap, out_res, stat_bk, hb):
        k = nxt() % 2
        sq, lnv = hb
        src = bank(src_bk)[:, 0:N]
        ACT(sq[:, k, 0:N], src, AF.Square, [bres(src_bk)], [f"hnsq{k}"])
        MM(bank(stat_bk)[:, 0:N], cbm(C_BONES), sq[:, k, 0:N], True, True, [f"hnsq{k}", "cst"], [bres(stat_bk)])
        ACT(lnv[:, k, 0:N], bank(stat_bk)[:, 0:N], AF.Ln, [bres(stat_bk)], [f"hnln{k}"], bias=EPS, scale=1.0 / 64)
        ACT(lnv[:, k, 0:N], lnv[:, k, 0:N], AF.Exp, [f"hnln{k}"], [f"hnln{k}"], scale=-0.5)
        STT("dve", out_ap, src, gain_col, lnv[:, k, 0:N], ALU.mult, ALU.mult,
            [bres(src_bk), f"hnln{k}", "pv"], [out_res])

    def load_params(l):
        S.dma("sp", "g_b", g_b[:], norm_g[l].partition_broadcast(128), writes=["g_b"])
        S.dma("sp", "mg_b", mg_b[:], mem_norm_g[l].partition_broadcast(128), writes=["mg_b"])
        for i, src in enumerate((fox_q_norm, fox_k_norm, mem_q_norm, mem_k_norm)):
            col = src[l].rearrange("(p o) -> p o", o=1)
            S.dma("sp", "pv", pv[0:64, i:i + 1], col, writes=["pv"])
            S.dma("sp", "pv", pv[64:128, i:i + 1], col, writes=["pv"])
        S.dma("sp", "hn", hn[:], hgrn_out_norm[l].rearrange("(c p) -> p c", p=128), writes=["hn"],
              allow_slow_non_contiguous=True)
        S.dma("sp", "psc", psc[:], pool_scale[l].rearrange("(c p) -> p c", p=128), writes=["psc"],
              allow_slow_non_contiguous=True)
        S.dma("sp", "fb_b", fb_b[:], fox_f_bias[l].partition_broadcast(128), writes=["fb_b"])
        MEMSET("dve", pwf[:], 0.0, ["pwf"])
        for g in range(4):
            hs = slice((g % 2) * 64, (g % 2) * 64 + 64)
            S.dma("sp", "pwf", pwf[hs, g // 2, (g % 2) * 64:(g % 2) * 64 + 64], pool_w[l, g], writes=["pwf"])
        COPY("dve", pwb[:], pwf[:], ["pwf"], ["pwb"])
        if l == 0:
            MEMSET("dve", lbp[:, 0:2], 0.0, ["lb"])
            MEMSET("dve", lbp[:, 2:4], 1.0, ["lb"])
            MEMSET("dve", lb_b[:], 0.0, ["lb"])
            MEMSET("dve", oml_b[:], 1.0, ["lb"])
        else:
            S.dma("sp", "lbl", lbl[:], hgrn_lb_logits.rearrange("l (c p) -> p l c", p=128), writes=["lbl"],
                  allow_slow_non_contiguous=True)
            S.dma("sp", "lblb", lblb[:], hgrn_lb_logits.partition_broadcast(128), writes=["lblb"])
            for (dst_lb, dst_oml, a0, a1) in ((lbp[:, 0:2], lbp[:, 2:4], lbl[:, 0, :], lbl[:, 1, :]),
                                             (lb_b[:], oml_b[:], lblb[:, 0, :], lblb[:, 1, :])):
                TT("dve", dst_oml, a1, a0, ALU.subtract, ["lbl", "lblb", "lb"], ["lb"])
                ACT(dst_oml, dst_oml, AF.Exp, ["lb"], ["lb"], scale=-1.0)
                TS("dve", dst_oml, dst_oml, 1.0, None, ALU.add, None, ["lb"], ["lb"])
                RECIP(dst_lb, dst_oml, ["lb"], ["lb"])
                TS("dve", dst_lb, dst_lb, 1.0 - 1e-6, None, ALU.min, None, ["lb"], ["lb"])
                TS("dve", dst_oml, dst_lb, -1.0, 1.0, ALU.mult, ALU.add, ["lb"], ["lb"])

    def rms_phase(b, l):
        src = x[b] if l == 0 else xs
        st1 = {}

        def stage1(blk):
            k = nxt("xst") % 2
            xr = f"xst{k}"
            rd = [] if l == 0 else [f"xs{blk}"]
            S.dma("sp", xr, xst[:, k, :], src[blk * 128:(blk + 1) * 128, :], reads=rd, writes=[xr])
            sc = ssb[:, blk:blk + 1]
            rc_ = ssb[:, 16 + blk:17 + blk]
            ACT(sqj[:], xst[:, k, :], AF.Square, [xr], ["sqj", f"ss{blk}"], accum_out=sc)
            ACT(rc_, sc, AF.Ln, [f"ss{blk}"], [f"rs{blk}"], bias=EPS, scale=1.0 / D)
            ACT(rc_, rc_, AF.Exp, [f"rs{blk}"], [f"rs{blk}"], scale=-0.5)
            hk = blk % 2
            STT("dve", hbuf[:, hk, :], xst[:, k, :], rc_, g_b[:], ALU.mult, ALU.mult,
                [xr, f"rs{blk}", "g_b"], [f"hb{hk}"])

        def stage2(blk):
            hk = blk % 2
            for c in range(8):
                S.pe(lambda e, c=c, hk=hk: e.transpose(out=PTt[:, c * 128:(c + 1) * 128],
                                                       in_=hbuf[:, hk, c * 128:(c + 1) * 128], identity=cbm(C_IDENT)),
                     [f"hb{hk}", "cst"], ["b7"])
            ACT(hT[:, :, blk * 128:(blk + 1) * 128], PTt[:, :].rearrange("p (c n) -> p c n", c=8), AF.Copy,
                ["b7"], [f"hT{blk // 4}"])

        stage1(0)
        for blk in range(NB):
            if blk + 1 < NB:
                stage1(blk + 1)
            stage2(blk)

    def attn_finish(h, c, qt, sg, mixc, with_den):
        pb = (h % 2) * 64
        hs = slice(pb, pb + 64)
        num = bank(4)[hs, :]
        mres = f"mx{mixc}_{qt}"
        if with_den:
            rec = ar_rec
            ACT(rec[hs, :], bank(5)[hs, :], AF.Ln, [bres(5)], ["rec"])
            ACT(rec[hs, :], rec[hs, :], AF.Exp, ["rec"], ["rec"], scale=-1.0)
            TT("dve", rec[hs, :], num, rec[hs, :], ALU.mult, [bres(4), "rec"], ["rec"])
            TT("dve", mixT[hs, mixc, qt * 512:(qt + 1) * 512], rec[hs, :], sg[hs, c, qt * 512:(qt + 1) * 512], ALU.mult,
               ["rec", "sg"], [mres])
        else:
            TT("dve", mixT[hs, mixc, qt * 512:(qt + 1) * 512], num, sg[hs, c, qt * 512:(qt + 1) * 512], ALU.mult,
               [bres(4), "sg"], [mres])

    def gates_full(s, sg):
        for c in range(2):
            for t in range(NT):
                bk = 6
                proj_fm(s, c * 128, t, bk)
                ACT(sg[:, c, t * 512:(t + 1) * 512], bank(bk), AF.Silu, [bres(bk)], ["sg"])

    def group_A(b, l, sl):
        nonlocal ar_rec
        S.barrier()
        ar.off = 0
        qT = ar.alloc([128, 2, T], BF16)
        kT = ar.alloc([128, 2, T], BF16)
        V = ar.alloc([128, NB, 256], BF16)
        sg = ar.alloc([128, 2, T], BF16)
        csp = ar.alloc([128, NB, 4], F32)
        mark = ar.off
        hb = (ar.alloc([128, 2, 512], BF16), ar.alloc([128, 2, 512], F32))
        tmpf = ar.alloc([128, 64], F32)
        tot = ar.alloc([128, 64], F32)
        inc = ar.alloc([128, 64], F32)
        for (s, dst, gcol, nm) in ((sl["Aq"], qT, 0, "q"), (sl["Ak"], kT, 1, "k")):
            for c in range(2):
                for t in range(NT):
                    bk = nxt() % 2
                    proj_fm(s, c * 128, t, bk)
                    headnorm(bk, 512, pv[:, gcol:gcol + 1], dst[:, c, t * 512:(t + 1) * 512], f"A{nm}{c}_{t}", 2 + bk, hb)
            slab_release(s)
        for blk in range(NB):
            bk = 4 + nxt() % 2
            proj_tm(sl["Av"], 0, 256, blk, bank(bk)[:, 0:256], bres(bk))
            COPY("dve", V[:, blk, :], bank(bk)[:, 0:256], [bres(bk)], [f"AV{blk}"])
        slab_release(sl["Av"])
        for blk in range(NB):
            proj_tm(sl["Aff"], 0, 4, blk, bank(6)[:, blk * 4:(blk + 1) * 4], bres(6))
        slab_release(sl["Aff"])
        TT("dve", tmpf[:].rearrange("p (b h) -> p b h", h=4), bank(6)[:, 0:64].rearrange("p (b h) -> p b h", h=4),
           fb_b[:].unsqueeze(1).to_broadcast([128, NB, 4]), ALU.add, [bres(6), "fb_b"], ["tmpf"])
        gates_full(sl["Ag"], sg)
        slab_release(sl["Ag"])
        ACT(tmpf[:], tmpf[:], AF.Exp, ["tmpf"], ["tmpf"], scale=-1.0)
        ACT(tmpf[:], tmpf[:], AF.Ln, ["tmpf"], ["tmpf"], bias=1.0)
        MM(bank(0)[:, 0:64], cfm(C_TRI), tmpf[:], True, True, ["tmpf", "cst"], [bres(0)])
        MM(bank(1)[:, 0:64], cfm(C_ONES), tmpf[:], True, True, ["tmpf", "cst"], [bres(1)])
        COPY("dve", tot[:], bank(1)[:, 0:64], [bres(1)], ["tot"])
        for h in range(4):
            S.dve(lambda e, h=h: e.tensor_tensor_scan(
                out=inc[:].rearrange("p (b h) -> p b h", h=4)[:, :, h], data0=cfm(C_ONES, 16),
                data1=tot[:].rearrange("p (b h) -> p b h", h=4)[:, :, h], initial=0.0, op0=ALU.mult, op1=ALU.add),
                ["tot", "cst"], ["inc"])
        TT("dve", inc[:], inc[:], tot[:], ALU.subtract, ["inc", "tot"], ["inc"])
        TT("dve", csp[:].rearrange("p b h -> p (b h)"), bank(0)[:, 0:64], inc[:], ALU.add, [bres(0), "inc"], ["csp"])
        S.barrier()
        ar.off = mark
        CTn = ar.alloc([128, T], F32)
        argb = ar.alloc([128, 3, 512], F32)
        ptb = ar.alloc([128, 3, 512], BF16)
        ar_rec = ar.alloc([128, 512], F32)
        scale = 64 ** -0.5
        for h in range(4):
            c, pb = h // 2, (h % 2) * 64
            hs = slice(pb, pb + 64)
            for q4 in range(4):
                bk = 4 + q4 % 2
                for i in range(4):
                    blk = q4 * 4 + i
                    MM(bank(bk)[:, i * 128:(i + 1) * 128], csp[:, blk, h:h + 1].to_broadcast([128, 128]), cfm(C_NEGID),
                       True, True, ["csp", "cst"], [bres(bk)])
                ACT(CTn[:, q4 * 512:(q4 + 1) * 512], bank(bk), AF.Copy, [bres(bk)], ["CTn"])
            iters = [(qt, kb) for qt in range(NT) for kb in range(4 * qt + 4)]
            LA = 2

            def emit_S(i):
                qt, kb = iters[i]
                off = max(0, kb - 4 * qt) * 128
                sbk = i % 4
                MM(bank(sbk)[:, off:512], kT[hs, c, kb * 128:(kb + 1) * 128], qT[hs, c, qt * 512 + off:(qt + 1) * 512],
                   True, True, [f"Ak{c}_{kb // 4}", f"Aq{c}_{qt}"], [bres(sbk)])

            for i in range(min(LA, len(iters))):
                emit_S(i)
            for i, (qt, kb) in enumerate(iters):
                if i + LA < len(iters):
                    emit_S(i + LA)
                nk = 4 * qt + 4
                off = max(0, kb - 4 * qt) * 128
                sbk = i % 4
                k3 = i % 3
                STT("dve", argb[:, k3, off:512], bank(sbk)[:, off:512], scale, CTn[:, qt * 512 + off:(qt + 1) * 512],
                    ALU.mult, ALU.add, [bres(sbk), "CTn"], [f"arg{k3}"])
                if kb >= 4 * qt:
                    TT("dve", argb[:, k3, off:off + 128], argb[:, k3, off:off + 128], cfm(C_NEGM), ALU.add,
                       [f"arg{k3}", "cst"], [f"arg{k3}"])
                ACT(ptb[:, k3, off:512], argb[:, k3, off:512], AF.Exp, [f"arg{k3}", "csp"], [f"pt{k3}"],
                    bias=csp[:, kb, h:h + 1])
                MM(bank(4)[hs, off:512], V[:, kb, h * 64:(h + 1) * 64], ptb[:, k3, off:512], kb == 0, kb == nk - 1,
                   [f"AV{kb}", f"pt{k3}"], [bres(4)])
                MM(bank(5)[hs, off:512], cbm(C_ONES, 64), ptb[:, k3, off:512], kb == 0, kb == nk - 1,
                   [f"pt{k3}", "cst"], [bres(5)])
                if kb == nk - 1:
                    attn_finish(h, c, qt, sg, c, True)

    def group_B(b, l, sl):
        S.barrier()
        ar.off = 0
        qT = ar.alloc([128, 2, T], BF16)
        kT = ar.alloc([128, 2, T], BF16)
        V = ar.alloc([128, NB, 256], BF16)
        sg = ar.alloc([128, 2, T], BF16)
        eb = ar.alloc([128, 2, 512], F32)
        spb = ar.alloc([128, 3, 512], BF16)
        atb = ar.alloc([128, 2, 512], BF16)
        acc = ar.alloc([128, 512], BF16)
        z64 = ar.alloc([128, 64], BF16)
        MEMSET("dve", z64[:], 0.0, ["z64"])
        scale = 64 ** -0.5
        for (s, dst, nm, scl) in ((sl["Bq"], qT, "q", scale), (sl["Bk"], kT, "k", 1.0)):
            for c in range(2):
                for t in range(NT):
                    bk = nxt() % 2
                    proj_fm(s, c * 128, t, bk)
                    ACT(dst[:, c, t * 512:(t + 1) * 512], bank(bk), AF.Copy, [bres(bk)], [f"B{nm}{c}_{t}"], scale=scl)
            slab_release(s)
        for blk in range(NB):
            bk = 4 + nxt() % 2
            proj_tm(sl["Bv"], 0, 256, blk, bank(bk)[:, 0:256], bres(bk))
            COPY("dve", V[:, blk, :], bank(bk)[:, 0:256], [bres(bk)], [f"BV{blk}"])
        slab_release(sl["Bv"])
        gates_full(sl["Bg"], sg)
        slab_release(sl["Bg"])
        for h in range(4):
            c, pb = h // 2, (h % 2) * 64
            hs = slice(pb, pb + 64)
            iters = [(qt, kb) for qt in range(NT) for kb in range(4 * qt + 3, -1, -1)]
            n_it = len(iters)

            def geo(i):
                qt, kb = iters[i]
                off = max(0, kb - 4 * qt) * 128
                return qt, kb, off, kT[hs, c, kb * 128:(kb + 1) * 128], qT[hs, c, qt * 512 + off:(qt + 1) * 512], \
                    [f"Bk{c}_{kb // 4}", f"Bq{c}_{qt}"]

            def emit_Z(i):
                qt, kb, off, ksl, qsl, kq = geo(i)
                MM(bank(i % 3)[:, off:512], ksl, qsl, True, True, kq, [bres(i % 3)])

            def emit_E(i):
                qt, kb, off, ksl, qsl, kq = geo(i)
                za, k2, k3 = i % 3, i % 2, i % 3
                ACT(eb[:, k2, off:512], bank(za)[:, off:512], AF.Exp, [bres(za)], [f"e{k2}"])
                ACT(spb[:, k3, off:512], eb[:, k2, off:512], AF.Ln, [f"e{k2}"], [f"sp{k3}"], bias=1.0)
                if kb >= 4 * qt:
                    TT("pool", spb[:, k3, off:off + 128], spb[:, k3, off:off + 128], cbm(C_MLT), ALU.mult,
                       [f"sp{k3}", "cst"], [f"sp{k3}"])

            emit_Z(0)
            if n_it > 1:
                emit_Z(1)
            emit_E(0)
            for i in range(n_it):
                qt, kb, off, ksl, qsl, kq = geo(i)
                if i + 2 < n_it:
                    emit_Z(i + 2)
                if i + 1 < n_it:
                    emit_E(i + 1)
                if kb == 4 * qt + 3:
                    MEMSET("pool", acc[:], 0.0, ["acc"])
                    MM(bank(4)[hs, :], z64[:], qT[:, c, qt * 512:(qt + 1) * 512], True, False, ["z64", f"Bq{c}_{qt}"], [bres(4)])
                k3, k2 = i % 3, i % 2
                zb = 3 if k2 == 0 else 5
                diag = kb >= 4 * qt
                MM(bank(zb)[:, off:512], cbm(C_NUINCL), spb[:, k3, off:512], True, False, [f"sp{k3}", "cst"], [bres(zb)])
                MM(bank(zb)[:, off:512], cbm(C_NONES), acc[:, off:512], False, False, ["acc", "cst"], [bres(zb)])
                MM(bank(zb)[:, off:512], ksl, qsl, False, True, kq, [bres(zb)])
                ACT(atb[:, k2, off:512], bank(zb)[:, off:512], AF.Exp, [bres(zb)], [f"at{k2}"])
                if diag:
                    TT("pool", atb[:, k2, off:off + 128], atb[:, k2, off:off + 128], cbm(C_MLT), ALU.mult,
                       [f"at{k2}", "cst"], [f"at{k2}"])
                if kb > 0:
                    TT("dve", acc[:, off:512], acc[:, off:512], spb[:, k3, off:512], ALU.add, ["acc", f"sp{k3}"], ["acc"])
                MM(bank(4)[hs, off:512], V[:, kb, h * 64:(h + 1) * 64], atb[:, k2, off:512], False, kb == 0,
                   [f"BV{kb}", f"at{k2}"], [bres(4)])
                if kb == 0:
                    attn_finish(h, c, qt, sg, 2 + c, False)

    def group_C(b, l, sl):
        S.barrier()
        ar.off = 0
        Sall = ar.alloc([128, 9, 2, 64], F32)
        MEMSET("dve", Sall[:, 0], 0.0, ["S0"])
        mark = ar.off
        for t in range(NT):
            S.barrier()
            ar.off = mark
            e_s = ar.alloc([128, 1024], F32)
            f_s = ar.alloc([128, 1024], F32)
            k_s = ar.alloc([128, 1024], F32)
            lf_hi = ar.alloc([128, 1024], BF16)
            lf_lo = ar.alloc([128, 1024], BF16)
            khat = ar.alloc([128, 4, 256], BF16)
            Eq = ar.alloc([128, 2, 512], F32)
            Ek = ar.alloc([128, 2, 512], F32)
            sq_s = ar.alloc([128, 2, 512], F32)
            qt_ = ar.alloc([128, 2, 512], BF16)
            kt_ = ar.alloc([128, 2, 512], BF16)
            Vc = ar.alloc([128, 4, 256], BF16)
            scm = ar.alloc([128, 16, 64], BF16)
            Sb = ar.alloc([128, 8, 2, 64], BF16)
            sqo = ar.alloc([128, 1024], BF16)
            sgc = ar.alloc([128, 512], F32)
            P0, P1, P2 = PD[0], PD[1], PD[2]
            b01, b23, b45 = [bres(0), bres(1)], [bres(2), bres(3)], [bres(4), bres(5)]
            for i in range(4):
                proj_tm(sl["Cf"], 0, 256, 4 * t + i, P0[:, i * 256:(i + 1) * 256], bres(i // 2))
            ACT(e_s[:], P0[:, :], AF.Exp, b01, ["e_s"], scale=-1.0)
            ACT(e_s[:], e_s[:], AF.Ln, ["e_s"], ["e_s"], bias=1.0)
            ACT(e_s[:], e_s[:], AF.Exp, ["e_s"], ["e_s"], scale=-1.0)
            omb = oml_b[:].unsqueeze(1).to_broadcast([128, 4, 256])
            lbb = lb_b[:].unsqueeze(1).to_broadcast([128, 4, 256])
            v3 = lambda a: a.rearrange("p (i n) -> p i n", i=4)
            TT("dve", v3(e_s[:]), v3(e_s[:]), omb, ALU.mult, ["e_s", "lb"], ["e_s"])
            TT("dve", v3(f_s[:]), v3(e_s[:]), lbb, ALU.add, ["e_s", "lb"], ["f_s"])
            TT("dve", v3(k_s[:]), omb, v3(e_s[:]), ALU.subtract, ["e_s", "lb"], ["k_s"])
            ACT(f_s[:], f_s[:], AF.Ln, ["f_s"], ["f_s"])
            COPY("dve", lf_hi[:], f_s[:], ["f_s"], ["lf_hi"])
            TT("dve", lf_lo[:], f_s[:], lf_hi[:], ALU.subtract, ["f_s", "lf_hi"], ["lf_lo"])
            if CSTOP < 2:
                continue
            for i in range(4):
                o = P1[:, i * 256:(i + 1) * 256]
                MM(o, cbm(C_TBDU), lf_hi[:, i * 256:(i + 1) * 256], True, False, ["lf_hi", "cst"], [bres(2 + i // 2)])
                MM(o, cbm(C_TBDU), lf_lo[:, i * 256:(i + 1) * 256], False, True, ["lf_lo", "cst"], [bres(2 + i // 2)])
            ACT(e_s[:], P1[:, :], AF.Exp, b23, ["e_s"])
            TT("dve", khat[:].rearrange("p i n -> p (i n)"), k_s[:], e_s[:], ALU.mult, ["k_s", "e_s"], ["khat"])
            for c in range(2):
                for i in range(4):
                    o = P2[:, c * 512 + i * 128:c * 512 + (i + 1) * 128]
                    MM(o, lf_hi[:, i * 256 + c * 128:i * 256 + (c + 1) * 128], cbm(C_TBD), True, False,
                       ["lf_hi", "cst"], [bres(4 + c)])
                    MM(o, lf_lo[:, i * 256 + c * 128:i * 256 + (c + 1) * 128], cbm(C_TBD), False, True,
                       ["lf_lo", "cst"], [bres(4 + c)])
            ACT(Eq[:].rearrange("p c n -> p (c n)"), P2[:, :], AF.Exp, b45, ["Eq"])
            ACT(Ek[:].rearrange("p c n -> p (c n)"), P2[:, :], AF.Exp, b45, ["Ek"], scale=-1.0)
            if CSTOP < 3:
                continue
            for c in range(2):
                bk = 6
                proj_fm(sl["Cq"], c * 128, t, bk)
                ACT(sq_s[:, c, :], bank(bk), AF.Silu, [bres(bk)], [f"sq_s{c}"])
                TT("dve", qt_[:, c, :], sq_s[:, c, :], Eq[:, c, :], ALU.mult, [f"sq_s{c}", "Eq"], ["qt_"])
            for c in range(2):
                bk = 6
                proj_fm(sl["Cf"], c * 128, t, bk)
                ACT(sq_s[:, c, :], bank(bk), AF.Exp, [bres(bk)], [f"sq_s{c}"])
                ACT(sq_s[:, c, :], sq_s[:, c, :], AF.Ln, [f"sq_s{c}"], [f"sq_s{c}"], bias=1.0)
                ACT(sq_s[:, c, :], sq_s[:, c, :], AF.Exp, [f"sq_s{c}"], [f"sq_s{c}"], scale=-1.0)
                STT("dve", kt_[:, c, :], sq_s[:, c, :], lbp[:, 2 + c:3 + c], Ek[:, c, :], ALU.mult, ALU.mult,
                    [f"sq_s{c}", "Ek", "lb"], ["kt_"])
            if CSTOP < 4:
                continue
            for i in range(4):
                proj_tm(sl["Ci"], 0, 256, 4 * t + i, P0[:, i * 256:(i + 1) * 256], bres(i // 2))
            COPY("dve", Vc[:].rearrange("p i n -> p (i n)"), P0[:, :], b01, ["Vc"])
            if CSTOP < 5:
                continue
            for hp in range(2):
                pb = hp * 64
                for ch in range(8):
                    i, half = ch // 2, ch % 2
                    for j in range(2):
                        slot = i * 2 + j
                        MM(P1[half * 64:(half + 1) * 64, hp * 512 + slot * 64:hp * 512 + (slot + 1) * 64],
                           kt_[pb:pb + 64, j, ch * 64:(ch + 1) * 64], qt_[pb:pb + 64, j, ch * 64:(ch + 1) * 64],
                           True, True, ["kt_", "qt_"], [bres(2 + hp)])
            TT("dve", scm[:], P1[:, :].rearrange("p (s n) -> p s n", n=64),
               cb[:, C_MASK64:C_MASK64 + 64].unsqueeze(1).to_broadcast([128, 16, 64]), ALU.mult, b23 + ["cst"], ["scm"])
            if CSTOP < 6:
                continue
            def uslot(ch, j):
                return (ch % 2) * 512 + ((ch // 2) * 2 + j) * 64
            for half in range(2):
                for ch in range(half, 8, 2):
                    i = ch // 2
                    for h in range(4):
                        j = h // 2
                        o0 = uslot(ch, j)
                        MM(P0[(h % 2) * 64:(h % 2) * 64 + 64, o0:o0 + 64],
                           khat[half * 64:(half + 1) * 64, i, h * 64:(h + 1) * 64],
                           Vc[half * 64:(half + 1) * 64, i, h * 64:(h + 1) * 64], True, True, ["khat", "Vc"], [bres(half)])
            if CSTOP < 7:
                continue
            for ch in range(8):
                for j in range(2):
                    o0 = uslot(ch, j)
                    STT("dve", Sall[:, ch + 1, j, :], Sall[:, ch, j, :], Eq[:, j, ch * 64 + 63:ch * 64 + 64],
                        P0[:, o0:o0 + 64], ALU.mult, ALU.add, [f"S{ch}", "Eq", bres(ch % 2)], [f"S{ch + 1}"])
            ACT(Sb[:].rearrange("p a b n -> p (a b n)"), Sall[:, 0:8].rearrange("p a b n -> p (a b n)"), AF.Copy,
                [f"S{k}" for k in range(8)], ["Sb"])
            COPY("dve", Sall[:, 0].rearrange("p b n -> p (b n)"), Sall[:, 8].rearrange("p b n -> p (b n)"), ["S8"], ["S0"])
            if CSTOP < 8:
                continue
            for hp in range(2):
                pb = hp * 64
                for j in range(2):
                    for ch in range(8):
                        MM(P2[pb:pb + 64, j * 512 + ch * 64:j * 512 + (ch + 1) * 64], Sb[pb:pb + 64, ch, j, :],
                           qt_[pb:pb + 64, j, ch * 64:(ch + 1) * 64], True, True, ["Sb", "qt_"], [bres(4 + j)])
            for half in range(2):
                for ch in range(half, 8, 2):
                    i = ch // 2
                    for h in range(4):
                        j, hp = h // 2, h % 2
                        o0 = half * 512 + (j * 4 + i) * 64
                        MM(P0[hp * 64:(hp + 1) * 64, o0:o0 + 64],
                           Vc[half * 64:(half + 1) * 64, i, h * 64:(h + 1) * 64],
                           scm[half * 64:(half + 1) * 64, hp * 8 + i * 2 + j, :], True, True, ["Vc", "scm"], [bres(half)])
            for half in range(2):
                COPY("dve", k_s[:].rearrange("p (j i h t) -> p j i h t", j=2, i=4, h=2)[:, :, :, half, :],
                     P0[:, half * 512:(half + 1) * 512].rearrange("p (j i t) -> p j i t", j=2, i=4), [bres(half)], ["k_s"])
            TT("dve", f_s[:], P2[:, :], k_s[:], ALU.add, b45 + ["k_s"], ["f_s"])
            if CSTOP < 9:
                continue
            ACT(sqo[:], f_s[:], AF.Square, ["f_s"], ["sqo"])
            for j in range(2):
                MM(bank(2 + j), cbm(C_BONES), sqo[:, j * 512:(j + 1) * 512], True, True, ["sqo", "cst"], [bres(2 + j)])
            ACT(e_s[:], P1[:, :], AF.Ln, b23, ["e_s"], bias=EPS, scale=1.0 / 64)
            ACT(e_s[:], e_s[:], AF.Exp, ["e_s"], ["e_s"], scale=-0.5)
            for j in range(2):
                STT("dve", f_s[:, j * 512:(j + 1) * 512], f_s[:, j * 512:(j + 1) * 512], hn[:, j:j + 1],
                    e_s[:, j * 512:(j + 1) * 512], ALU.mult, ALU.mult, ["f_s", "e_s", "hn"], ["f_s"])
                proj_fm(sl["Cg"], j * 128, t, 6)
                ACT(sgc[:], bank(6), AF.Silu, [bres(6)], ["sgc"])
                TT("dve", mixT[:, 4 + j, t * 512:(t + 1) * 512], f_s[:, j * 512:(j + 1) * 512], sgc[:], ALU.mult,
                   ["f_s", "sgc"], [f"mx{4 + j}_{t}"])
        for k in ("Cq", "Cf", "Ci", "Cg"):
            slab_release(sl[k])

    def group_D(b, l, sl):
        S.barrier()
        ar.off = 0
        PADW = 16 + T
        sgd = ar.alloc([128, 512], F32)
        for c in range(2):
            S.barrier()
            ar.off = 2048
            u = ar.alloc([128, PADW], F32)
            A1 = ar.alloc([128, PADW], F32)
            A2 = ar.alloc([128, PADW], F32)
            dfb = ar.alloc([128, T], BF16)
            tmp = ar.alloc([128, 16], F32)
            for a in (u, A1, A2):
                MEMSET("pool", a[:, 0:16], 0.0, ["dpad"])
            for t in range(NT):
                bk = nxt() % 2
                proj_fm(sl["Dv"], c * 128, t, bk)
                ACT(u[:, 16 + t * 512:16 + (t + 1) * 512], bank(bk), AF.Copy, [bres(bk)], ["u"])
            lo, hi = slice(0, 64), slice(64, 128)
            TT("pool", A1[:, 16:], u[:, 16:], u[:, 15:PADW - 1], ALU.add, ["u", "dpad"], ["A1"])
            if c == 0:
                TT("pool", A2[hi, 16:], A1[hi, 16:], A1[hi, 14:PADW - 2], ALU.add, ["A1", "dpad"], ["A2"])
                fins = ((lo, A1), (hi, A2))
            else:
                TT("pool", A2[:, 16:], A1[:, 16:], A1[:, 14:PADW - 2], ALU.add, ["A1", "dpad"], ["A2"])
                TT("pool", A1[:, 16:], A2[:, 16:], A2[:, 12:PADW - 4], ALU.add, ["A2", "dpad"], ["A1"])
                TT("pool", A2[hi, 16:], A1[hi, 16:], A1[hi, 8:PADW - 8], ALU.add, ["A1", "dpad"], ["A2"])
                fins = ((lo, A1), (hi, A2))
            for (hs, fin) in fins:
                STT("dve", dfb[hs, :], fin[hs, 16:], cf[hs, C_INVW + c:C_INVW + c + 1], u[hs, 16:], ALU.mult, ALU.subtract,
                    ["A1", "A2", "u", "cst"], ["dfb"])
                TT("dve", tmp[hs, :], fin[hs, 16:32], cf[hs, C_RC + c * 16:C_RC + (c + 1) * 16], ALU.mult,
                   ["A1", "A2", "cst"], ["dtmp"])
                TT("dve", dfb[hs, 0:16], tmp[hs, :], u[hs, 16:32], ALU.subtract, ["dtmp", "u", "dfb"], ["dfb"])
            for t in range(NT):
                bk = 2 + nxt() % 2
                MM(bank(bk), pwb[:, c, :], dfb[:, t * 512:(t + 1) * 512], True, True, ["dfb", "pwb"], [bres(bk)])
                proj_fm(sl["Dg"], c * 128, t, 6)
                ACT(sgd[:], bank(6), AF.Silu, [bres(6)], ["sgd"])
                STT("dve", mixT[:, 6 + c, t * 512:(t + 1) * 512], bank(bk), psc[:, c:c + 1], sgd[:], ALU.mult, ALU.mult,
                    [bres(bk), "sgd", "psc"], [f"mx{6 + c}_{t}"])
        slab_release(sl["Dv"])
        slab_release(sl["Dg"])

    def group_E(b, l, sl):
        nonlocal ar_rec
        S.barrier()
        ar.off = 0
        qm = ar.alloc([128, 2, T], BF16)
        hb = (ar.alloc([128, 2, 512], BF16), ar.alloc([128, 2, 512], F32))
        mt = ar.alloc([128, 2, D], F32)
        mnb = ar.alloc([128, D], BF16)
        mnT = ar.alloc([128, 8, 256], BF16)
        mss = ar.alloc([128, 4], F32)
        ptb = ar.alloc([128, 3, 512], BF16)
        ar_rec = ar.alloc([128, 512], F32)
        sge = ar.alloc([128, 512], F32)
        for mb in range(2):
            S.dma("sp", f"memld{mb}", mt[:, mb, :], mem[b, mb * 128:(mb + 1) * 128, :], writes=[f"mt{mb}"])
            sc, rc_ = mss[:, mb:mb + 1], mss[:, 2 + mb:3 + mb]
            ACT(sqj[:], mt[:, mb, :], AF.Square, [f"mt{mb}"], ["sqj", "mss"], accum_out=sc)
            ACT(rc_, sc, AF.Ln, ["mss"], ["mrs"], bias=EPS, scale=1.0 / D)
            ACT(rc_, rc_, AF.Exp, ["mrs"], ["mrs"], scale=-0.5)
            STT("dve", mnb[:], mt[:, mb, :], rc_, mg_b[:], ALU.mult, ALU.mult, [f"mt{mb}", "mrs", "mg_b"], ["mnb"])
            for c in range(8):
                S.pe(lambda e, c=c: e.transpose(out=PTt[:, c * 128:(c + 1) * 128], in_=mnb[:, c * 128:(c + 1) * 128],
                                                identity=cbm(C_IDENT)), ["mnb", "cst"], ["b7"])
            ACT(mnT[:, :, mb * 128:(mb + 1) * 128], PTt[:, :].rearrange("p (c n) -> p c n", c=8), AF.Copy, ["b7"], ["mnT"])
        for c in range(2):
            bk = nxt() % 2
            for kc in range(8):
                MM(bank(bk)[:, 0:256], sl["mK"].ap[:, kc, c * 128:(c + 1) * 128], mnT[:, kc, :], kc == 0, kc == 7,
                   [sl["mK"].res, "mnT"], [bres(bk)])
            headnorm(bk, 256, pv[:, 3:4], kmT[:, c, :], "kmT", 2 + bk, hb)
        slab_release(sl["mK"])
        for mb in range(2):
            bk = 4 + mb
            for kc in range(8):
                MM(bank(bk)[:, 0:256], mnT[:, kc, mb * 128:(mb + 1) * 128], sl["mV"].ap[:, kc, 0:256], kc == 0, kc == 7,
                   [sl["mV"].res, "mnT"], [bres(bk)])
            COPY("dve", Vm[:, mb, :], bank(bk)[:, 0:256], [bres(bk)], ["Vm"])
        slab_release(sl["mV"])
        for c in range(2):
            for t in range(NT):
                bk = nxt() % 2
                proj_fm(sl["Eq"], c * 128, t, bk)
                headnorm(bk, 512, pv[:, 2:3], qm[:, c, t * 512:(t + 1) * 512], f"Eq{c}_{t}", 2 + bk, hb)
        slab_release(sl["Eq"])
        scale = 64 ** -0.5
        for c in range(2):
            for t in range(NT):
                for hh in range(2):
                    h = 2 * c + hh
                    pb = hh * 64
                    hs = slice(pb, pb + 64)
                    for mb in range(2):
                        sbk = nxt("sbk") % 4
                        k3 = nxt("k3") % 3
                        MM(bank(sbk), kmT[hs, c, mb * 128:(mb + 1) * 128], qm[hs, c, t * 512:(t + 1) * 512], True, True,
                           ["kmT", f"Eq{c}_{t}"], [bres(sbk)])
                        ACT(ptb[:, k3, :], bank(sbk), AF.Exp, [bres(sbk)], [f"pt{k3}"], scale=scale)
                        MM(bank(4)[hs, :], Vm[:, mb, h * 64:(h + 1) * 64], ptb[:, k3, :], mb == 0, mb == 1,
                           ["Vm", f"pt{k3}"], [bres(4)])
                        MM(bank(5)[hs, :], cbm(C_ONES, 64), ptb[:, k3, :], mb == 0, mb == 1, [f"pt{k3}", "cst"], [bres(5)])
                ACT(ar_rec[:], bank(5), AF.Ln, [bres(5)], ["rec"])
                ACT(ar_rec[:], ar_rec[:], AF.Exp, ["rec"], ["rec"], scale=-1.0)
                TT("dve", ar_rec[:], bank(4), ar_rec[:], ALU.mult, [bres(4), "rec"], ["rec"])
                proj_fm(sl["Eg"], c * 128, t, 6)
                ACT(sge[:], bank(6), AF.Silu, [bres(6)], ["sge"])
                TT("dve", mixT[:, 8 + c, t * 512:(t + 1) * 512], ar_rec[:], sge[:], ALU.mult, ["rec", "sge"], [f"mx{8 + c}_{t}"])
        slab_release(sl["Eg"])

    def wout_phase(b, l):
        S.barrier()
        src = x[b] if l == 0 else xs
        if l < NLAYER - 1:
            dst = xs
        else:
            dst = y[b]
        for blk in range(NB):
            k = nxt("xst") % 2
            xr = f"xst{k}"
            rd = [] if l == 0 else [f"xs{blk}"]
            S.dma("sp", xr, xst[:, k, :], src[blk * 128:(blk + 1) * 128, :], reads=rd, writes=[xr])
            for dh in range(2):
                bk = nxt() % 4
                for kc in range(10):
                    MM(bank(bk), mixT[:, kc, blk * 128:(blk + 1) * 128], Wo[:, kc, dh * 512:(dh + 1) * 512], kc == 0, kc == 9,
                       [f"mx{kc}_{blk // 4}", "Wo"], [bres(bk)])
                TT("dve", xst[:, k, dh * 512:(dh + 1) * 512], xst[:, k, dh * 512:(dh + 1) * 512], bank(bk), ALU.add,
                   [xr, bres(bk)], [xr])
            if l < NLAYER - 1:
                S.dma("sp", "xout", dst[blk * 128:(blk + 1) * 128, :], xst[:, k, :], reads=[xr], writes=[f"xs{blk}"])
            else:
                S.dma("sp", "xout", dst[blk * 128:(blk + 1) * 128, :], xst[:, k, :], reads=[xr], writes=["yout"])

    ar_rec = None
    if dbg:
        MEMSET("dve", mixT[:], 0.0, [f"mx{c}_{t}" for c in range(10) for t in range(NT)])
    for b in range(NSEQ):
        for l in range(NLAYER):
            load_params(l)
            sl = {}
            for (k, c0, n) in (("Aq", 0, 256), ("Ak", 256, 256), ("Av", 512, 256), ("Aff", 1024, 4), ("Ag", 768, 256)):
                sl[k] = slab_load(win_cols(l, c0, n), n)
            rms_phase(b, l)
            if "A" in groups:
                group_A(b, l, sl)
            else:
                for k in ("Aq", "Ak", "Av", "Aff", "Ag"):
                    slab_release(sl[k])
            for (k, c0, n) in (("Bq", 1028, 256), ("Bk", 1284, 256), ("Bv", 1540, 256), ("Bg", 1796, 256)):
                sl[k] = slab_load(win_cols(l, c0, n), n)
            if "B" in groups:
                group_B(b, l, sl)
            else:
                for k in ("Bq", "Bk", "Bv", "Bg"):
                    slab_release(sl[k])
            for (k, c0, n) in (("Cq", 2052, 256), ("Cf", 2308, 256), ("Ci", 2564, 256), ("Cg", 2820, 256)):
                sl[k] = slab_load(win_cols(l, c0, n), n)
            for kc in range(10):
                S.dma("pool", "Wo", Wo[:, kc, :], w_out[l, kc * 128:(kc + 1) * 128, :], writes=["Wo"])
            if "C" in groups:
                group_C(b, l, sl)
            else:
                for k in ("Cq", "Cf", "Ci", "Cg"):
                    slab_release(sl[k])
            for (k, c0, n) in (("Dv", 3076, 256), ("Dg", 3332, 256)):
                sl[k] = slab_load(win_cols(l, c0, n), n)
            sl["mK"] = slab_load(wkv_cols(l, 0, 256), 256)
            sl["mV"] = slab_load(wkv_cols(l, 256, 256), 256)
            if "D" in groups:
                group_D(b, l, sl)
            else:
                slab_release(sl["Dv"])
                slab_release(sl["Dg"])
            for (k, c0, n) in (("Eq", 3588, 256), ("Eg", 3844, 256)):
                sl[k] = slab_load(win_cols(l, c0, n), n)
            if "E" in groups:
                group_E(b, l, sl)
            else:
                for k in ("mK", "mV", "Eq", "Eg"):
                    slab_release(sl[k])
            wout_phase(b, l)
    fin_reads = ["yout"]
    if dbg:
        S.barrier()
        S.dma("sp", "dbg", dbg_out, mixT[:], reads=[f"mx{c}_{t}" for c in range(10) for t in range(NT)], writes=["dbgo"])
        fin_reads.append("dbgo")
    S.add("sp", None, reads=fin_reads)
    S.emit(st)
    st.close()
    nc._arena_peak = ar.peak
    return nc


_CACHE = {}


def kernel(x, mem, norm_g, w_in, fox_f_bias, fox_q_norm, fox_k_norm, hgrn_lb_logits, hgrn_out_norm, pool_w,
           pool_scale, mem_norm_g, mem_w_kv, mem_q_norm, mem_k_norm, w_out):
    ncores = 8
    f = lambda a: np.ascontiguousarray(np.asarray(a, dtype=np.float32))
    x = f(x)
    mem = f(mem)
    B = x.shape[0]
    per = B // ncores
    if "nc" not in _CACHE:
        _CACHE["nc"] = build_nc(per, 2)
    nc = _CACHE["nc"]
    shared = dict(norm_g=f(norm_g), w_in=f(w_in), fox_f_bias=f(fox_f_bias), fox_q_norm=f(fox_q_norm),
                  fox_k_norm=f(fox_k_norm), hgrn_lb_logits=f(hgrn_lb_logits), hgrn_out_norm=f(hgrn_out_norm),
                  pool_w=f(pool_w), pool_scale=f(pool_scale), mem_norm_g=f(mem_norm_g), mem_w_kv=f(mem_w_kv),
                  mem_q_norm=f(mem_q_norm), mem_k_norm=f(mem_k_norm), w_out=f(w_out), cpack=make_cpack())
    in_maps = []
    for i in range(ncores):
        d = dict(shared)
        d["x"] = np.ascontiguousarray(x[i * per:(i + 1) * per])
        d["mem"] = np.ascontiguousarray(mem[i * per:(i + 1) * per])
        in_maps.append(d)
    res = run_bass_kernel_spmd(nc, in_maps, core_ids=list(range(ncores)))
    return np.concatenate([np.asarray(r["y"], dtype=np.float32) for r in res.results], axis=0)
```

```python
import numpy as np
from contextlib import ExitStack
import concourse.bass as bass
import concourse.mybir as mybir
from concourse.bass_utils import run_bass_kernel_spmd

F32 = mybir.dt.float32
BF16 = mybir.dt.bfloat16
AF = mybir.ActivationFunctionType
ALU = mybir.AluOpType

ENGS = ("pe", "act", "dve", "pool", "sp")


class Op:
    __slots__ = ("eng", "fn", "reads", "writes", "chan", "deps", "mile", "cnt", "waits", "idx")

    def __init__(self, eng, fn, reads, writes, chan):
        self.eng, self.fn, self.reads, self.writes, self.chan = eng, fn, tuple(reads), tuple(writes), chan
        self.deps = []
        self.mile = False
        self.cnt = 0
        self.waits = []


class Sched:
    def __init__(self, nc):
        self.nc = nc
        self.ops = []

    def add(self, eng, fn, reads=(), writes=(), chan=None):
        op = Op(eng, fn, reads, writes, chan)
        op.idx = len(self.ops)
        self.ops.append(op)
        return op

    def pe(self, fn, reads=(), writes=()):
        return self.add("pe", fn, reads, writes)

    def act(self, fn, reads=(), writes=()):
        return self.add("act", fn, reads, writes)

    def dve(self, fn, reads=(), writes=()):
        return self.add("dve", fn, reads, writes)

    def pool(self, fn, reads=(), writes=()):
        return self.add("pool", fn, reads, writes)

    def dma(self, q, chan, out, in_, reads=(), writes=(), **kw):
        return self.add(q, lambda e: e.dma_start(out=out, in_=in_, **kw), reads, writes, chan=chan)

    def barrier(self):
        return self.add("bar", None)

    def resolve(self):
        ops = self.ops
        last_w = {}
        readers = {}
        last_comp = {}
        pend = {e: [] for e in ENGS}
        for op in ops:
            if op.eng == "bar":
                L = list(last_comp.values())
                for e in ENGS:
                    pend[e] = list(L)
                continue
            deps = set(pend[op.eng])
            pend[op.eng] = []
            for r in op.reads:
                w = last_w.get(r)
                if w is not None:
                    deps.add(w)
            for r in op.writes:
                w = last_w.get(r)
                if w is not None:
                    deps.add(w)
                for rd in readers.get(r, ()):
                    deps.add(rd)
            deps.discard(op.idx)
            op.deps = sorted(deps)
            for r in op.writes:
                last_w[r] = op.idx
                readers[r] = []
            for r in op.reads:
                if r not in op.writes:
                    readers.setdefault(r, []).append(op.idx)
            if op.chan is None:
                last_comp[op.eng] = op.idx
        chan_cnt = {}
        need = []
        for op in ops:
            lst = []
            if op.eng != "bar":
                for d in op.deps:
                    dop = ops[d]
                    if dop.chan is not None:
                        lst.append(("chan", dop.chan, chan_cnt[dop.chan]))
                    else:
                        if dop.eng == op.eng and op.chan is None:
                            if op.eng == "pe":
                                continue
                        dop.mile = True
                        lst.append(("eng", dop.eng, d))
            need.append(lst)
            if op.chan is not None:
                chan_cnt[op.chan] = chan_cnt.get(op.chan, 0) + 1
        ecnt = {e: 0 for e in ENGS}
        for op in ops:
            if op.eng != "bar" and op.chan is None and op.mile:
                ecnt[op.eng] += 1
                op.cnt = ecnt[op.eng]
        waited = {e: {} for e in ENGS}
        for op, lst in zip(ops, need):
            if op.eng == "bar":
                continue
            w = {}
            for kind, key, v in lst:
                if kind == "chan":
                    k = ("chan", key)
                    val = 16 * v
                else:
                    k = ("eng", key)
                    val = ops[v].cnt
                if val > w.get(k, 0):
                    w[k] = val
            out = []
            for k, val in w.items():
                if waited[op.eng].get(k, 0) >= val:
                    continue
                waited[op.eng][k] = val
                out.append((k, val))
            op.waits = out
        self.chans = sorted({op.chan for op in ops if op.chan is not None})

    def emit(self, stack):
        nc = self.nc
        self.resolve()
        sems = {}
        for e in ENGS:
            sems[("eng", e)] = stack.enter_context(nc.semaphore("s_" + e))
        for c in self.chans:
            sems[("chan", c)] = stack.enter_context(nc.semaphore("c_" + str(c)))
        per = {e: [op for op in self.ops if op.eng == e] for e in ENGS}
        block = stack.enter_context(nc.Block())

        def run(eng_handle, lst):
            for op in lst:
                for k, val in op.waits:
                    eng_handle.wait_ge(sems[k], val)
                if op.fn is None:
                    continue
                inst = op.fn(eng_handle)
                if op.chan is not None:
                    inst.then_inc(sems[("chan", op.chan)], 16)
                elif op.mile:
                    inst.then_inc(sems[("eng", op.eng)], 1)

        @block.tensor
        def _(e):
            run(e, per["pe"])

        @block.scalar
        def _(e):
            run(e, per["act"])

        @block.vector
        def _(e):
            run(e, per["dve"])

        @block.gpsimd
        def _(e):
            run(e, per["pool"])

        @block.sync
        def _(e):
            run(e, per["sp"])


import os
CSTOP = int(os.environ.get('CSTOP', '99'))
T = 2048
D = 1024
DIN = 4100
NB = 16
NT = 4
EPS = 1e-6
POOL_WINDOWS = (2, 4, 8, 16)

C_IDENT, C_NEGID, C_TRI, C_ONES, C_NEGM, C_MLT, C_NUINCL, C_TBD, C_TBDU, C_BONES, C_NONES = range(11)
C_MASK64 = 11 * 128
C_RC = C_MASK64 + 64
C_INVW = C_RC + 32
C_ZERO = C_INVW + 2
NCF = C_ZERO + 64


def make_cpack():
    p = np.arange(128)[:, None]
    c = np.arange(128)[None, :]
    cp = np.zeros((128, NCF), np.float32)

    def put(i, m):
        cp[:, i * 128:(i + 1) * 128] = m.astype(np.float32)

    put(C_IDENT, p == c)
    put(C_NEGID, -(p == c).astype(np.float32))
    put(C_TRI, p <= c)
    put(C_ONES, np.ones((128, 128)))
    put(C_NEGM, np.where(p > c, -1e30, 0.0))
    put(C_MLT, p < c)
    put(C_NUINCL, -(p >= c).astype(np.float32))
    same = (p // 64) == (c // 64)
    put(C_TBD, same & (p <= c))
    put(C_TBDU, same & (p > c))
    put(C_BONES, same)
    put(C_NONES, -np.ones((128, 128)))
    cp[:, C_MASK64:C_MASK64 + 64] = ((p % 64) <= np.arange(64)[None, :]).astype(np.float32)
    for ch in range(2):
        for half in range(2):
            win = POOL_WINDOWS[2 * ch + half]
            pos = np.arange(1, 17, dtype=np.float32)
            cp[half * 64:(half + 1) * 64, C_RC + ch * 16:C_RC + (ch + 1) * 16] = 1.0 / np.minimum(pos, win)[None, :]
            cp[half * 64:(half + 1) * 64, C_INVW + ch] = 1.0 / win
    return cp


class Arena:
    def __init__(self, ap, nbytes):
        self.ap, self.cap, self.off = ap, nbytes, 0
        self.peak = 0

    def alloc(self, shape, dtype):
        isz = 4 if dtype == F32 else 2
        n = int(np.prod(shape[1:]))
        nb = n * isz
        off = (self.off + 63) // 64 * 64
        assert off + nb <= self.cap, ("arena overflow", off + nb, self.cap)
        a = self.ap[:, off // 2:(off + nb) // 2]
        if dtype == F32:
            a = a.bitcast(F32)
        if len(shape) == 3:
            a = a.rearrange("p (c n) -> p c n", c=shape[1])
        elif len(shape) == 4:
            a = a.rearrange("p (a b n) -> p a b n", a=shape[1], b=shape[2])
        self.off = off + nb
        self.peak = max(self.peak, self.off)
        return a


def build_nc(NSEQ=4, NLAYER=2, dbg=False, groups="ABCDE"):
    nc = bass.Bass("TRN2", target_bir_lowering=False)

    def din(name, shape):
        return nc.dram_tensor(name, list(shape), F32, kind="ExternalInput").ap()

    x = din("x", [NSEQ, T, D])
    mem = din("mem", [NSEQ, 256, D])
    norm_g = din("norm_g", [2, D])
    w_in = din("w_in", [2, D, DIN])
    fox_f_bias = din("fox_f_bias", [2, 4])
    fox_q_norm = din("fox_q_norm", [2, 64])
    fox_k_norm = din("fox_k_norm", [2, 64])
    hgrn_lb_logits = din("hgrn_lb_logits", [2, 256])
    hgrn_out_norm = din("hgrn_out_norm", [2, 256])
    pool_w = din("pool_w", [2, 4, 64, 64])
    pool_scale = din("pool_scale", [2, 256])
    mem_norm_g = din("mem_norm_g", [2, D])
    mem_w_kv = din("mem_w_kv", [2, D, 512])
    mem_q_norm = din("mem_q_norm", [2, 64])
    mem_k_norm = din("mem_k_norm", [2, 64])
    w_out = din("w_out", [2, 1280, D])
    cpack = din("cpack", [128, NCF])
    y = nc.dram_tensor("y", [NSEQ, T, D], F32, kind="ExternalOutput").ap()
    xs = nc.dram_tensor("xs_scratch", [T, D], F32).ap()
    if dbg:
        dbg_out = nc.dram_tensor("dbg", [128, 10, T], BF16, kind="ExternalOutput").ap()

    st = ExitStack()
    S = Sched(nc)

    def sb(name, shape, dtype):
        return st.enter_context(nc.sbuf_tensor(name, list(shape), dtype))

    hT = sb("hT", [128, 8, T], BF16)
    mixT = sb("mixT", [128, 10, T], BF16)
    Wo = sb("Wo", [128, 10, D], BF16)
    ws = sb("ws", [128, 5, 8, 256], BF16)
    cf = sb("cf", [128, NCF], F32)
    cb = sb("cb", [128, NCF], BF16)
    g_b = sb("g_b", [128, D], F32)
    mg_b = sb("mg_b", [128, D], F32)
    xst = sb("xst", [128, 2, D], F32)
    hbuf = sb("hbuf", [128, 2, D], BF16)
    sqj = sb("sqj", [128, D], BF16)
    ssb = sb("ssb", [128, 32], F32)
    pv = sb("pv", [128, 8], F32)
    hn = sb("hn", [128, 2], F32)
    psc = sb("psc", [128, 2], F32)
    lbl = sb("lbl", [128, 2, 2], F32)
    lblb = sb("lblb", [128, 2, 256], F32)
    lbp = sb("lbp", [128, 4], F32)
    lb_b = sb("lb_b", [128, 256], F32)
    oml_b = sb("oml_b", [128, 256], F32)
    fb_b = sb("fb_b", [128, 4], F32)
    pwf = sb("pwf", [128, 2, 128], F32)
    pwb = sb("pwb", [128, 2, 128], BF16)
    kmT = sb("kmT", [128, 2, 256], BF16)
    Vm = sb("Vm", [128, 2, 256], BF16)
    ARENA_BYTES = 54 * 1024
    arena_t = sb("arena", [128, ARENA_BYTES // 2], BF16)
    ar = Arena(arena_t, ARENA_BYTES)

    PD = [st.enter_context(nc.psum_tensor(f"pd{i}", [128, 1024], F32)) for i in range(3)]
    PX = st.enter_context(nc.psum_tensor("px", [128, 512], F32))
    PTt = st.enter_context(nc.psum_tensor("ptt", [128, 1024], BF16))

    def bank(i):
        if i < 6:
            return PD[i // 2][:, (i % 2) * 512:(i % 2 + 1) * 512]
        return PX[:, :]

    def bres(i):
        return f"b{i}"

    def MM(out, lhsT, rhs, start, stop, r, w):
        S.pe(lambda e: e.matmul(out, lhsT=lhsT, rhs=rhs, start=start, stop=stop, skip_group_check=True), r, w)

    def ACT(out, in_, func, r, w, **kw):
        S.act(lambda e: e.activation(out=out, in_=in_, func=func, **kw), r, w)

    def TT(eng, out, in0, in1, op, r, w):
        S.add(eng, lambda e: e.tensor_tensor(out=out, in0=in0, in1=in1, op=op), r, w)

    def TS(eng, out, in0, s1, s2, op0, op1, r, w):
        if s2 is None:
            S.add(eng, lambda e: e.tensor_scalar(out=out, in0=in0, scalar1=s1, scalar2=None, op0=op0), r, w)
        else:
            S.add(eng, lambda e: e.tensor_scalar(out=out, in0=in0, scalar1=s1, scalar2=s2, op0=op0, op1=op1), r, w)

    def STT(eng, out, in0, scalar, in1, op0, op1, r, w):
        S.add(eng, lambda e: e.scalar_tensor_tensor(out=out, in0=in0, scalar=scalar, in1=in1, op0=op0, op1=op1), r, w)

    def COPY(eng, out, in_, r, w):
        S.add(eng, lambda e: e.tensor_copy(out=out, in_=in_), r, w)

    def RECIP(out, in_, r, w):
        S.dve(lambda e: e.reciprocal(out=out, in_=in_), r, w)

    def MEMSET(eng, ap, val, w):
        S.add(eng, lambda e: e.memset(ap, val), (), w)

    def cbm(i, n=128):
        return cb[:, i * 128:i * 128 + n]

    def cfm(i, n=128):
        return cf[:, i * 128:i * 128 + n]

    S.dma("sp", "cst", cf[:], cpack, writes=["cst"])
    COPY("dve", cb[:], cf[:], ["cst"], ["cst"])

    slab_free = [True] * 5

    class Slab:
        pass

    def slab_load(src3d, ncols):
        slot = slab_free.index(True)
        slab_free[slot] = False
        s = Slab()
        s.slot, s.res, s.ap = slot, f"ws{slot}", ws[:, slot]
        S.dma("pool", f"ws{slot}", ws[:, slot, :, 0:ncols], src3d, writes=[s.res])
        return s

    def slab_release(s):
        slab_free[s.slot] = True

    def win_cols(l, c0, n):
        return w_in[l].rearrange("(c p) n -> p c n", p=128)[:, :, c0:c0 + n]

    def wkv_cols(l, c0, n):
        return mem_w_kv[l].rearrange("(c p) n -> p c n", p=128)[:, :, c0:c0 + n]

    def proj_fm(s, col0, t, bk):
        for kc in range(8):
            MM(bank(bk), s.ap[:, kc, col0:col0 + 128], hT[:, kc, t * 512:(t + 1) * 512], kc == 0, kc == 7,
               [s.res, f"hT{t}"], [bres(bk)])

    def proj_tm(s, col0, ncols, blk, out_ap, ores):
        for kc in range(8):
            MM(out_ap, hT[:, kc, blk * 128:(blk + 1) * 128], s.ap[:, kc, col0:col0 + ncols], kc == 0, kc == 7,
               [s.res, f"hT{blk // 4}"], [ores])

    rot = {}

    def nxt(name="g"):
        rot[name] = rot.get(name, 0) + 1
        return rot[name]

    def headnorm(src_bk, N, gain_col, out_ap, out_res, stat_bk, hb):
        k = nxt("hn") % 2
        sq, lnv = hb
        src = bank(src_bk)[:, 0:N]
        ACT(sq[:, k, 0:N], src, AF.Square, [bres(src_bk)], [f"hnsq{k}"])
        MM(bank(stat_bk)[:, 0:N], cbm(C_BONES), sq[:, k, 0:N], True, True, [f"hnsq{k}", "cst"], [bres(stat_bk)])
        ACT(lnv[:, k, 0:N], bank(stat_bk)[:, 0:N], AF.Ln, [bres(stat_bk)], [f"hnln{k}"], bias=EPS, scale=1.0 / 64)
        ACT(lnv[:, k, 0:N], lnv[:, k, 0:N], AF.Exp, [f"hnln{k}"], [f"hnln{k}"], scale=-0.5)
        STT("dve", out_ap, src, gain_col, lnv[:, k, 0:N], ALU.mult, ALU.mult,
            [bres(src_bk), f"hnln{k}", "pv"], [out_res])

    def hn_units(units, hb):
        n = len(units)
        if n:
            units[0][0](0)
        for u in range(n):
            if u + 1 < n:
                units[u + 1][0]((u + 1) % 4)
            _, N, gcol, oap, ores = units[u]
            headnorm(u % 4, N, gcol, oap, ores, 4 + u % 2, hb)

    def load_params(l):
        S.dma("sp", "g_b", g_b[:], norm_g[l].partition_broadcast(128), writes=["g_b"])
        S.dma("sp", "mg_b", mg_b[:], mem_norm_g[l].partition_broadcast(128), writes=["mg_b"])
        for i, src in enumerate((fox_q_norm, fox_k_norm, mem_q_norm, mem_k_norm)):
            col = src[l].rearrange("(p o) -> p o", o=1)
            S.dma("sp", "pv", pv[0:64, i:i + 1], col, writes=["pv"])
            S.dma("sp", "pv", pv[64:128, i:i + 1], col, writes=["pv"])
        S.dma("sp", "hn", hn[:], hgrn_out_norm[l].rearrange("(c p) -> p c", p=128), writes=["hn"],
              allow_slow_non_contiguous=True)
        S.dma("sp", "psc", psc[:], pool_scale[l].rearrange("(c p) -> p c", p=128), writes=["psc"],
              allow_slow_non_contiguous=True)
        S.dma("sp", "fb_b", fb_b[:], fox_f_bias[l].partition_broadcast(128), writes=["fb_b"])
        MEMSET("dve", pwf[:], 0.0, ["pwf"])
        for g in range(4):
            hs = slice((g % 2) * 64, (g % 2) * 64 + 64)
            S.dma("sp", "pwf", pwf[hs, g // 2, (g % 2) * 64:(g % 2) * 64 + 64], pool_w[l, g], writes=["pwf"])
        COPY("dve", pwb[:], pwf[:], ["pwf"], ["pwb"])
        if l == 0:
            MEMSET("dve", lbp[:, 0:2], 0.0, ["lb"])
            MEMSET("dve", lbp[:, 2:4], 1.0, ["lb"])
            MEMSET("dve", lb_b[:], 0.0, ["lb"])
            MEMSET("dve", oml_b[:], 1.0, ["lb"])
        else:
            S.dma("sp", "lbl", lbl[:], hgrn_lb_logits.rearrange("l (c p) -> p l c", p=128), writes=["lbl"],
                  allow_slow_non_contiguous=True)
            S.dma("sp", "lblb", lblb[:], hgrn_lb_logits.partition_broadcast(128), writes=["lblb"])
            for (dst_lb, dst_oml, a0, a1) in ((lbp[:, 0:2], lbp[:, 2:4], lbl[:, 0, :], lbl[:, 1, :]),
                                             (lb_b[:], oml_b[:], lblb[:, 0, :], lblb[:, 1, :])):
                TT("dve", dst_oml, a1, a0, ALU.subtract, ["lbl", "lblb", "lb"], ["lb"])
                ACT(dst_oml, dst_oml, AF.Exp, ["lb"], ["lb"], scale=-1.0)
                TS("dve", dst_oml, dst_oml, 1.0, None, ALU.add, None, ["lb"], ["lb"])
                RECIP(dst_lb, dst_oml, ["lb"], ["lb"])
                TS("dve", dst_lb, dst_lb, 1.0 - 1e-6, None, ALU.min, None, ["lb"], ["lb"])
                TS("dve", dst_oml, dst_lb, -1.0, 1.0, ALU.mult, ALU.add, ["lb"], ["lb"])

    def rms_phase(b, l):
        src = x[b] if l == 0 else xs
        st1 = {}

        def stage1(blk):
            k = nxt("xst") % 2
            xr = f"xst{k}"
            rd = [] if l == 0 else [f"xs{blk}"]
            S.dma("sp", xr, xst[:, k, :], src[blk * 128:(blk + 1) * 128, :], reads=rd, writes=[xr])
            sc = ssb[:, blk:blk + 1]
            rc_ = ssb[:, 16 + blk:17 + blk]
            ACT(sqj[:], xst[:, k, :], AF.Square, [xr], ["sqj", f"ss{blk}"], accum_out=sc)
            ACT(rc_, sc, AF.Ln, [f"ss{blk}"], [f"rs{blk}"], bias=EPS, scale=1.0 / D)
            ACT(rc_, rc_, AF.Exp, [f"rs{blk}"], [f"rs{blk}"], scale=-0.5)
            hk = blk % 2
            STT("dve", hbuf[:, hk, :], xst[:, k, :], rc_, g_b[:], ALU.mult, ALU.mult,
                [xr, f"rs{blk}", "g_b"], [f"hb{hk}"])

        def stage2(blk):
            hk = blk % 2
            for c in range(8):
                S.pe(lambda e, c=c, hk=hk: e.transpose(out=PTt[:, c * 128:(c + 1) * 128],
                                                       in_=hbuf[:, hk, c * 128:(c + 1) * 128], identity=cbm(C_IDENT)),
                     [f"hb{hk}", "cst"], ["b7"])
            ACT(hT[:, :, blk * 128:(blk + 1) * 128], PTt[:, :].rearrange("p (c n) -> p c n", c=8), AF.Copy,
                ["b7"], [f"hT{blk // 4}"])

        stage1(0)
        for blk in range(NB):
            if blk + 1 < NB:
                stage1(blk + 1)
            stage2(blk)

    def attn_finish(h, c, qt, sg, mixc, with_den):
        pb = (h % 2) * 64
        hs = slice(pb, pb + 64)
        num = bank(4)[hs, :]
        mres = f"mx{mixc}_{qt}"
        if with_den:
            rec = ar_rec
            ACT(rec[hs, :], bank(5)[hs, :], AF.Ln, [bres(5)], ["rec"])
            ACT(rec[hs, :], rec[hs, :], AF.Exp, ["rec"], ["rec"], scale=-1.0)
            TT("dve", rec[hs, :], num, rec[hs, :], ALU.mult, [bres(4), "rec"], ["rec"])
            TT("dve", mixT[hs, mixc, qt * 512:(qt + 1) * 512], rec[hs, :], sg[hs, c, qt * 512:(qt + 1) * 512], ALU.mult,
               ["rec", "sg"], [mres])
        else:
            TT("dve", mixT[hs, mixc, qt * 512:(qt + 1) * 512], num, sg[hs, c, qt * 512:(qt + 1) * 512], ALU.mult,
               [bres(4), "sg"], [mres])

    def gates_full(s, sg):
        for c in range(2):
            for t in range(NT):
                bk = 4 + nxt("gf") % 3
                proj_fm(s, c * 128, t, bk)
                ACT(sg[:, c, t * 512:(t + 1) * 512], bank(bk), AF.Silu, [bres(bk)], ["sg"])

    def group_A(b, l, sl, prefetch=None):
        nonlocal ar_rec
        S.barrier()
        ar.off = 0
        qT = ar.alloc([128, 2, T], BF16)
        kT = ar.alloc([128, 2, T], BF16)
        V = ar.alloc([128, NB, 256], BF16)
        sg = ar.alloc([128, 2, T], BF16)
        csp = ar.alloc([128, NB, 4], F32)
        mark = ar.off
        hb = (ar.alloc([128, 2, 512], BF16), ar.alloc([128, 2, 512], F32))
        tmpf = ar.alloc([128, 64], F32)
        tot = ar.alloc([128, 64], F32)
        inc = ar.alloc([128, 64], F32)
        units = []
        for (s_, dst, gcol, nm) in ((sl["Aq"], qT, 0, "q"), (sl["Ak"], kT, 1, "k")):
            for c in range(2):
                for t in range(NT):
                    units.append((lambda bk, s_=s_, c=c, t=t: proj_fm(s_, c * 128, t, bk), 512, pv[:, gcol:gcol + 1],
                                  dst[:, c, t * 512:(t + 1) * 512], f"A{nm}{c}_{t}"))
        hn_units(units, hb)
        slab_release(sl["Aq"])
        slab_release(sl["Ak"])
        for blk in range(NB):
            bk = 4 + nxt() % 2
            proj_tm(sl["Av"], 0, 256, blk, bank(bk)[:, 0:256], bres(bk))
            COPY("dve", V[:, blk, :], bank(bk)[:, 0:256], [bres(bk)], [f"AV{blk}"])
        slab_release(sl["Av"])
        for blk in range(NB):
            proj_tm(sl["Aff"], 0, 4, blk, bank(6)[:, blk * 4:(blk + 1) * 4], bres(6))
        slab_release(sl["Aff"])
        TT("dve", tmpf[:].rearrange("p (b h) -> p b h", h=4), bank(6)[:, 0:64].rearrange("p (b h) -> p b h", h=4),
           fb_b[:].unsqueeze(1).to_broadcast([128, NB, 4]), ALU.add, [bres(6), "fb_b"], ["tmpf"])
        gates_full(sl["Ag"], sg)
        slab_release(sl["Ag"])
        ACT(tmpf[:], tmpf[:], AF.Exp, ["tmpf"], ["tmpf"], scale=-1.0)
        ACT(tmpf[:], tmpf[:], AF.Ln, ["tmpf"], ["tmpf"], bias=1.0)
        MM(bank(0)[:, 0:64], cfm(C_TRI), tmpf[:], True, True, ["tmpf", "cst"], [bres(0)])
        MM(bank(1)[:, 0:64], cfm(C_ONES), tmpf[:], True, True, ["tmpf", "cst"], [bres(1)])
        COPY("dve", tot[:], bank(1)[:, 0:64], [bres(1)], ["tot"])
        for h in range(4):
            S.dve(lambda e, h=h: e.tensor_tensor_scan(
                out=inc[:].rearrange("p (b h) -> p b h", h=4)[:, :, h], data0=cfm(C_ONES, 16),
                data1=tot[:].rearrange("p (b h) -> p b h", h=4)[:, :, h], initial=0.0, op0=ALU.mult, op1=ALU.add),
                ["tot", "cst"], ["inc"])
        TT("dve", inc[:], inc[:], tot[:], ALU.subtract, ["inc", "tot"], ["inc"])
        TT("dve", csp[:].rearrange("p b h -> p (b h)"), bank(0)[:, 0:64], inc[:], ALU.add, [bres(0), "inc"], ["csp"])
        if prefetch is not None:
            prefetch()
        S.barrier()
        ar.off = mark
        CTn = ar.alloc([128, T], F32)
        argb = ar.alloc([128, 3, 512], F32)
        ptb = ar.alloc([128, 3, 512], BF16)
        ar_rec = ar.alloc([128, 512], F32)
        scale = 64 ** -0.5
        for h in range(4):
            c, pb = h // 2, (h % 2) * 64
            hs = slice(pb, pb + 64)
            for q4 in range(4):
                bk = 4 + q4 % 2
                for i in range(4):
                    blk = q4 * 4 + i
                    MM(bank(bk)[:, i * 128:(i + 1) * 128], csp[:, blk, h:h + 1].to_broadcast([128, 128]), cfm(C_NEGID),
                       True, True, ["csp", "cst"], [bres(bk)])
                ACT(CTn[:, q4 * 512:(q4 + 1) * 512], bank(bk), AF.Copy, [bres(bk)], ["CTn"])
            iters = [(qt, kb) for qt in range(NT) for kb in range(4 * qt + 4)]
            LA = 2

            def emit_S(i):
                qt, kb = iters[i]
                off = max(0, kb - 4 * qt) * 128
                sbk = i % 4
                MM(bank(sbk)[:, off:512], kT[hs, c, kb * 128:(kb + 1) * 128], qT[hs, c, qt * 512 + off:(qt + 1) * 512],
                   True, True, [f"Ak{c}_{kb // 4}", f"Aq{c}_{qt}"], [bres(sbk)])

            for i in range(min(LA, len(iters))):
                emit_S(i)
            for i, (qt, kb) in enumerate(iters):
                if i + LA < len(iters):
                    emit_S(i + LA)
                nk = 4 * qt + 4
                off = max(0, kb - 4 * qt) * 128
                sbk = i % 4
                k3 = i % 3
                STT("dve", argb[:, k3, off:512], bank(sbk)[:, off:512], scale, CTn[:, qt * 512 + off:(qt + 1) * 512],
                    ALU.mult, ALU.add, [bres(sbk), "CTn"], [f"arg{k3}"])
                if kb >= 4 * qt:
                    TT("dve", argb[:, k3, off:off + 128], argb[:, k3, off:off + 128], cfm(C_NEGM), ALU.add,
                       [f"arg{k3}", "cst"], [f"arg{k3}"])
                ACT(ptb[:, k3, off:512], argb[:, k3, off:512], AF.Exp, [f"arg{k3}", "csp"], [f"pt{k3}"],
                    bias=csp[:, kb, h:h + 1])
                MM(bank(4)[hs, off:512], V[:, kb, h * 64:(h + 1) * 64], ptb[:, k3, off:512], kb == 0, kb == nk - 1,
                   [f"AV{kb}", f"pt{k3}"], [bres(4)])
                MM(bank(5)[hs, off:512], cbm(C_ONES, 64), ptb[:, k3, off:512], kb == 0, kb == nk - 1,
                   [f"pt{k3}", "cst"], [bres(5)])
                if kb == nk - 1:
                    attn_finish(h, c, qt, sg, c, True)

    def group_B(b, l, sl, prefetch=None):
        S.barrier()
        ar.off = 0
        qT = ar.alloc([128, 2, T], BF16)
        kT = ar.alloc([128, 2, T], BF16)
        V = ar.alloc([128, NB, 256], BF16)
        sg = ar.alloc([128, 2, T], BF16)
        eb = ar.alloc([128, 2, 512], F32)
        spb = ar.alloc([128, 3, 512], BF16)
        atb = ar.alloc([128, 2, 512], BF16)
        acc = ar.alloc([128, 512], BF16)
        z64 = ar.alloc([128, 64], BF16)
        MEMSET("dve", z64[:], 0.0, ["z64"])
        scale = 64 ** -0.5
        for (s, dst, nm, scl) in ((sl["Bq"], qT, "q", scale), (sl["Bk"], kT, "k", 1.0)):
            for c in range(2):
                for t in range(NT):
                    bk = nxt() % 2
                    proj_fm(s, c * 128, t, bk)
                    ACT(dst[:, c, t * 512:(t + 1) * 512], bank(bk), AF.Copy, [bres(bk)], [f"B{nm}{c}_{t}"], scale=scl)
            slab_release(s)
        for blk in range(NB):
            bk = 4 + nxt() % 2
            proj_tm(sl["Bv"], 0, 256, blk, bank(bk)[:, 0:256], bres(bk))
            COPY("dve", V[:, blk, :], bank(bk)[:, 0:256], [bres(bk)], [f"BV{blk}"])
        slab_release(sl["Bv"])
        gates_full(sl["Bg"], sg)
        slab_release(sl["Bg"])
        if prefetch is not None:
            prefetch()
        for h in range(4):
            c, pb = h // 2, (h % 2) * 64
            hs = slice(pb, pb + 64)
            iters = [(qt, kb) for qt in range(NT) for kb in range(4 * qt + 3, -1, -1)]
            n_it = len(iters)

            def geo(i):
                qt, kb = iters[i]
                off = max(0, kb - 4 * qt) * 128
                return qt, kb, off, kT[hs, c, kb * 128:(kb + 1) * 128], qT[hs, c, qt * 512 + off:(qt + 1) * 512], \
                    [f"Bk{c}_{kb // 4}", f"Bq{c}_{qt}"]

            def emit_Z(i):
                qt, kb, off, ksl, qsl, kq = geo(i)
                MM(bank(i % 3)[:, off:512], ksl, qsl, True, True, kq, [bres(i % 3)])

            def emit_E(i):
                qt, kb, off, ksl, qsl, kq = geo(i)
                za, k2, k3 = i % 3, i % 2, i % 3
                ACT(eb[:, k2, off:512], bank(za)[:, off:512], AF.Exp, [bres(za)], [f"e{k2}"])
                ACT(spb[:, k3, off:512], eb[:, k2, off:512], AF.Ln, [f"e{k2}"], [f"sp{k3}"], bias=1.0)
                if kb >= 4 * qt:
                    TT("pool", spb[:, k3, off:off + 128], spb[:, k3, off:off + 128], cbm(C_MLT), ALU.mult,
                       [f"sp{k3}", "cst"], [f"sp{k3}"])

            emit_Z(0)
            if n_it > 1:
                emit_Z(1)
            emit_E(0)
            for i in range(n_it):
                qt, kb, off, ksl, qsl, kq = geo(i)
                if i + 2 < n_it:
                    emit_Z(i + 2)
                if i + 1 < n_it:
                    emit_E(i + 1)
                if kb == 4 * qt + 3:
                    MEMSET("pool", acc[:], 0.0, ["acc"])
                    MM(bank(4)[hs, :], z64[:], qT[:, c, qt * 512:(qt + 1) * 512], True, False, ["z64", f"Bq{c}_{qt}"], [bres(4)])
                k3, k2 = i % 3, i % 2
                zb = 3 if k2 == 0 else 5
                diag = kb >= 4 * qt
                MM(bank(zb)[:, off:512], cbm(C_NUINCL), spb[:, k3, off:512], True, False, [f"sp{k3}", "cst"], [bres(zb)])
                MM(bank(zb)[:, off:512], cbm(C_NONES), acc[:, off:512], False, False, ["acc", "cst"], [bres(zb)])
                MM(bank(zb)[:, off:512], ksl, qsl, False, True, kq, [bres(zb)])
                ACT(atb[:, k2, off:512], bank(zb)[:, off:512], AF.Exp, [bres(zb)], [f"at{k2}"])
                if diag:
                    TT("pool", atb[:, k2, off:off + 128], atb[:, k2, off:off + 128], cbm(C_MLT), ALU.mult,
                       [f"at{k2}", "cst"], [f"at{k2}"])
                if kb > 0:
                    TT("dve", acc[:, off:512], acc[:, off:512], spb[:, k3, off:512], ALU.add, ["acc", f"sp{k3}"], ["acc"])
                MM(bank(4)[hs, off:512], V[:, kb, h * 64:(h + 1) * 64], atb[:, k2, off:512], False, kb == 0,
                   [f"BV{kb}", f"at{k2}"], [bres(4)])
                if kb == 0:
                    attn_finish(h, c, qt, sg, 2 + c, False)

    def group_C(b, l, sl):
        S.barrier()
        ar.off = 0
        Sall = ar.alloc([128, 9, 2, 64], F32)
        MEMSET("dve", Sall[:, 0], 0.0, ["S0"])
        mark = ar.off
        for t in range(NT):
            S.barrier()
            ar.off = mark
            e_s = ar.alloc([128, 1024], F32)
            f_s = ar.alloc([128, 1024], F32)
            k_s = ar.alloc([128, 1024], F32)
            lf_hi = ar.alloc([128, 1024], BF16)
            lf_lo = ar.alloc([128, 1024], BF16)
            khat = ar.alloc([128, 4, 256], BF16)
            Eq = ar.alloc([128, 2, 512], F32)
            Ek = ar.alloc([128, 2, 512], F32)
            sq_s = ar.alloc([128, 2, 512], F32)
            qt_ = ar.alloc([128, 2, 512], BF16)
            kt_ = ar.alloc([128, 2, 512], BF16)
            Vc = ar.alloc([128, 4, 256], BF16)
            scm = ar.alloc([128, 16, 64], BF16)
            Sb = ar.alloc([128, 8, 2, 64], BF16)
            sqo = ar.alloc([128, 1024], BF16)
            sgc = ar.alloc([128, 512], F32)
            P0, P1, P2 = PD[0], PD[1], PD[2]
            b01, b23, b45 = [bres(0), bres(1)], [bres(2), bres(3)], [bres(4), bres(5)]
            for i in range(4):
                proj_tm(sl["Cf"], 0, 256, 4 * t + i, P0[:, i * 256:(i + 1) * 256], bres(i // 2))
            ACT(e_s[:], P0[:, :], AF.Exp, b01, ["e_s"], scale=-1.0)
            ACT(e_s[:], e_s[:], AF.Ln, ["e_s"], ["e_s"], bias=1.0)
            ACT(e_s[:], e_s[:], AF.Exp, ["e_s"], ["e_s"], scale=-1.0)
            omb = oml_b[:].unsqueeze(1).to_broadcast([128, 4, 256])
            lbb = lb_b[:].unsqueeze(1).to_broadcast([128, 4, 256])
            v3 = lambda a: a.rearrange("p (i n) -> p i n", i=4)
            TT("dve", v3(e_s[:]), v3(e_s[:]), omb, ALU.mult, ["e_s", "lb"], ["e_s"])
            TT("dve", v3(f_s[:]), v3(e_s[:]), lbb, ALU.add, ["e_s", "lb"], ["f_s"])
            TT("dve", v3(k_s[:]), omb, v3(e_s[:]), ALU.subtract, ["e_s", "lb"], ["k_s"])
            ACT(f_s[:], f_s[:], AF.Ln, ["f_s"], ["f_s"])
            COPY("dve", lf_hi[:], f_s[:], ["f_s"], ["lf_hi"])
            TT("dve", lf_lo[:], f_s[:], lf_hi[:], ALU.subtract, ["f_s", "lf_hi"], ["lf_lo"])
            if CSTOP < 2:
                continue
            for i in range(4):
                o = P1[:, i * 256:(i + 1) * 256]
                MM(o, cbm(C_TBDU), lf_hi[:, i * 256:(i + 1) * 256], True, False, ["lf_hi", "cst"], [bres(2 + i // 2)])
                MM(o, cbm(C_TBDU), lf_lo[:, i * 256:(i + 1) * 256], False, True, ["lf_lo", "cst"], [bres(2 + i // 2)])
            ACT(e_s[:], P1[:, :], AF.Exp, b23, ["e_s"])
            TT("dve", khat[:].rearrange("p i n -> p (i n)"), k_s[:], e_s[:], ALU.mult, ["k_s", "e_s"], ["khat"])
            for c in range(2):
                for i in range(4):
                    o = P2[:, c * 512 + i * 128:c * 512 + (i + 1) * 128]
                    MM(o, lf_hi[:, i * 256 + c * 128:i * 256 + (c + 1) * 128], cbm(C_TBD), True, False,
                       ["lf_hi", "cst"], [bres(4 + c)])
                    MM(o, lf_lo[:, i * 256 + c * 128:i * 256 + (c + 1) * 128], cbm(C_TBD), False, True,
                       ["lf_lo", "cst"], [bres(4 + c)])
            ACT(Eq[:].rearrange("p c n -> p (c n)"), P2[:, :], AF.Exp, b45, ["Eq"])
            ACT(Ek[:].rearrange("p c n -> p (c n)"), P2[:, :], AF.Exp, b45, ["Ek"], scale=-1.0)
            if CSTOP < 3:
                continue
            for c in range(2):
                bk = 6
                proj_fm(sl["Cq"], c * 128, t, bk)
                ACT(sq_s[:, c, :], bank(bk), AF.Silu, [bres(bk)], [f"sq_s{c}"])
                TT("dve", qt_[:, c, :], sq_s[:, c, :], Eq[:, c, :], ALU.mult, [f"sq_s{c}", "Eq"], ["qt_"])
            for c in range(2):
                bk = 6
                proj_fm(sl["Cf"], c * 128, t, bk)
                ACT(sq_s[:, c, :], bank(bk), AF.Exp, [bres(bk)], [f"sq_s{c}"])
                ACT(sq_s[:, c, :], sq_s[:, c, :], AF.Ln, [f"sq_s{c}"], [f"sq_s{c}"], bias=1.0)
                ACT(sq_s[:, c, :], sq_s[:, c, :], AF.Exp, [f"sq_s{c}"], [f"sq_s{c}"], scale=-1.0)
                STT("dve", kt_[:, c, :], sq_s[:, c, :], lbp[:, 2 + c:3 + c], Ek[:, c, :], ALU.mult, ALU.mult,
                    [f"sq_s{c}", "Ek", "lb"], ["kt_"])
            if CSTOP < 4:
                continue
            for i in range(4):
                proj_tm(sl["Ci"], 0, 256, 4 * t + i, P0[:, i * 256:(i + 1) * 256], bres(i // 2))
            COPY("dve", Vc[:].rearrange("p i n -> p (i n)"), P0[:, :], b01, ["Vc"])
            if CSTOP < 5:
                continue
            for hp in range(2):
                pb = hp * 64
                for ch in range(8):
                    i, half = ch // 2, ch % 2
                    for j in range(2):
                        slot = i * 2 + j
                        MM(P1[half * 64:(half + 1) * 64, hp * 512 + slot * 64:hp * 512 + (slot + 1) * 64],
                           kt_[pb:pb + 64, j, ch * 64:(ch + 1) * 64], qt_[pb:pb + 64, j, ch * 64:(ch + 1) * 64],
                           True, True, ["kt_", "qt_"], [bres(2 + hp)])
            TT("dve", scm[:], P1[:, :].rearrange("p (s n) -> p s n", n=64),
               cb[:, C_MASK64:C_MASK64 + 64].unsqueeze(1).to_broadcast([128, 16, 64]), ALU.mult, b23 + ["cst"], ["scm"])
            if CSTOP < 6:
                continue
            def uslot(ch, j):
                return (ch % 2) * 512 + ((ch // 2) * 2 + j) * 64
            for half in range(2):
                for ch in range(half, 8, 2):
                    i = ch // 2
                    for h in range(4):
                        j = h // 2
                        o0 = uslot(ch, j)
                        MM(P0[(h % 2) * 64:(h % 2) * 64 + 64, o0:o0 + 64],
                           khat[half * 64:(half + 1) * 64, i, h * 64:(h + 1) * 64],
                           Vc[half * 64:(half + 1) * 64, i, h * 64:(h + 1) * 64], True, True, ["khat", "Vc"], [bres(half)])
            if CSTOP < 7:
                continue
            for ch in range(8):
                for j in range(2):
                    o0 = uslot(ch, j)
                    STT("dve", Sall[:, ch + 1, j, :], Sall[:, ch, j, :], Eq[:, j, ch * 64 + 63:ch * 64 + 64],
                        P0[:, o0:o0 + 64], ALU.mult, ALU.add, [f"S{ch}", "Eq", bres(ch % 2)], [f"S{ch + 1}"])
            ACT(Sb[:].rearrange("p a b n -> p (a b n)"), Sall[:, 0:8].rearrange("p a b n -> p (a b n)"), AF.Copy,
                [f"S{k}" for k in range(8)], ["Sb"])
            COPY("dve", Sall[:, 0].rearrange("p b n -> p (b n)"), Sall[:, 8].rearrange("p b n -> p (b n)"), ["S8"], ["S0"])
            if CSTOP < 8:
                continue
            for hp in range(2):
                pb = hp * 64
                for j in range(2):
                    for ch in range(8):
                        MM(P2[pb:pb + 64, j * 512 + ch * 64:j * 512 + (ch + 1) * 64], Sb[pb:pb + 64, ch, j, :],
                           qt_[pb:pb + 64, j, ch * 64:(ch + 1) * 64], True, True, ["Sb", "qt_"], [bres(4 + j)])
            for half in range(2):
                for ch in range(half, 8, 2):
                    i = ch // 2
                    for h in range(4):
                        j, hp = h // 2, h % 2
                        o0 = half * 512 + (j * 4 + i) * 64
                        MM(P0[hp * 64:(hp + 1) * 64, o0:o0 + 64],
                           Vc[half * 64:(half + 1) * 64, i, h * 64:(h + 1) * 64],
                           scm[half * 64:(half + 1) * 64, hp * 8 + i * 2 + j, :], True, True, ["Vc", "scm"], [bres(half)])
            for half in range(2):
                COPY("dve", k_s[:].rearrange("p (j i h t) -> p j i h t", j=2, i=4, h=2)[:, :, :, half, :],
                     P0[:, half * 512:(half + 1) * 512].rearrange("p (j i t) -> p j i t", j=2, i=4), [bres(half)], ["k_s"])
            TT("dve", f_s[:], P2[:, :], k_s[:], ALU.add, b45 + ["k_s"], ["f_s"])
            if CSTOP < 9:
                continue
            ACT(sqo[:], f_s[:], AF.Square, ["f_s"], ["sqo"])
            for j in range(2):
                MM(bank(2 + j), cbm(C_BONES), sqo[:, j * 512:(j + 1) * 512], True, True, ["sqo", "cst"], [bres(2 + j)])
            ACT(e_s[:], P1[:, :], AF.Ln, b23, ["e_s"], bias=EPS, scale=1.0 / 64)
            ACT(e_s[:], e_s[:], AF.Exp, ["e_s"], ["e_s"], scale=-0.5)
            for j in range(2):
                STT("dve", f_s[:, j * 512:(j + 1) * 512], f_s[:, j * 512:(j + 1) * 512], hn[:, j:j + 1],
                    e_s[:, j * 512:(j + 1) * 512], ALU.mult, ALU.mult, ["f_s", "e_s", "hn"], ["f_s"])
                proj_fm(sl["Cg"], j * 128, t, 6)
                ACT(sgc[:], bank(6), AF.Silu, [bres(6)], ["sgc"])
                TT("dve", mixT[:, 4 + j, t * 512:(t + 1) * 512], f_s[:, j * 512:(j + 1) * 512], sgc[:], ALU.mult,
                   ["f_s", "sgc"], [f"mx{4 + j}_{t}"])
        for k in ("Cq", "Cf", "Ci", "Cg"):
            slab_release(sl[k])

    def group_D(b, l, sl):
        S.barrier()
        ar.off = 0
        PADW = 16 + T
        sgd = ar.alloc([128, 512], F32)
        for c in range(2):
            S.barrier()
            ar.off = 2048
            u = ar.alloc([128, PADW], F32)
            A1 = ar.alloc([128, PADW], F32)
            A2 = ar.alloc([128, PADW], F32)
            dfb = ar.alloc([128, T], BF16)
            tmp = ar.alloc([128, 16], F32)
            for a in (u, A1, A2):
                MEMSET("pool", a[:, 0:16], 0.0, ["dpad"])
            for t in range(NT):
                bk = nxt() % 2
                proj_fm(sl["Dv"], c * 128, t, bk)
                ACT(u[:, 16 + t * 512:16 + (t + 1) * 512], bank(bk), AF.Copy, [bres(bk)], ["u"])
            lo, hi = slice(0, 64), slice(64, 128)
            TT("pool", A1[:, 16:], u[:, 16:], u[:, 15:PADW - 1], ALU.add, ["u", "dpad"], ["A1"])
            if c == 0:
                TT("pool", A2[hi, 16:], A1[hi, 16:], A1[hi, 14:PADW - 2], ALU.add, ["A1", "dpad"], ["A2"])
                fins = ((lo, A1), (hi, A2))
            else:
                TT("pool", A2[:, 16:], A1[:, 16:], A1[:, 14:PADW - 2], ALU.add, ["A1", "dpad"], ["A2"])
                TT("pool", A1[:, 16:], A2[:, 16:], A2[:, 12:PADW - 4], ALU.add, ["A2", "dpad"], ["A1"])
                TT("pool", A2[hi, 16:], A1[hi, 16:], A1[hi, 8:PADW - 8], ALU.add, ["A1", "dpad"], ["A2"])
                fins = ((lo, A1), (hi, A2))
            for (hs, fin) in fins:
                STT("dve", dfb[hs, :], fin[hs, 16:], cf[hs, C_INVW + c:C_INVW + c + 1], u[hs, 16:], ALU.mult, ALU.subtract,
                    ["A1", "A2", "u", "cst"], ["dfb"])
                TT("dve", tmp[hs, :], fin[hs, 16:32], cf[hs, C_RC + c * 16:C_RC + (c + 1) * 16], ALU.mult,
                   ["A1", "A2", "cst"], ["dtmp"])
                TT("dve", dfb[hs, 0:16], tmp[hs, :], u[hs, 16:32], ALU.subtract, ["dtmp", "u", "dfb"], ["dfb"])
            for t in range(NT):
                bk = 2 + nxt() % 2
                MM(bank(bk), pwb[:, c, :], dfb[:, t * 512:(t + 1) * 512], True, True, ["dfb", "pwb"], [bres(bk)])
                proj_fm(sl["Dg"], c * 128, t, 6)
                ACT(sgd[:], bank(6), AF.Silu, [bres(6)], ["sgd"])
                STT("dve", mixT[:, 6 + c, t * 512:(t + 1) * 512], bank(bk), psc[:, c:c + 1], sgd[:], ALU.mult, ALU.mult,
                    [bres(bk), "sgd", "psc"], [f"mx{6 + c}_{t}"])
        slab_release(sl["Dv"])
        slab_release(sl["Dg"])

    def group_E(b, l, sl):
        nonlocal ar_rec
        S.barrier()
        ar.off = 0
        qm = ar.alloc([128, 2, T], BF16)
        hb = (ar.alloc([128, 2, 512], BF16), ar.alloc([128, 2, 512], F32))
        mt = ar.alloc([128, 2, D], F32)
        mnb = ar.alloc([128, D], BF16)
        mnT = ar.alloc([128, 8, 256], BF16)
        mss = ar.alloc([128, 4], F32)
        ptb = ar.alloc([128, 3, 512], BF16)
        ar_rec = ar.alloc([128, 512], F32)
        sge = ar.alloc([128, 512], F32)
        for mb in range(2):
            S.dma("sp", f"memld{mb}", mt[:, mb, :], mem[b, mb * 128:(mb + 1) * 128, :], writes=[f"mt{mb}"])
            sc, rc_ = mss[:, mb:mb + 1], mss[:, 2 + mb:3 + mb]
            ACT(sqj[:], mt[:, mb, :], AF.Square, [f"mt{mb}"], ["sqj", "mss"], accum_out=sc)
            ACT(rc_, sc, AF.Ln, ["mss"], ["mrs"], bias=EPS, scale=1.0 / D)
            ACT(rc_, rc_, AF.Exp, ["mrs"], ["mrs"], scale=-0.5)
            STT("dve", mnb[:], mt[:, mb, :], rc_, mg_b[:], ALU.mult, ALU.mult, [f"mt{mb}", "mrs", "mg_b"], ["mnb"])
            for c in range(8):
                S.pe(lambda e, c=c: e.transpose(out=PTt[:, c * 128:(c + 1) * 128], in_=mnb[:, c * 128:(c + 1) * 128],
                                                identity=cbm(C_IDENT)), ["mnb", "cst"], ["b7"])
            ACT(mnT[:, :, mb * 128:(mb + 1) * 128], PTt[:, :].rearrange("p (c n) -> p c n", c=8), AF.Copy, ["b7"], ["mnT"])
        for c in range(2):
            bk = nxt() % 2
            for kc in range(8):
                MM(bank(bk)[:, 0:256], sl["mK"].ap[:, kc, c * 128:(c + 1) * 128], mnT[:, kc, :], kc == 0, kc == 7,
                   [sl["mK"].res, "mnT"], [bres(bk)])
            headnorm(bk, 256, pv[:, 3:4], kmT[:, c, :], "kmT", 2 + bk, hb)
        slab_release(sl["mK"])
        for mb in range(2):
            bk = 4 + mb
            for kc in range(8):
                MM(bank(bk)[:, 0:256], mnT[:, kc, mb * 128:(mb + 1) * 128], sl["mV"].ap[:, kc, 0:256], kc == 0, kc == 7,
                   [sl["mV"].res, "mnT"], [bres(bk)])
            COPY("dve", Vm[:, mb, :], bank(bk)[:, 0:256], [bres(bk)], ["Vm"])
        slab_release(sl["mV"])
        units = []
        for c in range(2):
            for t in range(NT):
                units.append((lambda bk, c=c, t=t: proj_fm(sl["Eq"], c * 128, t, bk), 512, pv[:, 2:3],
                              qm[:, c, t * 512:(t + 1) * 512], f"Eq{c}_{t}"))
        hn_units(units, hb)
        slab_release(sl["Eq"])
        scale = 64 ** -0.5
        iters = [(c, t, hh, mb) for c in range(2) for t in range(NT) for hh in range(2) for mb in range(2)]

        def emit_SE(i):
            c, t, hh, mb = iters[i]
            hs = slice(hh * 64, hh * 64 + 64)
            MM(bank(i % 4), kmT[hs, c, mb * 128:(mb + 1) * 128], qm[hs, c, t * 512:(t + 1) * 512], True, True,
               ["kmT", f"Eq{c}_{t}"], [bres(i % 4)])

        emit_SE(0)
        emit_SE(1)
        for i, (c, t, hh, mb) in enumerate(iters):
            if i + 2 < len(iters):
                emit_SE(i + 2)
            h = 2 * c + hh
            hs = slice(hh * 64, hh * 64 + 64)
            k3 = i % 3
            ACT(ptb[:, k3, :], bank(i % 4), AF.Exp, [bres(i % 4)], [f"pt{k3}"], scale=scale)
            MM(bank(4)[hs, :], Vm[:, mb, h * 64:(h + 1) * 64], ptb[:, k3, :], mb == 0, mb == 1, ["Vm", f"pt{k3}"], [bres(4)])
            MM(bank(5)[hs, :], cbm(C_ONES, 64), ptb[:, k3, :], mb == 0, mb == 1, [f"pt{k3}", "cst"], [bres(5)])
            if hh == 1 and mb == 1:
                ACT(ar_rec[:], bank(5), AF.Ln, [bres(5)], ["rec"])
                ACT(ar_rec[:], ar_rec[:], AF.Exp, ["rec"], ["rec"], scale=-1.0)
                TT("dve", ar_rec[:], bank(4), ar_rec[:], ALU.mult, [bres(4), "rec"], ["rec"])
                proj_fm(sl["Eg"], c * 128, t, 6)
                ACT(sge[:], bank(6), AF.Silu, [bres(6)], ["sge"])
                TT("dve", mixT[:, 8 + c, t * 512:(t + 1) * 512], ar_rec[:], sge[:], ALU.mult, ["rec", "sge"], [f"mx{8 + c}_{t}"])
        slab_release(sl["Eg"])

    def wout_phase(b, l):
        S.barrier()
        src = x[b] if l == 0 else xs
        if l < NLAYER - 1:
            dst = xs
        else:
            dst = y[b]
        for blk in range(NB):
            k = nxt("xst") % 2
            xr = f"xst{k}"
            rd = [] if l == 0 else [f"xs{blk}"]
            S.dma("sp", xr, xst[:, k, :], src[blk * 128:(blk + 1) * 128, :], reads=rd, writes=[xr])
            for dh in range(2):
                bk = nxt() % 4
                for kc in range(10):
                    MM(bank(bk), mixT[:, kc, blk * 128:(blk + 1) * 128], Wo[:, kc, dh * 512:(dh + 1) * 512], kc == 0, kc == 9,
                       [f"mx{kc}_{blk // 4}", "Wo"], [bres(bk)])
                TT("dve", xst[:, k, dh * 512:(dh + 1) * 512], xst[:, k, dh * 512:(dh + 1) * 512], bank(bk), ALU.add,
                   [xr, bres(bk)], [xr])
            if l < NLAYER - 1:
                S.dma("sp", "xout", dst[blk * 128:(blk + 1) * 128, :], xst[:, k, :], reads=[xr], writes=[f"xs{blk}"])
            else:
                S.dma("sp", "xout", dst[blk * 128:(blk + 1) * 128, :], xst[:, k, :], reads=[xr], writes=["yout"])

    ar_rec = None
    if dbg:
        MEMSET("dve", mixT[:], 0.0, [f"mx{c}_{t}" for c in range(10) for t in range(NT)])
    for b in range(NSEQ):
        for l in range(NLAYER):
            load_params(l)
            sl = {}
            for (k, c0, n) in (("Aq", 0, 256), ("Ak", 256, 256), ("Av", 512, 256), ("Aff", 1024, 4), ("Ag", 768, 256)):
                sl[k] = slab_load(win_cols(l, c0, n), n)
            rms_phase(b, l)
            def load_B(l=l, sl=sl):
                if "Bq" in sl:
                    return
                for (k, c0, n) in (("Bq", 1028, 256), ("Bk", 1284, 256), ("Bv", 1540, 256), ("Bg", 1796, 256)):
                    sl[k] = slab_load(win_cols(l, c0, n), n)

            def load_C(l=l, sl=sl):
                if "Cq" in sl:
                    return
                for (k, c0, n) in (("Cq", 2052, 256), ("Cf", 2308, 256), ("Ci", 2564, 256), ("Cg", 2820, 256)):
                    sl[k] = slab_load(win_cols(l, c0, n), n)
                for kc in range(10):
                    S.dma("pool", "Wo", Wo[:, kc, :], w_out[l, kc * 128:(kc + 1) * 128, :], writes=["Wo"])

            if "A" in groups:
                group_A(b, l, sl, prefetch=load_B)
            else:
                for k in ("Aq", "Ak", "Av", "Aff", "Ag"):
                    slab_release(sl[k])
            load_B()
            if "B" in groups:
                group_B(b, l, sl, prefetch=load_C)
            else:
                for k in ("Bq", "Bk", "Bv", "Bg"):
                    slab_release(sl[k])
            load_C()
            if "C" in groups:
                group_C(b, l, sl)
            else:
                for k in ("Cq", "Cf", "Ci", "Cg"):
                    slab_release(sl[k])
            for (k, c0, n) in (("Dv", 3076, 256), ("Dg", 3332, 256)):
                sl[k] = slab_load(win_cols(l, c0, n), n)
            sl["mK"] = slab_load(wkv_cols(l, 0, 256), 256)
            sl["mV"] = slab_load(wkv_cols(l, 256, 256), 256)
            if "D" in groups:
                group_D(b, l, sl)
            else:
                slab_release(sl["Dv"])
                slab_release(sl["Dg"])
            for (k, c0, n) in (("Eq", 3588, 256), ("Eg", 3844, 256)):
                sl[k] = slab_load(win_cols(l, c0, n), n)
            if "E" in groups:
                group_E(b, l, sl)
            else:
                for k in ("mK", "mV", "Eq", "Eg"):
                    slab_release(sl[k])
            wout_phase(b, l)
    fin_reads = ["yout"]
    if dbg:
        S.barrier()
        S.dma("sp", "dbg", dbg_out, mixT[:], reads=[f"mx{c}_{t}" for c in range(10) for t in range(NT)], writes=["dbgo"])
        fin_reads.append("dbgo")
    S.add("sp", None, reads=fin_reads)
    S.emit(st)
    st.close()
    nc._arena_peak = ar.peak
    return nc


_CACHE = {}


def kernel(x, mem, norm_g, w_in, fox_f_bias, fox_q_norm, fox_k_norm, hgrn_lb_logits, hgrn_out_norm, pool_w,
           pool_scale, mem_norm_g, mem_w_kv, mem_q_norm, mem_k_norm, w_out):
    ncores = 8
    f = lambda a: np.ascontiguousarray(np.asarray(a, dtype=np.float32))
    x = f(x)
    mem = f(mem)
    B = x.shape[0]
    per = B // ncores
    if "nc" not in _CACHE:
        _CACHE["nc"] = build_nc(per, 2)
    nc = _CACHE["nc"]
    shared = dict(norm_g=f(norm_g), w_in=f(w_in), fox_f_bias=f(fox_f_bias), fox_q_norm=f(fox_q_norm),
                  fox_k_norm=f(fox_k_norm), hgrn_lb_logits=f(hgrn_lb_logits), hgrn_out_norm=f(hgrn_out_norm),
                  pool_w=f(pool_w), pool_scale=f(pool_scale), mem_norm_g=f(mem_norm_g), mem_w_kv=f(mem_w_kv),
                  mem_q_norm=f(mem_q_norm), mem_k_norm=f(mem_k_norm), w_out=f(w_out), cpack=make_cpack())
    in_maps = []
    for i in range(ncores):
        d = dict(shared)
        d["x"] = np.ascontiguousarray(x[i * per:(i + 1) * per])
        d["mem"] = np.ascontiguousarray(mem[i * per:(i + 1) * per])
        in_maps.append(d)
    res = run_bass_kernel_spmd(nc, in_maps, core_ids=list(range(ncores)))
    return np.concatenate([np.asarray(r["y"], dtype=np.float32) for r in res.results], axis=0)
```

```python
import numpy as np
from contextlib import ExitStack
import concourse.bass as bass
import concourse.mybir as mybir
from concourse.bass_utils import run_bass_kernel_spmd

F32 = mybir.dt.float32
BF16 = mybir.dt.bfloat16
AF = mybir.ActivationFunctionType
ALU = mybir.AluOpType

ENGS = ("pe", "act", "dve", "pool", "sp")


class Op:
    __slots__ = ("eng", "fn", "reads", "writes", "chan", "deps", "mile", "cnt", "waits", "idx")

    def __init__(self, eng, fn, reads, writes, chan):
        self.eng, self.fn, self.reads, self.writes, self.chan = eng, fn, tuple(reads), tuple(writes), chan
        self.deps = []
        self.mile = False
        self.cnt = 0
        self.waits = []


class Sched:
    def __init__(self, nc):
        self.nc = nc
        self.ops = []

    def add(self, eng, fn, reads=(), writes=(), chan=None):
        op = Op(eng, fn, reads, writes, chan)
        op.idx = len(self.ops)
        self.ops.append(op)
        return op

    def pe(self, fn, reads=(), writes=()):
        return self.add("pe", fn, reads, writes)

    def act(self, fn, reads=(), writes=()):
        return self.add("act", fn, reads, writes)

    def dve(self, fn, reads=(), writes=()):
        return self.add("dve", fn, reads, writes)

    def pool(self, fn, reads=(), writes=()):
        return self.add("pool", fn, reads, writes)

    def dma(self, q, chan, out, in_, reads=(), writes=(), **kw):
        return self.add(q, lambda e: e.dma_start(out=out, in_=in_, **kw), reads, writes, chan=chan)

    def barrier(self):
        return self.add("bar", None)

    def resolve(self):
        ops = self.ops
        last_w = {}
        readers = {}
        last_comp = {}
        pend = {e: [] for e in ENGS}
        for op in ops:
            if op.eng == "bar":
                L = list(last_comp.values())
                for e in ENGS:
                    pend[e] = list(L)
                continue
            deps = set(pend[op.eng])
            pend[op.eng] = []
            for r in op.reads:
                w = last_w.get(r)
                if w is not None:
                    deps.add(w)
            for r in op.writes:
                w = last_w.get(r)
                if w is not None:
                    deps.add(w)
                for rd in readers.get(r, ()):
                    deps.add(rd)
            deps.discard(op.idx)
            op.deps = sorted(deps)
            for r in op.writes:
                last_w[r] = op.idx
                readers[r] = []
            for r in op.reads:
                if r not in op.writes:
                    readers.setdefault(r, []).append(op.idx)
            if op.chan is None:
                last_comp[op.eng] = op.idx
        chan_cnt = {}
        need = []
        for op in ops:
            lst = []
            if op.eng != "bar":
                for d in op.deps:
                    dop = ops[d]
                    if dop.chan is not None:
                        lst.append(("chan", dop.chan, chan_cnt[dop.chan]))
                    else:
                        if dop.eng == op.eng and op.chan is None:
                            if op.eng == "pe":
                                continue
                        dop.mile = True
                        lst.append(("eng", dop.eng, d))
            need.append(lst)
            if op.chan is not None:
                chan_cnt[op.chan] = chan_cnt.get(op.chan, 0) + 1
        ecnt = {e: 0 for e in ENGS}
        for op in ops:
            if op.eng != "bar" and op.chan is None and op.mile:
                ecnt[op.eng] += 1
                op.cnt = ecnt[op.eng]
        waited = {e: {} for e in ENGS}
        for op, lst in zip(ops, need):
            if op.eng == "bar":
                continue
            w = {}
            for kind, key, v in lst:
                if kind == "chan":
                    k = ("chan", key)
                    val = 16 * v
                else:
                    k = ("eng", key)
                    val = ops[v].cnt
                if val > w.get(k, 0):
                    w[k] = val
            out = []
            for k, val in w.items():
                if waited[op.eng].get(k, 0) >= val:
                    continue
                waited[op.eng][k] = val
                out.append((k, val))
            op.waits = out
        self.chans = sorted({op.chan for op in ops if op.chan is not None})

    def emit(self, stack):
        nc = self.nc
        self.resolve()
        sems = {}
        for e in ENGS:
            sems[("eng", e)] = stack.enter_context(nc.semaphore("s_" + e))
        for c in self.chans:
            sems[("chan", c)] = stack.enter_context(nc.semaphore("c_" + str(c)))
        per = {e: [op for op in self.ops if op.eng == e] for e in ENGS}
        block = stack.enter_context(nc.Block())

        def run(eng_handle, lst):
            for op in lst:
                for k, val in op.waits:
                    eng_handle.wait_ge(sems[k], val)
                if op.fn is None:
                    continue
                inst = op.fn(eng_handle)
                if op.chan is not None:
                    inst.then_inc(sems[("chan", op.chan)], 16)
                elif op.mile:
                    inst.then_inc(sems[("eng", op.eng)], 1)

        @block.tensor
        def _(e):
            run(e, per["pe"])

        @block.scalar
        def _(e):
            run(e, per["act"])

        @block.vector
        def _(e):
            run(e, per["dve"])

        @block.gpsimd
        def _(e):
            run(e, per["pool"])

        @block.sync
        def _(e):
            run(e, per["sp"])


import os
CSTOP = int(os.environ.get('CSTOP', '99'))
T = 2048
D = 1024
DIN = 4100
NB = 16
NT = 4
EPS = 1e-6
POOL_WINDOWS = (2, 4, 8, 16)

C_IDENT, C_NEGID, C_TRI, C_ONES, C_NEGM, C_MLT, C_NUINCL, C_TBD, C_TBDU, C_BONES, C_NONES = range(11)
C_MASK64 = 11 * 128
C_RC = C_MASK64 + 64
C_INVW = C_RC + 32
C_ZERO = C_INVW + 2
NCF = C_ZERO + 64


def make_cpack():
    p = np.arange(128)[:, None]
    c = np.arange(128)[None, :]
    cp = np.zeros((128, NCF), np.float32)

    def put(i, m):
        cp[:, i * 128:(i + 1) * 128] = m.astype(np.float32)

    put(C_IDENT, p == c)
    put(C_NEGID, -(p == c).astype(np.float32))
    put(C_TRI, p <= c)
    put(C_ONES, np.ones((128, 128)))
    put(C_NEGM, np.where(p > c, -1e30, 0.0))
    put(C_MLT, p < c)
    put(C_NUINCL, -(p >= c).astype(np.float32))
    same = (p // 64) == (c // 64)
    put(C_TBD, same & (p <= c))
    put(C_TBDU, same & (p > c))
    put(C_BONES, same)
    put(C_NONES, -np.ones((128, 128)))
    cp[:, C_MASK64:C_MASK64 + 64] = ((p % 64) <= np.arange(64)[None, :]).astype(np.float32)
    for ch in range(2):
        for half in range(2):
            win = POOL_WINDOWS[2 * ch + half]
            pos = np.arange(1, 17, dtype=np.float32)
            cp[half * 64:(half + 1) * 64, C_RC + ch * 16:C_RC + (ch + 1) * 16] = 1.0 / np.minimum(pos, win)[None, :]
            cp[half * 64:(half + 1) * 64, C_INVW + ch] = 1.0 / win
    return cp


class Arena:
    def __init__(self, ap, nbytes):
        self.ap, self.cap, self.off = ap, nbytes, 0
        self.peak = 0

    def alloc(self, shape, dtype):
        isz = 4 if dtype == F32 else 2
        n = int(np.prod(shape[1:]))
        nb = n * isz
        off = (self.off + 63) // 64 * 64
        assert off + nb <= self.cap, ("arena overflow", off + nb, self.cap)
        a = self.ap[:, off // 2:(off + nb) // 2]
        if dtype == F32:
            a = a.bitcast(F32)
        if len(shape) == 3:
            a = a.rearrange("p (c n) -> p c n", c=shape[1])
        elif len(shape) == 4:
            a = a.rearrange("p (a b n) -> p a b n", a=shape[1], b=shape[2])
        self.off = off + nb
        self.peak = max(self.peak, self.off)
        return a


def build_nc(NSEQ=4, NLAYER=2, dbg=False, groups="ABCDE"):
    nc = bass.Bass("TRN2", target_bir_lowering=False)

    def din(name, shape):
        return nc.dram_tensor(name, list(shape), F32, kind="ExternalInput").ap()

    x = din("x", [NSEQ, T, D])
    mem = din("mem", [NSEQ, 256, D])
    norm_g = din("norm_g", [2, D])
    w_in = din("w_in", [2, D, DIN])
    fox_f_bias = din("fox_f_bias", [2, 4])
    fox_q_norm = din("fox_q_norm", [2, 64])
    fox_k_norm = din("fox_k_norm", [2, 64])
    hgrn_lb_logits = din("hgrn_lb_logits", [2, 256])
    hgrn_out_norm = din("hgrn_out_norm", [2, 256])
    pool_w = din("pool_w", [2, 4, 64, 64])
    pool_scale = din("pool_scale", [2, 256])
    mem_norm_g = din("mem_norm_g", [2, D])
    mem_w_kv = din("mem_w_kv", [2, D, 512])
    mem_q_norm = din("mem_q_norm", [2, 64])
    mem_k_norm = din("mem_k_norm", [2, 64])
    w_out = din("w_out", [2, 1280, D])
    cpack = din("cpack", [128, NCF])
    y = nc.dram_tensor("y", [NSEQ, T, D], F32, kind="ExternalOutput").ap()
    xs = nc.dram_tensor("xs_scratch", [T, D], F32).ap()
    if dbg:
        dbg_out = nc.dram_tensor("dbg", [128, 10, T], BF16, kind="ExternalOutput").ap()

    st = ExitStack()
    S = Sched(nc)

    def sb(name, shape, dtype):
        return st.enter_context(nc.sbuf_tensor(name, list(shape), dtype))

    hT = sb("hT", [128, 8, T], BF16)
    mixT = sb("mixT", [128, 10, T], BF16)
    Wo = sb("Wo", [128, 10, D], BF16)
    ws = sb("ws", [128, 5, 8, 256], BF16)
    cf = sb("cf", [128, NCF], F32)
    cb = sb("cb", [128, NCF], BF16)
    g_b = sb("g_b", [128, D], F32)
    mg_b = sb("mg_b", [128, D], F32)
    xst = sb("xst", [128, 2, D], F32)
    hbuf = sb("hbuf", [128, 2, D], BF16)
    sqj = sb("sqj", [128, D], BF16)
    ssb = sb("ssb", [128, 32], F32)
    pv = sb("pv", [128, 8], F32)
    hn = sb("hn", [128, 2], F32)
    psc = sb("psc", [128, 2], F32)
    lbl = sb("lbl", [128, 2, 2], F32)
    lblb = sb("lblb", [128, 2, 256], F32)
    lbp = sb("lbp", [128, 4], F32)
    lb_b = sb("lb_b", [128, 256], F32)
    oml_b = sb("oml_b", [128, 256], F32)
    fb_b = sb("fb_b", [128, 4], F32)
    pwf = sb("pwf", [128, 2, 128], F32)
    pwb = sb("pwb", [128, 2, 128], BF16)
    kmT = sb("kmT", [128, 2, 256], BF16)
    Vm = sb("Vm", [128, 2, 256], BF16)
    ARENA_BYTES = 54 * 1024
    arena_t = sb("arena", [128, ARENA_BYTES // 2], BF16)
    ar = Arena(arena_t, ARENA_BYTES)

    PD = [st.enter_context(nc.psum_tensor(f"pd{i}", [128, 1024], F32)) for i in range(3)]
    PX = st.enter_context(nc.psum_tensor("px", [128, 512], F32))
    PTt = st.enter_context(nc.psum_tensor("ptt", [128, 1024], BF16))

    def bank(i):
        if i < 6:
            return PD[i // 2][:, (i % 2) * 512:(i % 2 + 1) * 512]
        return PX[:, :]

    def bres(i):
        return f"b{i}"

    def MM(out, lhsT, rhs, start, stop, r, w):
        S.pe(lambda e: e.matmul(out, lhsT=lhsT, rhs=rhs, start=start, stop=stop, skip_group_check=True), r, w)

    def ACT(out, in_, func, r, w, **kw):
        S.act(lambda e: e.activation(out=out, in_=in_, func=func, **kw), r, w)

    def TT(eng, out, in0, in1, op, r, w):
        S.add(eng, lambda e: e.tensor_tensor(out=out, in0=in0, in1=in1, op=op), r, w)

    def TS(eng, out, in0, s1, s2, op0, op1, r, w):
        if s2 is None:
            S.add(eng, lambda e: e.tensor_scalar(out=out, in0=in0, scalar1=s1, scalar2=None, op0=op0), r, w)
        else:
            S.add(eng, lambda e: e.tensor_scalar(out=out, in0=in0, scalar1=s1, scalar2=s2, op0=op0, op1=op1), r, w)

    def STT(eng, out, in0, scalar, in1, op0, op1, r, w):
        S.add(eng, lambda e: e.scalar_tensor_tensor(out=out, in0=in0, scalar=scalar, in1=in1, op0=op0, op1=op1), r, w)

    def COPY(eng, out, in_, r, w):
        S.add(eng, lambda e: e.tensor_copy(out=out, in_=in_), r, w)

    def RECIP(out, in_, r, w):
        S.dve(lambda e: e.reciprocal(out=out, in_=in_), r, w)

    def MEMSET(eng, ap, val, w):
        S.add(eng, lambda e: e.memset(ap, val), (), w)

    def cbm(i, n=128):
        return cb[:, i * 128:i * 128 + n]

    def cfm(i, n=128):
        return cf[:, i * 128:i * 128 + n]

    S.dma("sp", "cst", cf[:], cpack, writes=["cst"])
    COPY("dve", cb[:], cf[:], ["cst"], ["cst"])

    slab_free = [True] * 5

    class Slab:
        pass

    def slab_load(src3d, ncols):
        slot = slab_free.index(True)
        slab_free[slot] = False
        s = Slab()
        s.slot, s.res, s.ap = slot, f"ws{slot}", ws[:, slot]
        S.dma("pool", f"ws{slot}", ws[:, slot, :, 0:ncols], src3d, writes=[s.res])
        return s

    def slab_release(s):
        slab_free[s.slot] = True

    def win_cols(l, c0, n):
        return w_in[l].rearrange("(c p) n -> p c n", p=128)[:, :, c0:c0 + n]

    def wkv_cols(l, c0, n):
        return mem_w_kv[l].rearrange("(c p) n -> p c n", p=128)[:, :, c0:c0 + n]

    def proj_fm(s, col0, t, bk):
        for kc in range(8):
            MM(bank(bk), s.ap[:, kc, col0:col0 + 128], hT[:, kc, t * 512:(t + 1) * 512], kc == 0, kc == 7,
               [s.res, f"hT{t}"], [bres(bk)])

    def proj_tm(s, col0, ncols, blk, out_ap, ores):
        for kc in range(8):
            MM(out_ap, hT[:, kc, blk * 128:(blk + 1) * 128], s.ap[:, kc, col0:col0 + ncols], kc == 0, kc == 7,
               [s.res, f"hT{blk // 4}"], [ores])

    rot = {}

    def nxt(name="g"):
        rot[name] = rot.get(name, 0) + 1
        return rot[name]

    def headnorm(src_bk, N, gain_col, out_ap, out_res, stat_bk, hb):
        k = nxt("hn") % 2
        sq, lnv = hb
        src = bank(src_bk)[:, 0:N]
        ACT(sq[:, k, 0:N], src, AF.Square, [bres(src_bk)], [f"hnsq{k}"])
        MM(bank(stat_bk)[:, 0:N], cbm(C_BONES), sq[:, k, 0:N], True, True, [f"hnsq{k}", "cst"], [bres(stat_bk)])
        ACT(lnv[:, k, 0:N], bank(stat_bk)[:, 0:N], AF.Ln, [bres(stat_bk)], [f"hnln{k}"], bias=EPS, scale=1.0 / 64)
        ACT(lnv[:, k, 0:N], lnv[:, k, 0:N], AF.Exp, [f"hnln{k}"], [f"hnln{k}"], scale=-0.5)
        STT("dve", out_ap, src, gain_col, lnv[:, k, 0:N], ALU.mult, ALU.mult,
            [bres(src_bk), f"hnln{k}", "pv"], [out_res])

    def hn_units(units, hb):
        n = len(units)
        if n:
            units[0][0](0)
        for u in range(n):
            if u + 1 < n:
                units[u + 1][0]((u + 1) % 4)
            _, N, gcol, oap, ores = units[u]
            headnorm(u % 4, N, gcol, oap, ores, 4 + u % 2, hb)

    def load_params(l):
        S.dma("sp", "g_b", g_b[:], norm_g[l].partition_broadcast(128), writes=["g_b"])
        S.dma("sp", "mg_b", mg_b[:], mem_norm_g[l].partition_broadcast(128), writes=["mg_b"])
        for i, src in enumerate((fox_q_norm, fox_k_norm, mem_q_norm, mem_k_norm)):
            col = src[l].rearrange("(p o) -> p o", o=1)
            S.dma("sp", "pv", pv[0:64, i:i + 1], col, writes=["pv"])
            S.dma("sp", "pv", pv[64:128, i:i + 1], col, writes=["pv"])
        S.dma("sp", "hn", hn[:], hgrn_out_norm[l].rearrange("(c p) -> p c", p=128), writes=["hn"],
              allow_slow_non_contiguous=True)
        S.dma("sp", "psc", psc[:], pool_scale[l].rearrange("(c p) -> p c", p=128), writes=["psc"],
              allow_slow_non_contiguous=True)
        S.dma("sp", "fb_b", fb_b[:], fox_f_bias[l].partition_broadcast(128), writes=["fb_b"])
        MEMSET("dve", pwf[:], 0.0, ["pwf"])
        for g in range(4):
            hs = slice((g % 2) * 64, (g % 2) * 64 + 64)
            S.dma("sp", "pwf", pwf[hs, g // 2, (g % 2) * 64:(g % 2) * 64 + 64], pool_w[l, g], writes=["pwf"])
        COPY("dve", pwb[:], pwf[:], ["pwf"], ["pwb"])
        if l == 0:
            MEMSET("dve", lbp[:, 0:2], 0.0, ["lb"])
            MEMSET("dve", lbp[:, 2:4], 1.0, ["lb"])
            MEMSET("dve", lb_b[:], 0.0, ["lb"])
            MEMSET("dve", oml_b[:], 1.0, ["lb"])
        else:
            S.dma("sp", "lbl", lbl[:], hgrn_lb_logits.rearrange("l (c p) -> p l c", p=128), writes=["lbl"],
                  allow_slow_non_contiguous=True)
            S.dma("sp", "lblb", lblb[:], hgrn_lb_logits.partition_broadcast(128), writes=["lblb"])
            for (dst_lb, dst_oml, a0, a1) in ((lbp[:, 0:2], lbp[:, 2:4], lbl[:, 0, :], lbl[:, 1, :]),
                                             (lb_b[:], oml_b[:], lblb[:, 0, :], lblb[:, 1, :])):
                TT("dve", dst_oml, a1, a0, ALU.subtract, ["lbl", "lblb", "lb"], ["lb"])
                ACT(dst_oml, dst_oml, AF.Exp, ["lb"], ["lb"], scale=-1.0)
                TS("dve", dst_oml, dst_oml, 1.0, None, ALU.add, None, ["lb"], ["lb"])
                RECIP(dst_lb, dst_oml, ["lb"], ["lb"])
                TS("dve", dst_lb, dst_lb, 1.0 - 1e-6, None, ALU.min, None, ["lb"], ["lb"])
                TS("dve", dst_oml, dst_lb, -1.0, 1.0, ALU.mult, ALU.add, ["lb"], ["lb"])

    def rms_phase(b, l):
        src = x[b] if l == 0 else xs
        st1 = {}

        def stage1(blk):
            k = nxt("xst") % 2
            xr = f"xst{k}"
            rd = [] if l == 0 else [f"xs{blk}"]
            S.dma("sp", xr, xst[:, k, :], src[blk * 128:(blk + 1) * 128, :], reads=rd, writes=[xr])
            sc = ssb[:, blk:blk + 1]
            rc_ = ssb[:, 16 + blk:17 + blk]
            ACT(sqj[:], xst[:, k, :], AF.Square, [xr], ["sqj", f"ss{blk}"], accum_out=sc)
            ACT(rc_, sc, AF.Ln, [f"ss{blk}"], [f"rs{blk}"], bias=EPS, scale=1.0 / D)
            ACT(rc_, rc_, AF.Exp, [f"rs{blk}"], [f"rs{blk}"], scale=-0.5)
            hk = blk % 2
            STT("dve", hbuf[:, hk, :], xst[:, k, :], rc_, g_b[:], ALU.mult, ALU.mult,
                [xr, f"rs{blk}", "g_b"], [f"hb{hk}"])

        def stage2(blk):
            hk = blk % 2
            for c in range(8):
                S.pe(lambda e, c=c, hk=hk: e.transpose(out=PTt[:, c * 128:(c + 1) * 128],
                                                       in_=hbuf[:, hk, c * 128:(c + 1) * 128], identity=cbm(C_IDENT)),
                     [f"hb{hk}", "cst"], ["b7"])
            ACT(hT[:, :, blk * 128:(blk + 1) * 128], PTt[:, :].rearrange("p (c n) -> p c n", c=8), AF.Copy,
                ["b7"], [f"hT{blk // 4}"])

        stage1(0)
        for blk in range(NB):
            if blk + 1 < NB:
                stage1(blk + 1)
            stage2(blk)

    def attn_finish(h, c, qt, sg, mixc, with_den):
        pb = (h % 2) * 64
        hs = slice(pb, pb + 64)
        num = bank(4)[hs, :]
        mres = f"mx{mixc}_{qt}"
        if with_den:
            rec = ar_rec
            ACT(rec[hs, :], bank(5)[hs, :], AF.Ln, [bres(5)], ["rec"])
            ACT(rec[hs, :], rec[hs, :], AF.Exp, ["rec"], ["rec"], scale=-1.0)
            TT("dve", rec[hs, :], num, rec[hs, :], ALU.mult, [bres(4), "rec"], ["rec"])
            TT("dve", mixT[hs, mixc, qt * 512:(qt + 1) * 512], rec[hs, :], sg[hs, c, qt * 512:(qt + 1) * 512], ALU.mult,
               ["rec", "sg"], [mres])
        else:
            TT("dve", mixT[hs, mixc, qt * 512:(qt + 1) * 512], num, sg[hs, c, qt * 512:(qt + 1) * 512], ALU.mult,
               [bres(4), "sg"], [mres])

    def gates_full(s, sg):
        for c in range(2):
            for t in range(NT):
                bk = 4 + nxt("gf") % 3
                proj_fm(s, c * 128, t, bk)
                ACT(sg[:, c, t * 512:(t + 1) * 512], bank(bk), AF.Silu, [bres(bk)], ["sg"])

    def group_A(b, l, sl, prefetch=None):
        nonlocal ar_rec
        S.barrier()
        ar.off = 0
        qT = ar.alloc([128, 2, T], BF16)
        kT = ar.alloc([128, 2, T], BF16)
        V = ar.alloc([128, NB, 256], BF16)
        sg = ar.alloc([128, 2, T], BF16)
        csp = ar.alloc([128, NB, 4], F32)
        mark = ar.off
        hb = (ar.alloc([128, 2, 512], BF16), ar.alloc([128, 2, 512], F32))
        tmpf = ar.alloc([128, 64], F32)
        tot = ar.alloc([128, 64], F32)
        inc = ar.alloc([128, 64], F32)
        units = []
        for (s_, dst, gcol, nm) in ((sl["Aq"], qT, 0, "q"), (sl["Ak"], kT, 1, "k")):
            for c in range(2):
                for t in range(NT):
                    units.append((lambda bk, s_=s_, c=c, t=t: proj_fm(s_, c * 128, t, bk), 512, pv[:, gcol:gcol + 1],
                                  dst[:, c, t * 512:(t + 1) * 512], f"A{nm}{c}_{t}"))
        hn_units(units, hb)
        slab_release(sl["Aq"])
        slab_release(sl["Ak"])
        for blk in range(NB):
            bk = 4 + nxt() % 2
            proj_tm(sl["Av"], 0, 256, blk, bank(bk)[:, 0:256], bres(bk))
            COPY("dve", V[:, blk, :], bank(bk)[:, 0:256], [bres(bk)], [f"AV{blk}"])
        slab_release(sl["Av"])
        for blk in range(NB):
            proj_tm(sl["Aff"], 0, 4, blk, bank(6)[:, blk * 4:(blk + 1) * 4], bres(6))
        slab_release(sl["Aff"])
        TT("dve", tmpf[:].rearrange("p (b h) -> p b h", h=4), bank(6)[:, 0:64].rearrange("p (b h) -> p b h", h=4),
           fb_b[:].unsqueeze(1).to_broadcast([128, NB, 4]), ALU.add, [bres(6), "fb_b"], ["tmpf"])
        gates_full(sl["Ag"], sg)
        slab_release(sl["Ag"])
        ACT(tmpf[:], tmpf[:], AF.Exp, ["tmpf"], ["tmpf"], scale=-1.0)
        ACT(tmpf[:], tmpf[:], AF.Ln, ["tmpf"], ["tmpf"], bias=1.0)
        MM(bank(0)[:, 0:64], cfm(C_TRI), tmpf[:], True, True, ["tmpf", "cst"], [bres(0)])
        MM(bank(1)[:, 0:64], cfm(C_ONES), tmpf[:], True, True, ["tmpf", "cst"], [bres(1)])
        COPY("dve", tot[:], bank(1)[:, 0:64], [bres(1)], ["tot"])
        for h in range(4):
            S.dve(lambda e, h=h: e.tensor_tensor_scan(
                out=inc[:].rearrange("p (b h) -> p b h", h=4)[:, :, h], data0=cfm(C_ONES, 16),
                data1=tot[:].rearrange("p (b h) -> p b h", h=4)[:, :, h], initial=0.0, op0=ALU.mult, op1=ALU.add),
                ["tot", "cst"], ["inc"])
        TT("dve", inc[:], inc[:], tot[:], ALU.subtract, ["inc", "tot"], ["inc"])
        TT("dve", csp[:].rearrange("p b h -> p (b h)"), bank(0)[:, 0:64], inc[:], ALU.add, [bres(0), "inc"], ["csp"])
        if prefetch is not None:
            prefetch()
        S.barrier()
        ar.off = mark
        CTn = ar.alloc([128, T], F32)
        argb = ar.alloc([128, 3, 512], F32)
        ptb = ar.alloc([128, 3, 512], BF16)
        ar_rec = ar.alloc([128, 512], F32)
        scale = 64 ** -0.5
        for h in range(4):
            c, pb = h // 2, (h % 2) * 64
            hs = slice(pb, pb + 64)
            for q4 in range(4):
                bk = 4 + q4 % 2
                for i in range(4):
                    blk = q4 * 4 + i
                    MM(bank(bk)[:, i * 128:(i + 1) * 128], csp[:, blk, h:h + 1].to_broadcast([128, 128]), cfm(C_NEGID),
                       True, True, ["csp", "cst"], [bres(bk)])
                ACT(CTn[:, q4 * 512:(q4 + 1) * 512], bank(bk), AF.Copy, [bres(bk)], ["CTn"])
            iters = [(qt, kb) for qt in range(NT) for kb in range(4 * qt + 4)]
            LA = 2

            def emit_S(i):
                qt, kb = iters[i]
                off = max(0, kb - 4 * qt) * 128
                sbk = i % 4
                MM(bank(sbk)[:, off:512], kT[hs, c, kb * 128:(kb + 1) * 128], qT[hs, c, qt * 512 + off:(qt + 1) * 512],
                   True, True, [f"Ak{c}_{kb // 4}", f"Aq{c}_{qt}"], [bres(sbk)])

            for i in range(min(LA, len(iters))):
                emit_S(i)
            for i, (qt, kb) in enumerate(iters):
                if i + LA < len(iters):
                    emit_S(i + LA)
                nk = 4 * qt + 4
                off = max(0, kb - 4 * qt) * 128
                sbk = i % 4
                k3 = i % 3
                STT("dve", argb[:, k3, off:512], bank(sbk)[:, off:512], scale, CTn[:, qt * 512 + off:(qt + 1) * 512],
                    ALU.mult, ALU.add, [bres(sbk), "CTn"], [f"arg{k3}"])
                if kb >= 4 * qt:
                    TT("dve", argb[:, k3, off:off + 128], argb[:, k3, off:off + 128], cfm(C_NEGM), ALU.add,
                       [f"arg{k3}", "cst"], [f"arg{k3}"])
                ACT(ptb[:, k3, off:512], argb[:, k3, off:512], AF.Exp, [f"arg{k3}", "csp"], [f"pt{k3}"],
                    bias=csp[:, kb, h:h + 1])
                MM(bank(4)[hs, off:512], V[:, kb, h * 64:(h + 1) * 64], ptb[:, k3, off:512], kb == 0, kb == nk - 1,
                   [f"AV{kb}", f"pt{k3}"], [bres(4)])
                MM(bank(5)[hs, off:512], cbm(C_ONES, 64), ptb[:, k3, off:512], kb == 0, kb == nk - 1,
                   [f"pt{k3}", "cst"], [bres(5)])
                if kb == nk - 1:
                    attn_finish(h, c, qt, sg, c, True)

    def group_B(b, l, sl, prefetch=None):
        S.barrier()
        ar.off = 0
        qT = ar.alloc([128, 2, T], BF16)
        kT = ar.alloc([128, 2, T], BF16)
        V = ar.alloc([128, NB, 256], BF16)
        sg = ar.alloc([128, 2, T], BF16)
        eb = ar.alloc([128, 2, 512], F32)
        spb = ar.alloc([128, 3, 512], BF16)
        atb = ar.alloc([128, 2, 512], BF16)
        acc = ar.alloc([128, 512], BF16)
        z64 = ar.alloc([128, 64], BF16)
        MEMSET("dve", z64[:], 0.0, ["z64"])
        scale = 64 ** -0.5
        for (s, dst, nm, scl) in ((sl["Bq"], qT, "q", scale), (sl["Bk"], kT, "k", 1.0)):
            for c in range(2):
                for t in range(NT):
                    bk = nxt() % 2
                    proj_fm(s, c * 128, t, bk)
                    ACT(dst[:, c, t * 512:(t + 1) * 512], bank(bk), AF.Copy, [bres(bk)], [f"B{nm}{c}_{t}"], scale=scl)
            slab_release(s)
        for blk in range(NB):
            bk = 4 + nxt() % 2
            proj_tm(sl["Bv"], 0, 256, blk, bank(bk)[:, 0:256], bres(bk))
            COPY("dve", V[:, blk, :], bank(bk)[:, 0:256], [bres(bk)], [f"BV{blk}"])
        slab_release(sl["Bv"])
        gates_full(sl["Bg"], sg)
        slab_release(sl["Bg"])
        if prefetch is not None:
            prefetch()
        for h in range(4):
            c, pb = h // 2, (h % 2) * 64
            hs = slice(pb, pb + 64)
            iters = [(qt, kb) for qt in range(NT) for kb in range(4 * qt + 3, -1, -1)]
            n_it = len(iters)

            def geo(i):
                qt, kb = iters[i]
                off = max(0, kb - 4 * qt) * 128
                return qt, kb, off, kT[hs, c, kb * 128:(kb + 1) * 128], qT[hs, c, qt * 512 + off:(qt + 1) * 512], \
                    [f"Bk{c}_{kb // 4}", f"Bq{c}_{qt}"]

            def emit_Z(i):
                qt, kb, off, ksl, qsl, kq = geo(i)
                MM(bank(i % 3)[:, off:512], ksl, qsl, True, True, kq, [bres(i % 3)])

            def emit_E(i):
                qt, kb, off, ksl, qsl, kq = geo(i)
                za, k2, k3 = i % 3, i % 2, i % 3
                ACT(eb[:, k2, off:512], bank(za)[:, off:512], AF.Exp, [bres(za)], [f"e{k2}"])
                ACT(spb[:, k3, off:512], eb[:, k2, off:512], AF.Ln, [f"e{k2}"], [f"sp{k3}"], bias=1.0)
                if kb >= 4 * qt:
                    TT("pool", spb[:, k3, off:off + 128], spb[:, k3, off:off + 128], cbm(C_MLT), ALU.mult,
                       [f"sp{k3}", "cst"], [f"sp{k3}"])

            emit_Z(0)
            if n_it > 1:
                emit_Z(1)
            emit_E(0)
            for i in range(n_it):
                qt, kb, off, ksl, qsl, kq = geo(i)
                if i + 2 < n_it:
                    emit_Z(i + 2)
                if i + 1 < n_it:
                    emit_E(i + 1)
                if kb == 4 * qt + 3:
                    MEMSET("pool", acc[:], 0.0, ["acc"])
                    MM(bank(4)[hs, :], z64[:], qT[:, c, qt * 512:(qt + 1) * 512], True, False, ["z64", f"Bq{c}_{qt}"], [bres(4)])
                k3, k2 = i % 3, i % 2
                zb = 3 if k2 == 0 else 5
                diag = kb >= 4 * qt
                MM(bank(zb)[:, off:512], cbm(C_NUINCL), spb[:, k3, off:512], True, False, [f"sp{k3}", "cst"], [bres(zb)])
                MM(bank(zb)[:, off:512], cbm(C_NONES), acc[:, off:512], False, False, ["acc", "cst"], [bres(zb)])
                MM(bank(zb)[:, off:512], ksl, qsl, False, True, kq, [bres(zb)])
                ACT(atb[:, k2, off:512], bank(zb)[:, off:512], AF.Exp, [bres(zb)], [f"at{k2}"])
                if diag:
                    TT("pool", atb[:, k2, off:off + 128], atb[:, k2, off:off + 128], cbm(C_MLT), ALU.mult,
                       [f"at{k2}", "cst"], [f"at{k2}"])
                if kb > 0:
                    TT("dve", acc[:, off:512], acc[:, off:512], spb[:, k3, off:512], ALU.add, ["acc", f"sp{k3}"], ["acc"])
                MM(bank(4)[hs, off:512], V[:, kb, h * 64:(h + 1) * 64], atb[:, k2, off:512], False, kb == 0,
                   [f"BV{kb}", f"at{k2}"], [bres(4)])
                if kb == 0:
                    attn_finish(h, c, qt, sg, 2 + c, False)

    def group_C(b, l, sl):
        S.barrier()
        ar.off = 0
        Sall = ar.alloc([128, 9, 2, 64], F32)
        MEMSET("dve", Sall[:, 0], 0.0, ["S0"])
        mark = ar.off
        for t in range(NT):
            S.barrier()
            ar.off = mark
            e_s = ar.alloc([128, 1024], F32)
            f_s = ar.alloc([128, 1024], F32)
            k_s = ar.alloc([128, 1024], F32)
            lf_hi = ar.alloc([128, 1024], BF16)
            lf_lo = ar.alloc([128, 1024], BF16)
            khat = ar.alloc([128, 4, 256], BF16)
            Eq = ar.alloc([128, 2, 512], F32)
            Ek = ar.alloc([128, 2, 512], F32)
            sq_s = ar.alloc([128, 2, 512], F32)
            qt_ = ar.alloc([128, 2, 512], BF16)
            kt_ = ar.alloc([128, 2, 512], BF16)
            Vc = ar.alloc([128, 4, 256], BF16)
            scm = ar.alloc([128, 16, 64], BF16)
            Sb = ar.alloc([128, 8, 2, 64], BF16)
            sqo = ar.alloc([128, 1024], BF16)
            sgc = ar.alloc([128, 512], F32)
            P0, P1, P2 = PD[0], PD[1], PD[2]
            b01, b23, b45 = [bres(0), bres(1)], [bres(2), bres(3)], [bres(4), bres(5)]
            for i in range(4):
                proj_tm(sl["Cf"], 0, 256, 4 * t + i, P0[:, i * 256:(i + 1) * 256], bres(i // 2))
            ACT(e_s[:], P0[:, :], AF.Exp, b01, ["e_s"], scale=-1.0)
            ACT(e_s[:], e_s[:], AF.Ln, ["e_s"], ["e_s"], bias=1.0)
            ACT(e_s[:], e_s[:], AF.Exp, ["e_s"], ["e_s"], scale=-1.0)
            omb = oml_b[:].unsqueeze(1).to_broadcast([128, 4, 256])
            lbb = lb_b[:].unsqueeze(1).to_broadcast([128, 4, 256])
            v3 = lambda a: a.rearrange("p (i n) -> p i n", i=4)
            TT("dve", v3(e_s[:]), v3(e_s[:]), omb, ALU.mult, ["e_s", "lb"], ["e_s"])
            TT("dve", v3(f_s[:]), v3(e_s[:]), lbb, ALU.add, ["e_s", "lb"], ["f_s"])
            TT("dve", v3(k_s[:]), omb, v3(e_s[:]), ALU.subtract, ["e_s", "lb"], ["k_s"])
            ACT(f_s[:], f_s[:], AF.Ln, ["f_s"], ["f_s"])
            COPY("dve", lf_hi[:], f_s[:], ["f_s"], ["lf_hi"])
            TT("dve", lf_lo[:], f_s[:], lf_hi[:], ALU.subtract, ["f_s", "lf_hi"], ["lf_lo"])
            if CSTOP < 2:
                continue
            for i in range(4):
                o = P1[:, i * 256:(i + 1) * 256]
                MM(o, cbm(C_TBDU), lf_hi[:, i * 256:(i + 1) * 256], True, False, ["lf_hi", "cst"], [bres(2 + i // 2)])
                MM(o, cbm(C_TBDU), lf_lo[:, i * 256:(i + 1) * 256], False, True, ["lf_lo", "cst"], [bres(2 + i // 2)])
            ACT(e_s[:], P1[:, :], AF.Exp, b23, ["e_s"])
            TT("dve", khat[:].rearrange("p i n -> p (i n)"), k_s[:], e_s[:], ALU.mult, ["k_s", "e_s"], ["khat"])
            for c in range(2):
                for i in range(4):
                    o = P2[:, c * 512 + i * 128:c * 512 + (i + 1) * 128]
                    MM(o, lf_hi[:, i * 256 + c * 128:i * 256 + (c + 1) * 128], cbm(C_TBD), True, False,
                       ["lf_hi", "cst"], [bres(4 + c)])
                    MM(o, lf_lo[:, i * 256 + c * 128:i * 256 + (c + 1) * 128], cbm(C_TBD), False, True,
                       ["lf_lo", "cst"], [bres(4 + c)])
            ACT(Eq[:].rearrange("p c n -> p (c n)"), P2[:, :], AF.Exp, b45, ["Eq"])
            ACT(Ek[:].rearrange("p c n -> p (c n)"), P2[:, :], AF.Exp, b45, ["Ek"], scale=-1.0)
            if CSTOP < 3:
                continue
            for c in range(2):
                bk = 6
                proj_fm(sl["Cq"], c * 128, t, bk)
                ACT(sq_s[:, c, :], bank(bk), AF.Silu, [bres(bk)], [f"sq_s{c}"])
                TT("dve", qt_[:, c, :], sq_s[:, c, :], Eq[:, c, :], ALU.mult, [f"sq_s{c}", "Eq"], ["qt_"])
            for c in range(2):
                bk = 6
                proj_fm(sl["Cf"], c * 128, t, bk)
                ACT(sq_s[:, c, :], bank(bk), AF.Exp, [bres(bk)], [f"sq_s{c}"])
                ACT(sq_s[:, c, :], sq_s[:, c, :], AF.Ln, [f"sq_s{c}"], [f"sq_s{c}"], bias=1.0)
                ACT(sq_s[:, c, :], sq_s[:, c, :], AF.Exp, [f"sq_s{c}"], [f"sq_s{c}"], scale=-1.0)
                STT("dve", kt_[:, c, :], sq_s[:, c, :], lbp[:, 2 + c:3 + c], Ek[:, c, :], ALU.mult, ALU.mult,
                    [f"sq_s{c}", "Ek", "lb"], ["kt_"])
            if CSTOP < 4:
                continue
            for i in range(4):
                proj_tm(sl["Ci"], 0, 256, 4 * t + i, P0[:, i * 256:(i + 1) * 256], bres(i // 2))
            COPY("dve", Vc[:].rearrange("p i n -> p (i n)"), P0[:, :], b01, ["Vc"])
            if CSTOP < 5:
                continue
            for hp in range(2):
                pb = hp * 64
                for ch in range(8):
                    i, half = ch // 2, ch % 2
                    for j in range(2):
                        slot = i * 2 + j
                        MM(P1[half * 64:(half + 1) * 64, hp * 512 + slot * 64:hp * 512 + (slot + 1) * 64],
                           kt_[pb:pb + 64, j, ch * 64:(ch + 1) * 64], qt_[pb:pb + 64, j, ch * 64:(ch + 1) * 64],
                           True, True, ["kt_", "qt_"], [bres(2 + hp)])
            TT("dve", scm[:], P1[:, :].rearrange("p (s n) -> p s n", n=64),
               cb[:, C_MASK64:C_MASK64 + 64].unsqueeze(1).to_broadcast([128, 16, 64]), ALU.mult, b23 + ["cst"], ["scm"])
            if CSTOP < 6:
                continue
            def uslot(ch, j):
                return (ch % 2) * 512 + ((ch // 2) * 2 + j) * 64
            for half in range(2):
                for ch in range(half, 8, 2):
                    i = ch // 2
                    for h in range(4):
                        j = h // 2
                        o0 = uslot(ch, j)
                        MM(P0[(h % 2) * 64:(h % 2) * 64 + 64, o0:o0 + 64],
                           khat[half * 64:(half + 1) * 64, i, h * 64:(h + 1) * 64],
                           Vc[half * 64:(half + 1) * 64, i, h * 64:(h + 1) * 64], True, True, ["khat", "Vc"], [bres(half)])
            if CSTOP < 7:
                continue
            for ch in range(8):
                for j in range(2):
                    o0 = uslot(ch, j)
                    STT("dve", Sall[:, ch + 1, j, :], Sall[:, ch, j, :], Eq[:, j, ch * 64 + 63:ch * 64 + 64],
                        P0[:, o0:o0 + 64], ALU.mult, ALU.add, [f"S{ch}", "Eq", bres(ch % 2)], [f"S{ch + 1}"])
            ACT(Sb[:].rearrange("p a b n -> p (a b n)"), Sall[:, 0:8].rearrange("p a b n -> p (a b n)"), AF.Copy,
                [f"S{k}" for k in range(8)], ["Sb"])
            COPY("dve", Sall[:, 0].rearrange("p b n -> p (b n)"), Sall[:, 8].rearrange("p b n -> p (b n)"), ["S8"], ["S0"])
            if CSTOP < 8:
                continue
            for hp in range(2):
                pb = hp * 64
                for j in range(2):
                    for ch in range(8):
                        MM(P2[pb:pb + 64, j * 512 + ch * 64:j * 512 + (ch + 1) * 64], Sb[pb:pb + 64, ch, j, :],
                           qt_[pb:pb + 64, j, ch * 64:(ch + 1) * 64], True, True, ["Sb", "qt_"], [bres(4 + j)])
            for half in range(2):
                for ch in range(half, 8, 2):
                    i = ch // 2
                    for h in range(4):
                        j, hp = h // 2, h % 2
                        o0 = half * 512 + (j * 4 + i) * 64
                        MM(P0[hp * 64:(hp + 1) * 64, o0:o0 + 64],
                           Vc[half * 64:(half + 1) * 64, i, h * 64:(h + 1) * 64],
                           scm[half * 64:(half + 1) * 64, hp * 8 + i * 2 + j, :], True, True, ["Vc", "scm"], [bres(half)])
            for half in range(2):
                COPY("dve", k_s[:].rearrange("p (j i h t) -> p j i h t", j=2, i=4, h=2)[:, :, :, half, :],
                     P0[:, half * 512:(half + 1) * 512].rearrange("p (j i t) -> p j i t", j=2, i=4), [bres(half)], ["k_s"])
            TT("dve", f_s[:], P2[:, :], k_s[:], ALU.add, b45 + ["k_s"], ["f_s"])
            if CSTOP < 9:
                continue
            ACT(sqo[:], f_s[:], AF.Square, ["f_s"], ["sqo"])
            for j in range(2):
                MM(bank(2 + j), cbm(C_BONES), sqo[:, j * 512:(j + 1) * 512], True, True, ["sqo", "cst"], [bres(2 + j)])
            ACT(e_s[:], P1[:, :], AF.Ln, b23, ["e_s"], bias=EPS, scale=1.0 / 64)
            ACT(e_s[:], e_s[:], AF.Exp, ["e_s"], ["e_s"], scale=-0.5)
            for j in range(2):
                STT("dve", f_s[:, j * 512:(j + 1) * 512], f_s[:, j * 512:(j + 1) * 512], hn[:, j:j + 1],
                    e_s[:, j * 512:(j + 1) * 512], ALU.mult, ALU.mult, ["f_s", "e_s", "hn"], ["f_s"])
                proj_fm(sl["Cg"], j * 128, t, 6)
                ACT(sgc[:], bank(6), AF.Silu, [bres(6)], ["sgc"])
                TT("dve", mixT[:, 4 + j, t * 512:(t + 1) * 512], f_s[:, j * 512:(j + 1) * 512], sgc[:], ALU.mult,
                   ["f_s", "sgc"], [f"mx{4 + j}_{t}"])
        for k in ("Cq", "Cf", "Ci", "Cg"):
            slab_release(sl[k])

    def group_D(b, l, sl):
        S.barrier()
        ar.off = 0
        PADW = 16 + T
        sgd = ar.alloc([128, 512], F32)
        for c in range(2):
            S.barrier()
            ar.off = 2048
            u = ar.alloc([128, PADW], F32)
            A1 = ar.alloc([128, PADW], F32)
            A2 = ar.alloc([128, PADW], F32)
            dfb = ar.alloc([128, T], BF16)
            tmp = ar.alloc([128, 16], F32)
            for a in (u, A1, A2):
                MEMSET("pool", a[:, 0:16], 0.0, ["dpad"])
            for t in range(NT):
                bk = nxt() % 2
                proj_fm(sl["Dv"], c * 128, t, bk)
                ACT(u[:, 16 + t * 512:16 + (t + 1) * 512], bank(bk), AF.Copy, [bres(bk)], ["u"])
            lo, hi = slice(0, 64), slice(64, 128)
            TT("pool", A1[:, 16:], u[:, 16:], u[:, 15:PADW - 1], ALU.add, ["u", "dpad"], ["A1"])
            if c == 0:
                TT("pool", A2[hi, 16:], A1[hi, 16:], A1[hi, 14:PADW - 2], ALU.add, ["A1", "dpad"], ["A2"])
                fins = ((lo, A1), (hi, A2))
            else:
                TT("pool", A2[:, 16:], A1[:, 16:], A1[:, 14:PADW - 2], ALU.add, ["A1", "dpad"], ["A2"])
                TT("pool", A1[:, 16:], A2[:, 16:], A2[:, 12:PADW - 4], ALU.add, ["A2", "dpad"], ["A1"])
                TT("pool", A2[hi, 16:], A1[hi, 16:], A1[hi, 8:PADW - 8], ALU.add, ["A1", "dpad"], ["A2"])
                fins = ((lo, A1), (hi, A2))
            for (hs, fin) in fins:
                STT("dve", dfb[hs, :], fin[hs, 16:], cf[hs, C_INVW + c:C_INVW + c + 1], u[hs, 16:], ALU.mult, ALU.subtract,
                    ["A1", "A2", "u", "cst"], ["dfb"])
                TT("dve", tmp[hs, :], fin[hs, 16:32], cf[hs, C_RC + c * 16:C_RC + (c + 1) * 16], ALU.mult,
                   ["A1", "A2", "cst"], ["dtmp"])
                TT("dve", dfb[hs, 0:16], tmp[hs, :], u[hs, 16:32], ALU.subtract, ["dtmp", "u", "dfb"], ["dfb"])
            for t in range(NT):
                bk = 2 + nxt() % 2
                MM(bank(bk), pwb[:, c, :], dfb[:, t * 512:(t + 1) * 512], True, True, ["dfb", "pwb"], [bres(bk)])
                proj_fm(sl["Dg"], c * 128, t, 6)
                ACT(sgd[:], bank(6), AF.Silu, [bres(6)], ["sgd"])
                STT("dve", mixT[:, 6 + c, t * 512:(t + 1) * 512], bank(bk), psc[:, c:c + 1], sgd[:], ALU.mult, ALU.mult,
                    [bres(bk), "sgd", "psc"], [f"mx{6 + c}_{t}"])
        slab_release(sl["Dv"])
        slab_release(sl["Dg"])

    def group_E(b, l, sl):
        nonlocal ar_rec
        S.barrier()
        ar.off = 0
        qm = ar.alloc([128, 2, T], BF16)
        hb = (ar.alloc([128, 2, 512], BF16), ar.alloc([128, 2, 512], F32))
        mt = ar.alloc([128, 2, D], F32)
        mnb = ar.alloc([128, D], BF16)
        mnT = ar.alloc([128, 8, 256], BF16)
        mss = ar.alloc([128, 4], F32)
        ptb = ar.alloc([128, 3, 512], BF16)
        ar_rec = ar.alloc([128, 512], F32)
        sge = ar.alloc([128, 512], F32)
        for mb in range(2):
            S.dma("sp", f"memld{mb}", mt[:, mb, :], mem[b, mb * 128:(mb + 1) * 128, :], writes=[f"mt{mb}"])
            sc, rc_ = mss[:, mb:mb + 1], mss[:, 2 + mb:3 + mb]
            ACT(sqj[:], mt[:, mb, :], AF.Square, [f"mt{mb}"], ["sqj", "mss"], accum_out=sc)
            ACT(rc_, sc, AF.Ln, ["mss"], ["mrs"], bias=EPS, scale=1.0 / D)
            ACT(rc_, rc_, AF.Exp, ["mrs"], ["mrs"], scale=-0.5)
            STT("dve", mnb[:], mt[:, mb, :], rc_, mg_b[:], ALU.mult, ALU.mult, [f"mt{mb}", "mrs", "mg_b"], ["mnb"])
            for c in range(8):
                S.pe(lambda e, c=c: e.transpose(out=PTt[:, c * 128:(c + 1) * 128], in_=mnb[:, c * 128:(c + 1) * 128],
                                                identity=cbm(C_IDENT)), ["mnb", "cst"], ["b7"])
            ACT(mnT[:, :, mb * 128:(mb + 1) * 128], PTt[:, :].rearrange("p (c n) -> p c n", c=8), AF.Copy, ["b7"], ["mnT"])
        for c in range(2):
            bk = nxt() % 2
            for kc in range(8):
                MM(bank(bk)[:, 0:256], sl["mK"].ap[:, kc, c * 128:(c + 1) * 128], mnT[:, kc, :], kc == 0, kc == 7,
                   [sl["mK"].res, "mnT"], [bres(bk)])
            headnorm(bk, 256, pv[:, 3:4], kmT[:, c, :], "kmT", 2 + bk, hb)
        slab_release(sl["mK"])
        for mb in range(2):
            bk = 4 + mb
            for kc in range(8):
                MM(bank(bk)[:, 0:256], mnT[:, kc, mb * 128:(mb + 1) * 128], sl["mV"].ap[:, kc, 0:256], kc == 0, kc == 7,
                   [sl["mV"].res, "mnT"], [bres(bk)])
            COPY("dve", Vm[:, mb, :], bank(bk)[:, 0:256], [bres(bk)], ["Vm"])
        slab_release(sl["mV"])
        units = []
        for c in range(2):
            for t in range(NT):
                units.append((lambda bk, c=c, t=t: proj_fm(sl["Eq"], c * 128, t, bk), 512, pv[:, 2:3],
                              qm[:, c, t * 512:(t + 1) * 512], f"Eq{c}_{t}"))
        hn_units(units, hb)
        slab_release(sl["Eq"])
        scale = 64 ** -0.5
        iters = [(c, t, hh, mb) for c in range(2) for t in range(NT) for hh in range(2) for mb in range(2)]

        def emit_SE(i):
            c, t, hh, mb = iters[i]
            hs = slice(hh * 64, hh * 64 + 64)
            MM(bank(i % 4), kmT[hs, c, mb * 128:(mb + 1) * 128], qm[hs, c, t * 512:(t + 1) * 512], True, True,
               ["kmT", f"Eq{c}_{t}"], [bres(i % 4)])

        emit_SE(0)
        emit_SE(1)
        for i, (c, t, hh, mb) in enumerate(iters):
            if i + 2 < len(iters):
                emit_SE(i + 2)
            h = 2 * c + hh
            hs = slice(hh * 64, hh * 64 + 64)
            k3 = i % 3
            ACT(ptb[:, k3, :], bank(i % 4), AF.Exp, [bres(i % 4)], [f"pt{k3}"], scale=scale)
            MM(bank(4)[hs, :], Vm[:, mb, h * 64:(h + 1) * 64], ptb[:, k3, :], mb == 0, mb == 1, ["Vm", f"pt{k3}"], [bres(4)])
            MM(bank(5)[hs, :], cbm(C_ONES, 64), ptb[:, k3, :], mb == 0, mb == 1, [f"pt{k3}", "cst"], [bres(5)])
            if hh == 1 and mb == 1:
                ACT(ar_rec[:], bank(5), AF.Ln, [bres(5)], ["rec"])
                ACT(ar_rec[:], ar_rec[:], AF.Exp, ["rec"], ["rec"], scale=-1.0)
                TT("dve", ar_rec[:], bank(4), ar_rec[:], ALU.mult, [bres(4), "rec"], ["rec"])
                proj_fm(sl["Eg"], c * 128, t, 6)
                ACT(sge[:], bank(6), AF.Silu, [bres(6)], ["sge"])
                TT("dve", mixT[:, 8 + c, t * 512:(t + 1) * 512], ar_rec[:], sge[:], ALU.mult, ["rec", "sge"], [f"mx{8 + c}_{t}"])
        slab_release(sl["Eg"])

    def wout_phase(b, l, fuse_rms=False):
        S.barrier()
        src = x[b] if l == 0 else xs
        if l < NLAYER - 1:
            dst = xs
        else:
            dst = y[b]

        def rms2(blk):
            hk = blk % 2
            for c in range(8):
                S.pe(lambda e, c=c, hk=hk: e.transpose(out=PTt[:, c * 128:(c + 1) * 128],
                                                       in_=hbuf[:, hk, c * 128:(c + 1) * 128], identity=cbm(C_IDENT)),
                     [f"hb{hk}", "cst"], ["b7"])
            ACT(hT[:, :, blk * 128:(blk + 1) * 128], PTt[:, :].rearrange("p (c n) -> p c n", c=8), AF.Copy,
                ["b7"], [f"hT{blk // 4}"])

        for blk in range(NB):
            k = nxt("xst") % 2
            xr = f"xst{k}"
            rd = [] if l == 0 else [f"xs{blk}"]
            S.dma("sp", xr, xst[:, k, :], src[blk * 128:(blk + 1) * 128, :], reads=rd, writes=[xr])
            for dh in range(2):
                bk = nxt() % 4
                for kc in range(10):
                    MM(bank(bk), mixT[:, kc, blk * 128:(blk + 1) * 128], Wo[:, kc, dh * 512:(dh + 1) * 512], kc == 0, kc == 9,
                       [f"mx{kc}_{blk // 4}", "Wo"], [bres(bk)])
                TT("dve", xst[:, k, dh * 512:(dh + 1) * 512], xst[:, k, dh * 512:(dh + 1) * 512], bank(bk), ALU.add,
                   [xr, bres(bk)], [xr])
            if l < NLAYER - 1:
                S.dma("sp", "xout", dst[blk * 128:(blk + 1) * 128, :], xst[:, k, :], reads=[xr], writes=[f"xs{blk}"])
            else:
                S.dma("sp", "xout", dst[blk * 128:(blk + 1) * 128, :], xst[:, k, :], reads=[xr], writes=["yout"])
            if fuse_rms:
                sc = ssb[:, blk:blk + 1]
                rc_ = ssb[:, 16 + blk:17 + blk]
                ACT(sqj[:], xst[:, k, :], AF.Square, [xr], ["sqj", f"ss{blk}"], accum_out=sc)
                ACT(rc_, sc, AF.Ln, [f"ss{blk}"], [f"rs{blk}"], bias=EPS, scale=1.0 / D)
                ACT(rc_, rc_, AF.Exp, [f"rs{blk}"], [f"rs{blk}"], scale=-0.5)
                hk = blk % 2
                STT("dve", hbuf[:, hk, :], xst[:, k, :], rc_, g_b[:], ALU.mult, ALU.mult,
                    [xr, f"rs{blk}", "g_b"], [f"hb{hk}"])
                if blk > 0:
                    rms2(blk - 1)
        if fuse_rms:
            rms2(NB - 1)

    ar_rec = None
    if dbg:
        MEMSET("dve", mixT[:], 0.0, [f"mx{c}_{t}" for c in range(10) for t in range(NT)])
    for b in range(NSEQ):
        for l in range(NLAYER):
            if l == 0:
                load_params(l)
            sl = {}
            for (k, c0, n) in (("Aq", 0, 256), ("Ak", 256, 256), ("Av", 512, 256), ("Aff", 1024, 4), ("Ag", 768, 256)):
                sl[k] = slab_load(win_cols(l, c0, n), n)
            if l == 0:
                rms_phase(b, l)
            def load_B(l=l, sl=sl):
                if "Bq" in sl:
                    return
                for (k, c0, n) in (("Bq", 1028, 256), ("Bk", 1284, 256), ("Bv", 1540, 256), ("Bg", 1796, 256)):
                    sl[k] = slab_load(win_cols(l, c0, n), n)

            def load_C(l=l, sl=sl):
                if "Cq" in sl:
                    return
                for (k, c0, n) in (("Cq", 2052, 256), ("Cf", 2308, 256), ("Ci", 2564, 256), ("Cg", 2820, 256)):
                    sl[k] = slab_load(win_cols(l, c0, n), n)
                for kc in range(10):
                    S.dma("pool", "Wo", Wo[:, kc, :], w_out[l, kc * 128:(kc + 1) * 128, :], writes=["Wo"])

            if "A" in groups:
                group_A(b, l, sl, prefetch=load_B)
            else:
                for k in ("Aq", "Ak", "Av", "Aff", "Ag"):
                    slab_release(sl[k])
            load_B()
            if "B" in groups:
                group_B(b, l, sl, prefetch=load_C)
            else:
                for k in ("Bq", "Bk", "Bv", "Bg"):
                    slab_release(sl[k])
            load_C()
            if "C" in groups:
                group_C(b, l, sl)
            else:
                for k in ("Cq", "Cf", "Ci", "Cg"):
                    slab_release(sl[k])
            for (k, c0, n) in (("Dv", 3076, 256), ("Dg", 3332, 256)):
                sl[k] = slab_load(win_cols(l, c0, n), n)
            sl["mK"] = slab_load(wkv_cols(l, 0, 256), 256)
            sl["mV"] = slab_load(wkv_cols(l, 256, 256), 256)
            if "D" in groups:
                group_D(b, l, sl)
            else:
                slab_release(sl["Dv"])
                slab_release(sl["Dg"])
            for (k, c0, n) in (("Eq", 3588, 256), ("Eg", 3844, 256)):
                sl[k] = slab_load(win_cols(l, c0, n), n)
            if "E" in groups:
                group_E(b, l, sl)
            else:
                for k in ("mK", "mV", "Eq", "Eg"):
                    slab_release(sl[k])
            if l + 1 < NLAYER:
                S.barrier()
                load_params(l + 1)
                wout_phase(b, l, fuse_rms=True)
            else:
                wout_phase(b, l)
    fin_reads = ["yout"]
    if dbg:
        S.barrier()
        S.dma("sp", "dbg", dbg_out, mixT[:], reads=[f"mx{c}_{t}" for c in range(10) for t in range(NT)], writes=["dbgo"])
        fin_reads.append("dbgo")
    S.add("sp", None, reads=fin_reads)
    S.emit(st)
    st.close()
    nc._arena_peak = ar.peak
    return nc


_CACHE = {}


def kernel(x, mem, norm_g, w_in, fox_f_bias, fox_q_norm, fox_k_norm, hgrn_lb_logits, hgrn_out_norm, pool_w,
           pool_scale, mem_norm_g, mem_w_kv, mem_q_norm, mem_k_norm, w_out):
    ncores = 8
    f = lambda a: np.ascontiguousarray(np.asarray(a, dtype=np.float32))
    x = f(x)
    mem = f(mem)
    B = x.shape[0]
    per = B // ncores
    if "nc" not in _CACHE:
        _CACHE["nc"] = build_nc(per, 2)
    nc = _CACHE["nc"]
    shared = dict(norm_g=f(norm_g), w_in=f(w_in), fox_f_bias=f(fox_f_bias), fox_q_norm=f(fox_q_norm),
                  fox_k_norm=f(fox_k_norm), hgrn_lb_logits=f(hgrn_lb_logits), hgrn_out_norm=f(hgrn_out_norm),
                  pool_w=f(pool_w), pool_scale=f(pool_scale), mem_norm_g=f(mem_norm_g), mem_w_kv=f(mem_w_kv),
                  mem_q_norm=f(mem_q_norm), mem_k_norm=f(mem_k_norm), w_out=f(w_out), cpack=make_cpack())
    in_maps = []
    for i in range(ncores):
        d = dict(shared)
        d["x"] = np.ascontiguousarray(x[i * per:(i + 1) * per])
        d["mem"] = np.ascontiguousarray(mem[i * per:(i + 1) * per])
        in_maps.append(d)
    res = run_bass_kernel_spmd(nc, in_maps, core_ids=list(range(ncores)))
    return np.concatenate([np.asarray(r["y"], dtype=np.float32) for r in res.results], axis=0)
```

```python
import numpy as np
from contextlib import ExitStack
import concourse.bass as bass
import concourse.mybir as mybir
from concourse.bass_utils import run_bass_kernel_spmd

F32 = mybir.dt.float32
BF16 = mybir.dt.bfloat16
AF = mybir.ActivationFunctionType
ALU = mybir.AluOpType

ENGS = ("pe", "act", "dve", "pool", "sp")


class Op:
    __slots__ = ("eng", "fn", "reads", "writes", "chan", "deps", "mile", "cnt", "waits", "idx")

    def __init__(self, eng, fn, reads, writes, chan):
        self.eng, self.fn, self.reads, self.writes, self.chan = eng, fn, tuple(reads), tuple(writes), chan
        self.deps = []
        self.mile = False
        self.cnt = 0
        self.waits = []


class Sched:
    def __init__(self, nc):
        self.nc = nc
        self.ops = []

    def add(self, eng, fn, reads=(), writes=(), chan=None):
        op = Op(eng, fn, reads, writes, chan)
        op.idx = len(self.ops)
        self.ops.append(op)
        return op

    def pe(self, fn, reads=(), writes=()):
        return self.add("pe", fn, reads, writes)

    def act(self, fn, reads=(), writes=()):
        return self.add("act", fn, reads, writes)

    def dve(self, fn, reads=(), writes=()):
        return self.add("dve", fn, reads, writes)

    def pool(self, fn, reads=(), writes=()):
        return self.add("pool", fn, reads, writes)

    def dma(self, q, chan, out, in_, reads=(), writes=(), **kw):
        return self.add(q, lambda e: e.dma_start(out=out, in_=in_, **kw), reads, writes, chan=chan)

    def barrier(self):
        return self.add("bar", None)

    def resolve(self):
        ops = self.ops
        last_w = {}
        readers = {}
        last_comp = {}
        pend = {e: [] for e in ENGS}
        for op in ops:
            if op.eng == "bar":
                L = list(last_comp.values())
                for e in ENGS:
                    pend[e] = list(L)
                continue
            deps = set(pend[op.eng])
            pend[op.eng] = []
            for r in op.reads:
                w = last_w.get(r)
                if w is not None:
                    deps.add(w)
            for r in op.writes:
                w = last_w.get(r)
                if w is not None:
                    deps.add(w)
                for rd in readers.get(r, ()):
                    deps.add(rd)
            deps.discard(op.idx)
            op.deps = sorted(deps)
            for r in op.writes:
                last_w[r] = op.idx
                readers[r] = []
            for r in op.reads:
                if r not in op.writes:
                    readers.setdefault(r, []).append(op.idx)
            if op.chan is None:
                last_comp[op.eng] = op.idx
        chan_cnt = {}
        need = []
        for op in ops:
            lst = []
            if op.eng != "bar":
                for d in op.deps:
                    dop = ops[d]
                    if dop.chan is not None:
                        lst.append(("chan", dop.chan, chan_cnt[dop.chan]))
                    else:
                        if dop.eng == op.eng and op.chan is None:
                            if op.eng == "pe":
                                continue
                        dop.mile = True
                        lst.append(("eng", dop.eng, d))
            need.append(lst)
            if op.chan is not None:
                chan_cnt[op.chan] = chan_cnt.get(op.chan, 0) + 1
        ecnt = {e: 0 for e in ENGS}
        for op in ops:
            if op.eng != "bar" and op.chan is None and op.mile:
                ecnt[op.eng] += 1
                op.cnt = ecnt[op.eng]
        waited = {e: {} for e in ENGS}
        for op, lst in zip(ops, need):
            if op.eng == "bar":
                continue
            w = {}
            for kind, key, v in lst:
                if kind == "chan":
                    k = ("chan", key)
                    val = 16 * v
                else:
                    k = ("eng", key)
                    val = ops[v].cnt
                if val > w.get(k, 0):
                    w[k] = val
            out = []
            for k, val in w.items():
                if waited[op.eng].get(k, 0) >= val:
                    continue
                waited[op.eng][k] = val
                out.append((k, val))
            op.waits = out
        self.chans = sorted({op.chan for op in ops if op.chan is not None})

    def emit(self, stack):
        nc = self.nc
        self.resolve()
        sems = {}
        for e in ENGS:
            sems[("eng", e)] = stack.enter_context(nc.semaphore("s_" + e))
        for c in self.chans:
            sems[("chan", c)] = stack.enter_context(nc.semaphore("c_" + str(c)))
        per = {e: [op for op in self.ops if op.eng == e] for e in ENGS}
        block = stack.enter_context(nc.Block())

        def run(eng_handle, lst):
            for op in lst:
                for k, val in op.waits:
                    eng_handle.wait_ge(sems[k], val)
                if op.fn is None:
                    continue
                inst = op.fn(eng_handle)
                if op.chan is not None:
                    inst.then_inc(sems[("chan", op.chan)], 16)
                elif op.mile:
                    inst.then_inc(sems[("eng", op.eng)], 1)

        @block.tensor
        def _(e):
            run(e, per["pe"])

        @block.scalar
        def _(e):
            run(e, per["act"])

        @block.vector
        def _(e):
            run(e, per["dve"])

        @block.gpsimd
        def _(e):
            run(e, per["pool"])

        @block.sync
        def _(e):
            run(e, per["sp"])


import os
CSTOP = int(os.environ.get('CSTOP', '99'))
T = 2048
D = 1024
DIN = 4100
NB = 16
NT = 4
EPS = 1e-6
POOL_WINDOWS = (2, 4, 8, 16)

C_IDENT, C_NEGID, C_TRI, C_ONES, C_NEGM, C_MLT, C_NUINCL, C_TBD, C_TBDU, C_BONES, C_NONES = range(11)
C_MASK64 = 11 * 128
C_RC = C_MASK64 + 64
C_INVW = C_RC + 32
C_ZERO = C_INVW + 2
NCF = C_ZERO + 64


def make_cpack():
    p = np.arange(128)[:, None]
    c = np.arange(128)[None, :]
    cp = np.zeros((128, NCF), np.float32)

    def put(i, m):
        cp[:, i * 128:(i + 1) * 128] = m.astype(np.float32)

    put(C_IDENT, p == c)
    put(C_NEGID, -(p == c).astype(np.float32))
    put(C_TRI, p <= c)
    put(C_ONES, np.ones((128, 128)))
    put(C_NEGM, np.where(p > c, -1e30, 0.0))
    put(C_MLT, p < c)
    put(C_NUINCL, -(p >= c).astype(np.float32))
    same = (p // 64) == (c // 64)
    put(C_TBD, same & (p <= c))
    put(C_TBDU, same & (p > c))
    put(C_BONES, same)
    put(C_NONES, -np.ones((128, 128)))
    cp[:, C_MASK64:C_MASK64 + 64] = ((p % 64) <= np.arange(64)[None, :]).astype(np.float32)
    for ch in range(2):
        for half in range(2):
            win = POOL_WINDOWS[2 * ch + half]
            pos = np.arange(1, 17, dtype=np.float32)
            cp[half * 64:(half + 1) * 64, C_RC + ch * 16:C_RC + (ch + 1) * 16] = 1.0 / np.minimum(pos, win)[None, :]
            cp[half * 64:(half + 1) * 64, C_INVW + ch] = 1.0 / win
    return cp


class Arena:
    def __init__(self, ap, nbytes):
        self.ap, self.cap, self.off = ap, nbytes, 0
        self.peak = 0

    def alloc(self, shape, dtype):
        isz = 4 if dtype == F32 else 2
        n = int(np.prod(shape[1:]))
        nb = n * isz
        off = (self.off + 63) // 64 * 64
        assert off + nb <= self.cap, ("arena overflow", off + nb, self.cap)
        a = self.ap[:, off // 2:(off + nb) // 2]
        if dtype == F32:
            a = a.bitcast(F32)
        if len(shape) == 3:
            a = a.rearrange("p (c n) -> p c n", c=shape[1])
        elif len(shape) == 4:
            a = a.rearrange("p (a b n) -> p a b n", a=shape[1], b=shape[2])
        self.off = off + nb
        self.peak = max(self.peak, self.off)
        return a


def build_nc(NSEQ=4, NLAYER=2, dbg=False, groups="ABCDE"):
    nc = bass.Bass("TRN2", target_bir_lowering=False)

    def din(name, shape):
        return nc.dram_tensor(name, list(shape), F32, kind="ExternalInput").ap()

    x = din("x", [NSEQ, T, D])
    mem = din("mem", [NSEQ, 256, D])
    norm_g = din("norm_g", [2, D])
    w_in = din("w_in", [2, D, DIN])
    fox_f_bias = din("fox_f_bias", [2, 4])
    fox_q_norm = din("fox_q_norm", [2, 64])
    fox_k_norm = din("fox_k_norm", [2, 64])
    hgrn_lb_logits = din("hgrn_lb_logits", [2, 256])
    hgrn_out_norm = din("hgrn_out_norm", [2, 256])
    pool_w = din("pool_w", [2, 4, 64, 64])
    pool_scale = din("pool_scale", [2, 256])
    mem_norm_g = din("mem_norm_g", [2, D])
    mem_w_kv = din("mem_w_kv", [2, D, 512])
    mem_q_norm = din("mem_q_norm", [2, 64])
    mem_k_norm = din("mem_k_norm", [2, 64])
    w_out = din("w_out", [2, 1280, D])
    cpack = din("cpack", [128, NCF])
    y = nc.dram_tensor("y", [NSEQ, T, D], F32, kind="ExternalOutput").ap()
    xs = nc.dram_tensor("xs_scratch", [T, D], F32).ap()
    if dbg:
        dbg_out = nc.dram_tensor("dbg", [128, 10, T], BF16, kind="ExternalOutput").ap()

    st = ExitStack()
    S = Sched(nc)

    def sb(name, shape, dtype):
        return st.enter_context(nc.sbuf_tensor(name, list(shape), dtype))

    hT = sb("hT", [128, 8, T], BF16)
    mixT = sb("mixT", [128, 10, T], BF16)
    Wo = sb("Wo", [128, 10, D], BF16)
    ws = sb("ws", [128, 5, 8, 256], BF16)
    cf = sb("cf", [128, NCF], F32)
    cb = sb("cb", [128, NCF], BF16)
    g_b = sb("g_b", [128, D], F32)
    mg_b = sb("mg_b", [128, D], F32)
    xst = sb("xst", [128, 2, D], F32)
    hbuf = sb("hbuf", [128, 2, D], BF16)
    sqj = sb("sqj", [128, D], BF16)
    ssb = sb("ssb", [128, 32], F32)
    pv = sb("pv", [128, 8], F32)
    hn = sb("hn", [128, 2], F32)
    psc = sb("psc", [128, 2], F32)
    lbl = sb("lbl", [128, 2, 2], F32)
    lblb = sb("lblb", [128, 2, 256], F32)
    lbp = sb("lbp", [128, 4], F32)
    lb_b = sb("lb_b", [128, 256], F32)
    oml_b = sb("oml_b", [128, 256], F32)
    fb_b = sb("fb_b", [128, 4], F32)
    pwf = sb("pwf", [128, 2, 128], F32)
    pwb = sb("pwb", [128, 2, 128], BF16)
    kmT = sb("kmT", [128, 2, 256], BF16)
    Vm = sb("Vm", [128, 2, 256], BF16)
    ARENA_BYTES = 54 * 1024
    arena_t = sb("arena", [128, ARENA_BYTES // 2], BF16)
    ar = Arena(arena_t, ARENA_BYTES)

    PD = [st.enter_context(nc.psum_tensor(f"pd{i}", [128, 1024], F32)) for i in range(3)]
    PX = st.enter_context(nc.psum_tensor("px", [128, 512], F32))
    PTt = st.enter_context(nc.psum_tensor("ptt", [128, 1024], BF16))

    def bank(i):
        if i < 6:
            return PD[i // 2][:, (i % 2) * 512:(i % 2 + 1) * 512]
        return PX[:, :]

    def bres(i):
        return f"b{i}"

    def MM(out, lhsT, rhs, start, stop, r, w):
        S.pe(lambda e: e.matmul(out, lhsT=lhsT, rhs=rhs, start=start, stop=stop, skip_group_check=True), r, w)

    def ACT(out, in_, func, r, w, **kw):
        S.act(lambda e: e.activation(out=out, in_=in_, func=func, **kw), r, w)

    def TT(eng, out, in0, in1, op, r, w):
        S.add(eng, lambda e: e.tensor_tensor(out=out, in0=in0, in1=in1, op=op), r, w)

    def TS(eng, out, in0, s1, s2, op0, op1, r, w):
        if s2 is None:
            S.add(eng, lambda e: e.tensor_scalar(out=out, in0=in0, scalar1=s1, scalar2=None, op0=op0), r, w)
        else:
            S.add(eng, lambda e: e.tensor_scalar(out=out, in0=in0, scalar1=s1, scalar2=s2, op0=op0, op1=op1), r, w)

    def STT(eng, out, in0, scalar, in1, op0, op1, r, w):
        S.add(eng, lambda e: e.scalar_tensor_tensor(out=out, in0=in0, scalar=scalar, in1=in1, op0=op0, op1=op1), r, w)

    def COPY(eng, out, in_, r, w):
        S.add(eng, lambda e: e.tensor_copy(out=out, in_=in_), r, w)

    def RECIP(out, in_, r, w):
        S.dve(lambda e: e.reciprocal(out=out, in_=in_), r, w)

    def MEMSET(eng, ap, val, w):
        S.add(eng, lambda e: e.memset(ap, val), (), w)

    def cbm(i, n=128):
        return cb[:, i * 128:i * 128 + n]

    def cfm(i, n=128):
        return cf[:, i * 128:i * 128 + n]

    S.dma("sp", "cst", cf[:], cpack, writes=["cst"])
    COPY("dve", cb[:], cf[:], ["cst"], ["cst"])

    slab_free = [True] * 5

    class Slab:
        pass

    def slab_load(src3d, ncols):
        slot = slab_free.index(True)
        slab_free[slot] = False
        s = Slab()
        s.slot, s.res, s.ap = slot, f"ws{slot}", ws[:, slot]
        S.dma("pool", f"ws{slot}", ws[:, slot, :, 0:ncols], src3d, writes=[s.res])
        return s

    def slab_release(s):
        slab_free[s.slot] = True

    def win_cols(l, c0, n):
        return w_in[l].rearrange("(c p) n -> p c n", p=128)[:, :, c0:c0 + n]

    def wkv_cols(l, c0, n):
        return mem_w_kv[l].rearrange("(c p) n -> p c n", p=128)[:, :, c0:c0 + n]

    def proj_fm(s, col0, t, bk):
        for kc in range(8):
            MM(bank(bk), s.ap[:, kc, col0:col0 + 128], hT[:, kc, t * 512:(t + 1) * 512], kc == 0, kc == 7,
               [s.res, f"hT{t}"], [bres(bk)])

    def proj_tm(s, col0, ncols, blk, out_ap, ores):
        for kc in range(8):
            MM(out_ap, hT[:, kc, blk * 128:(blk + 1) * 128], s.ap[:, kc, col0:col0 + ncols], kc == 0, kc == 7,
               [s.res, f"hT{blk // 4}"], [ores])

    rot = {}

    def nxt(name="g"):
        rot[name] = rot.get(name, 0) + 1
        return rot[name]

    def headnorm(src_bk, N, gain_col, out_ap, out_res, stat_bk, hb):
        k = nxt("hn") % 2
        sq, lnv = hb
        src = bank(src_bk)[:, 0:N]
        ACT(sq[:, k, 0:N], src, AF.Square, [bres(src_bk)], [f"hnsq{k}"])
        MM(bank(stat_bk)[:, 0:N], cbm(C_BONES), sq[:, k, 0:N], True, True, [f"hnsq{k}", "cst"], [bres(stat_bk)])
        ACT(lnv[:, k, 0:N], bank(stat_bk)[:, 0:N], AF.Ln, [bres(stat_bk)], [f"hnln{k}"], bias=EPS, scale=1.0 / 64)
        ACT(lnv[:, k, 0:N], lnv[:, k, 0:N], AF.Exp, [f"hnln{k}"], [f"hnln{k}"], scale=-0.5)
        STT("dve", out_ap, src, gain_col, lnv[:, k, 0:N], ALU.mult, ALU.mult,
            [bres(src_bk), f"hnln{k}", "pv"], [out_res])

    def hn_units(units, hb):
        n = len(units)
        if n:
            units[0][0](0)
        for u in range(n):
            if u + 1 < n:
                units[u + 1][0]((u + 1) % 4)
            _, N, gcol, oap, ores = units[u]
            headnorm(u % 4, N, gcol, oap, ores, 4 + u % 2, hb)

    def load_params(l):
        S.dma("sp", "g_b", g_b[:], norm_g[l].partition_broadcast(128), writes=["g_b"])
        S.dma("sp", "mg_b", mg_b[:], mem_norm_g[l].partition_broadcast(128), writes=["mg_b"])
        for i, src in enumerate((fox_q_norm, fox_k_norm, mem_q_norm, mem_k_norm)):
            col = src[l].rearrange("(p o) -> p o", o=1)
            S.dma("sp", "pv", pv[0:64, i:i + 1], col, writes=["pv"])
            S.dma("sp", "pv", pv[64:128, i:i + 1], col, writes=["pv"])
        S.dma("sp", "hn", hn[:], hgrn_out_norm[l].rearrange("(c p) -> p c", p=128), writes=["hn"],
              allow_slow_non_contiguous=True)
        S.dma("sp", "psc", psc[:], pool_scale[l].rearrange("(c p) -> p c", p=128), writes=["psc"],
              allow_slow_non_contiguous=True)
        S.dma("sp", "fb_b", fb_b[:], fox_f_bias[l].partition_broadcast(128), writes=["fb_b"])
        MEMSET("dve", pwf[:], 0.0, ["pwf"])
        for g in range(4):
            hs = slice((g % 2) * 64, (g % 2) * 64 + 64)
            S.dma("sp", "pwf", pwf[hs, g // 2, (g % 2) * 64:(g % 2) * 64 + 64], pool_w[l, g], writes=["pwf"])
        COPY("dve", pwb[:], pwf[:], ["pwf"], ["pwb"])
        if l == 0:
            MEMSET("dve", lbp[:, 0:2], 0.0, ["lb"])
            MEMSET("dve", lbp[:, 2:4], 1.0, ["lb"])
            MEMSET("dve", lb_b[:], 0.0, ["lb"])
            MEMSET("dve", oml_b[:], 1.0, ["lb"])
        else:
            S.dma("sp", "lbl", lbl[:], hgrn_lb_logits.rearrange("l (c p) -> p l c", p=128), writes=["lbl"],
                  allow_slow_non_contiguous=True)
            S.dma("sp", "lblb", lblb[:], hgrn_lb_logits.partition_broadcast(128), writes=["lblb"])
            for (dst_lb, dst_oml, a0, a1) in ((lbp[:, 0:2], lbp[:, 2:4], lbl[:, 0, :], lbl[:, 1, :]),
                                             (lb_b[:], oml_b[:], lblb[:, 0, :], lblb[:, 1, :])):
                TT("dve", dst_oml, a1, a0, ALU.subtract, ["lbl", "lblb", "lb"], ["lb"])
                ACT(dst_oml, dst_oml, AF.Exp, ["lb"], ["lb"], scale=-1.0)
                TS("dve", dst_oml, dst_oml, 1.0, None, ALU.add, None, ["lb"], ["lb"])
                RECIP(dst_lb, dst_oml, ["lb"], ["lb"])
                TS("dve", dst_lb, dst_lb, 1.0 - 1e-6, None, ALU.min, None, ["lb"], ["lb"])
                TS("dve", dst_oml, dst_lb, -1.0, 1.0, ALU.mult, ALU.add, ["lb"], ["lb"])

    def rms_phase(b, l):
        src = x[b] if l == 0 else xs
        st1 = {}

        def stage1(blk):
            k = nxt("xst") % 2
            xr = f"xst{k}"
            rd = [] if l == 0 else [f"xs{blk}"]
            S.dma("sp", xr, xst[:, k, :], src[blk * 128:(blk + 1) * 128, :], reads=rd, writes=[xr])
            sc = ssb[:, blk:blk + 1]
            rc_ = ssb[:, 16 + blk:17 + blk]
            ACT(sqj[:], xst[:, k, :], AF.Square, [xr], ["sqj", f"ss{blk}"], accum_out=sc)
            ACT(rc_, sc, AF.Ln, [f"ss{blk}"], [f"rs{blk}"], bias=EPS, scale=1.0 / D)
            ACT(rc_, rc_, AF.Exp, [f"rs{blk}"], [f"rs{blk}"], scale=-0.5)
            hk = blk % 2
            STT("dve", hbuf[:, hk, :], xst[:, k, :], rc_, g_b[:], ALU.mult, ALU.mult,
                [xr, f"rs{blk}", "g_b"], [f"hb{hk}"])

        def stage2(blk):
            hk = blk % 2
            for c in range(8):
                S.pe(lambda e, c=c, hk=hk: e.transpose(out=PTt[:, c * 128:(c + 1) * 128],
                                                       in_=hbuf[:, hk, c * 128:(c + 1) * 128], identity=cbm(C_IDENT)),
                     [f"hb{hk}", "cst"], ["b7"])
            ACT(hT[:, :, blk * 128:(blk + 1) * 128], PTt[:, :].rearrange("p (c n) -> p c n", c=8), AF.Copy,
                ["b7"], [f"hT{blk // 4}"])

        stage1(0)
        for blk in range(NB):
            if blk + 1 < NB:
                stage1(blk + 1)
            stage2(blk)

    def attn_finish(h, c, qt, sg, mixc, with_den):
        pb = (h % 2) * 64
        hs = slice(pb, pb + 64)
        num = bank(4)[hs, :]
        mres = f"mx{mixc}_{qt}"
        if with_den:
            rec = ar_rec
            ACT(rec[hs, :], bank(5)[hs, :], AF.Ln, [bres(5)], ["rec"])
            ACT(rec[hs, :], rec[hs, :], AF.Exp, ["rec"], ["rec"], scale=-1.0)
            TT("dve", rec[hs, :], num, rec[hs, :], ALU.mult, [bres(4), "rec"], ["rec"])
            TT("dve", mixT[hs, mixc, qt * 512:(qt + 1) * 512], rec[hs, :], sg[hs, c, qt * 512:(qt + 1) * 512], ALU.mult,
               ["rec", "sg"], [mres])
        else:
            TT("dve", mixT[hs, mixc, qt * 512:(qt + 1) * 512], num, sg[hs, c, qt * 512:(qt + 1) * 512], ALU.mult,
               [bres(4), "sg"], [mres])

    def gates_full(s, sg):
        for c in range(2):
            for t in range(NT):
                bk = 4 + nxt("gf") % 3
                proj_fm(s, c * 128, t, bk)
                ACT(sg[:, c, t * 512:(t + 1) * 512], bank(bk), AF.Silu, [bres(bk)], ["sg"])

    def group_A(b, l, sl, prefetch=None):
        nonlocal ar_rec
        S.barrier()
        ar.off = 0
        qT = ar.alloc([128, 2, T], BF16)
        kT = ar.alloc([128, 2, T], BF16)
        V = ar.alloc([128, NB, 256], BF16)
        sg = ar.alloc([128, 2, T], BF16)
        csp = ar.alloc([128, NB, 4], F32)
        mark = ar.off
        hb = (ar.alloc([128, 2, 512], BF16), ar.alloc([128, 2, 512], F32))
        tmpf = ar.alloc([128, 64], F32)
        tot = ar.alloc([128, 64], F32)
        inc = ar.alloc([128, 64], F32)
        units = []
        for (s_, dst, gcol, nm) in ((sl["Aq"], qT, 0, "q"), (sl["Ak"], kT, 1, "k")):
            for c in range(2):
                for t in range(NT):
                    units.append((lambda bk, s_=s_, c=c, t=t: proj_fm(s_, c * 128, t, bk), 512, pv[:, gcol:gcol + 1],
                                  dst[:, c, t * 512:(t + 1) * 512], f"A{nm}{c}_{t}"))
        hn_units(units, hb)
        slab_release(sl["Aq"])
        slab_release(sl["Ak"])
        for blk in range(NB):
            bk = 4 + nxt() % 2
            proj_tm(sl["Av"], 0, 256, blk, bank(bk)[:, 0:256], bres(bk))
            COPY("dve", V[:, blk, :], bank(bk)[:, 0:256], [bres(bk)], [f"AV{blk}"])
        slab_release(sl["Av"])
        for blk in range(NB):
            proj_tm(sl["Aff"], 0, 4, blk, bank(6)[:, blk * 4:(blk + 1) * 4], bres(6))
        slab_release(sl["Aff"])
        TT("dve", tmpf[:].rearrange("p (b h) -> p b h", h=4), bank(6)[:, 0:64].rearrange("p (b h) -> p b h", h=4),
           fb_b[:].unsqueeze(1).to_broadcast([128, NB, 4]), ALU.add, [bres(6), "fb_b"], ["tmpf"])
        gates_full(sl["Ag"], sg)
        slab_release(sl["Ag"])
        ACT(tmpf[:], tmpf[:], AF.Exp, ["tmpf"], ["tmpf"], scale=-1.0)
        ACT(tmpf[:], tmpf[:], AF.Ln, ["tmpf"], ["tmpf"], bias=1.0)
        MM(bank(0)[:, 0:64], cfm(C_TRI), tmpf[:], True, True, ["tmpf", "cst"], [bres(0)])
        MM(bank(1)[:, 0:64], cfm(C_ONES), tmpf[:], True, True, ["tmpf", "cst"], [bres(1)])
        COPY("dve", tot[:], bank(1)[:, 0:64], [bres(1)], ["tot"])
        for h in range(4):
            S.dve(lambda e, h=h: e.tensor_tensor_scan(
                out=inc[:].rearrange("p (b h) -> p b h", h=4)[:, :, h], data0=cfm(C_ONES, 16),
                data1=tot[:].rearrange("p (b h) -> p b h", h=4)[:, :, h], initial=0.0, op0=ALU.mult, op1=ALU.add),
                ["tot", "cst"], ["inc"])
        TT("dve", inc[:], inc[:], tot[:], ALU.subtract, ["inc", "tot"], ["inc"])
        TT("dve", csp[:].rearrange("p b h -> p (b h)"), bank(0)[:, 0:64], inc[:], ALU.add, [bres(0), "inc"], ["csp"])
        if prefetch is not None:
            prefetch()
        S.barrier()
        ar.off = mark
        CTn = ar.alloc([128, T], F32)
        argb = ar.alloc([128, 3, 512], F32)
        ptb = ar.alloc([128, 3, 512], BF16)
        ar_rec = ar.alloc([128, 512], F32)
        scale = 64 ** -0.5
        for h in range(4):
            c, pb = h // 2, (h % 2) * 64
            hs = slice(pb, pb + 64)
            for q4 in range(4):
                bk = 4 + q4 % 2
                for i in range(4):
                    blk = q4 * 4 + i
                    MM(bank(bk)[:, i * 128:(i + 1) * 128], csp[:, blk, h:h + 1].to_broadcast([128, 128]), cfm(C_NEGID),
                       True, True, ["csp", "cst"], [bres(bk)])
                ACT(CTn[:, q4 * 512:(q4 + 1) * 512], bank(bk), AF.Copy, [bres(bk)], ["CTn"])
            iters = [(qt, kb) for qt in range(NT) for kb in range(4 * qt + 4)]
            LA = 2

            def emit_S(i):
                qt, kb = iters[i]
                off = max(0, kb - 4 * qt) * 128
                sbk = i % 4
                MM(bank(sbk)[:, off:512], kT[hs, c, kb * 128:(kb + 1) * 128], qT[hs, c, qt * 512 + off:(qt + 1) * 512],
                   True, True, [f"Ak{c}_{kb // 4}", f"Aq{c}_{qt}"], [bres(sbk)])

            for i in range(min(LA, len(iters))):
                emit_S(i)
            for i, (qt, kb) in enumerate(iters):
                if i + LA < len(iters):
                    emit_S(i + LA)
                nk = 4 * qt + 4
                off = max(0, kb - 4 * qt) * 128
                sbk = i % 4
                k3 = i % 3
                STT("dve", argb[:, k3, off:512], bank(sbk)[:, off:512], scale, CTn[:, qt * 512 + off:(qt + 1) * 512],
                    ALU.mult, ALU.add, [bres(sbk), "CTn"], [f"arg{k3}"])
                if kb >= 4 * qt:
                    TT("dve", argb[:, k3, off:off + 128], argb[:, k3, off:off + 128], cfm(C_NEGM), ALU.add,
                       [f"arg{k3}", "cst"], [f"arg{k3}"])
                ACT(ptb[:, k3, off:512], argb[:, k3, off:512], AF.Exp, [f"arg{k3}", "csp"], [f"pt{k3}"],
                    bias=csp[:, kb, h:h + 1])
                MM(bank(4)[hs, off:512], V[:, kb, h * 64:(h + 1) * 64], ptb[:, k3, off:512], kb == 0, kb == nk - 1,
                   [f"AV{kb}", f"pt{k3}"], [bres(4)])
                MM(bank(5)[hs, off:512], cbm(C_ONES, 64), ptb[:, k3, off:512], kb == 0, kb == nk - 1,
                   [f"pt{k3}", "cst"], [bres(5)])
                if kb == nk - 1:
                    attn_finish(h, c, qt, sg, c, True)

    def group_B(b, l, sl, prefetch=None):
        S.barrier()
        ar.off = 0
        qT = ar.alloc([128, 2, T], BF16)
        kT = ar.alloc([128, 2, T], BF16)
        V = ar.alloc([128, NB, 256], BF16)
        sg = ar.alloc([128, 2, T], BF16)
        eb = ar.alloc([128, 2, 512], F32)
        spb = ar.alloc([128, 3, 512], BF16)
        atb = ar.alloc([128, 2, 512], BF16)
        acc = ar.alloc([128, 512], BF16)
        z64 = ar.alloc([128, 64], BF16)
        MEMSET("dve", z64[:], 0.0, ["z64"])
        scale = 64 ** -0.5
        for (s, dst, nm, scl) in ((sl["Bq"], qT, "q", scale), (sl["Bk"], kT, "k", 1.0)):
            for c in range(2):
                for t in range(NT):
                    bk = nxt() % 2
                    proj_fm(s, c * 128, t, bk)
                    ACT(dst[:, c, t * 512:(t + 1) * 512], bank(bk), AF.Copy, [bres(bk)], [f"B{nm}{c}_{t}"], scale=scl)
            slab_release(s)
        for blk in range(NB):
            bk = 4 + nxt() % 2
            proj_tm(sl["Bv"], 0, 256, blk, bank(bk)[:, 0:256], bres(bk))
            COPY("dve", V[:, blk, :], bank(bk)[:, 0:256], [bres(bk)], [f"BV{blk}"])
        slab_release(sl["Bv"])
        gates_full(sl["Bg"], sg)
        slab_release(sl["Bg"])
        if prefetch is not None:
            prefetch()
        for h in range(4):
            c, pb = h // 2, (h % 2) * 64
            hs = slice(pb, pb + 64)
            iters = [(qt, kb) for qt in range(NT) for kb in range(4 * qt + 3, -1, -1)]
            n_it = len(iters)

            def geo(i):
                qt, kb = iters[i]
                off = max(0, kb - 4 * qt) * 128
                return qt, kb, off, kT[hs, c, kb * 128:(kb + 1) * 128], qT[hs, c, qt * 512 + off:(qt + 1) * 512], \
                    [f"Bk{c}_{kb // 4}", f"Bq{c}_{qt}"]

            def emit_Z(i):
                qt, kb, off, ksl, qsl, kq = geo(i)
                MM(bank(i % 3)[:, off:512], ksl, qsl, True, True, kq, [bres(i % 3)])

            def emit_E(i):
                qt, kb, off, ksl, qsl, kq = geo(i)
                za, k2, k3 = i % 3, i % 2, i % 3
                ACT(eb[:, k2, off:512], bank(za)[:, off:512], AF.Exp, [bres(za)], [f"e{k2}"])
                ACT(spb[:, k3, off:512], eb[:, k2, off:512], AF.Ln, [f"e{k2}"], [f"sp{k3}"], bias=1.0)
                if kb >= 4 * qt:
                    TT("pool", spb[:, k3, off:off + 128], spb[:, k3, off:off + 128], cbm(C_MLT), ALU.mult,
                       [f"sp{k3}", "cst"], [f"sp{k3}"])

            emit_Z(0)
            if n_it > 1:
                emit_Z(1)
            emit_E(0)
            for i in range(n_it):
                qt, kb, off, ksl, qsl, kq = geo(i)
                if i + 2 < n_it:
                    emit_Z(i + 2)
                if i + 1 < n_it:
                    emit_E(i + 1)
                if kb == 4 * qt + 3:
                    MEMSET("pool", acc[:], 0.0, ["acc"])
                    MM(bank(4)[hs, :], z64[:], qT[:, c, qt * 512:(qt + 1) * 512], True, False, ["z64", f"Bq{c}_{qt}"], [bres(4)])
                k3, k2 = i % 3, i % 2
                zb = 3 if k2 == 0 else 5
                diag = kb >= 4 * qt
                MM(bank(zb)[:, off:512], cbm(C_NUINCL), spb[:, k3, off:512], True, False, [f"sp{k3}", "cst"], [bres(zb)])
                MM(bank(zb)[:, off:512], cbm(C_NONES), acc[:, off:512], False, False, ["acc", "cst"], [bres(zb)])
                MM(bank(zb)[:, off:512], ksl, qsl, False, True, kq, [bres(zb)])
                ACT(atb[:, k2, off:512], bank(zb)[:, off:512], AF.Exp, [bres(zb)], [f"at{k2}"])
                if diag:
                    TT("pool", atb[:, k2, off:off + 128], atb[:, k2, off:off + 128], cbm(C_MLT), ALU.mult,
                       [f"at{k2}", "cst"], [f"at{k2}"])
                if kb > 0:
                    TT("dve", acc[:, off:512], acc[:, off:512], spb[:, k3, off:512], ALU.add, ["acc", f"sp{k3}"], ["acc"])
                MM(bank(4)[hs, off:512], V[:, kb, h * 64:(h + 1) * 64], atb[:, k2, off:512], False, kb == 0,
                   [f"BV{kb}", f"at{k2}"], [bres(4)])
                if kb == 0:
                    attn_finish(h, c, qt, sg, 2 + c, False)

    def group_C(b, l, sl):
        S.barrier()
        ar.off = 0
        Sall = ar.alloc([128, 9, 2, 64], F32)
        MEMSET("dve", Sall[:, 0], 0.0, ["S0"])
        mark = ar.off
        for t in range(NT):
            ar.off = mark
            e_s = ar.alloc([128, 1024], F32)
            f_s = ar.alloc([128, 1024], F32)
            k_s = ar.alloc([128, 1024], F32)
            lf_hi = ar.alloc([128, 1024], BF16)
            lf_lo = ar.alloc([128, 1024], BF16)
            khat = ar.alloc([128, 4, 256], BF16)
            Eq = ar.alloc([128, 2, 512], F32)
            Ek = ar.alloc([128, 2, 512], F32)
            sq_s = ar.alloc([128, 2, 512], F32)
            qt_ = ar.alloc([128, 2, 512], BF16)
            kt_ = ar.alloc([128, 2, 512], BF16)
            Vc = ar.alloc([128, 4, 256], BF16)
            scm = ar.alloc([128, 16, 64], BF16)
            Sb = ar.alloc([128, 8, 2, 64], BF16)
            sqo = ar.alloc([128, 1024], BF16)
            sgc = ar.alloc([128, 512], F32)
            P0, P1, P2 = PD[0], PD[1], PD[2]
            b01, b23, b45 = [bres(0), bres(1)], [bres(2), bres(3)], [bres(4), bres(5)]
            for i in range(4):
                proj_tm(sl["Cf"], 0, 256, 4 * t + i, P0[:, i * 256:(i + 1) * 256], bres(i // 2))
            ACT(e_s[:], P0[:, :], AF.Exp, b01, ["e_s"], scale=-1.0)
            ACT(e_s[:], e_s[:], AF.Ln, ["e_s"], ["e_s"], bias=1.0)
            ACT(e_s[:], e_s[:], AF.Exp, ["e_s"], ["e_s"], scale=-1.0)
            omb = oml_b[:].unsqueeze(1).to_broadcast([128, 4, 256])
            lbb = lb_b[:].unsqueeze(1).to_broadcast([128, 4, 256])
            v3 = lambda a: a.rearrange("p (i n) -> p i n", i=4)
            TT("dve", v3(e_s[:]), v3(e_s[:]), omb, ALU.mult, ["e_s", "lb"], ["e_s"])
            TT("dve", v3(f_s[:]), v3(e_s[:]), lbb, ALU.add, ["e_s", "lb"], ["f_s"])
            TT("dve", v3(k_s[:]), omb, v3(e_s[:]), ALU.subtract, ["e_s", "lb"], ["k_s"])
            ACT(f_s[:], f_s[:], AF.Ln, ["f_s"], ["f_s"])
            COPY("dve", lf_hi[:], f_s[:], ["f_s"], ["lf_hi"])
            TT("dve", lf_lo[:], f_s[:], lf_hi[:], ALU.subtract, ["f_s", "lf_hi"], ["lf_lo"])
            if CSTOP < 2:
                continue
            for i in range(4):
                o = P1[:, i * 256:(i + 1) * 256]
                MM(o, cbm(C_TBDU), lf_hi[:, i * 256:(i + 1) * 256], True, False, ["lf_hi", "cst"], [bres(2 + i // 2)])
                MM(o, cbm(C_TBDU), lf_lo[:, i * 256:(i + 1) * 256], False, True, ["lf_lo", "cst"], [bres(2 + i // 2)])
            ACT(e_s[:], P1[:, :], AF.Exp, b23, ["e_s"])
            TT("dve", khat[:].rearrange("p i n -> p (i n)"), k_s[:], e_s[:], ALU.mult, ["k_s", "e_s"], ["khat"])
            for c in range(2):
                for i in range(4):
                    o = P2[:, c * 512 + i * 128:c * 512 + (i + 1) * 128]
                    MM(o, lf_hi[:, i * 256 + c * 128:i * 256 + (c + 1) * 128], cbm(C_TBD), True, False,
                       ["lf_hi", "cst"], [bres(4 + c)])
                    MM(o, lf_lo[:, i * 256 + c * 128:i * 256 + (c + 1) * 128], cbm(C_TBD), False, True,
                       ["lf_lo", "cst"], [bres(4 + c)])
            ACT(Eq[:].rearrange("p c n -> p (c n)"), P2[:, :], AF.Exp, b45, ["Eq"])
            ACT(Ek[:].rearrange("p c n -> p (c n)"), P2[:, :], AF.Exp, b45, ["Ek"], scale=-1.0)
            if CSTOP < 3:
                continue
            for c in range(2):
                bk = 6
                proj_fm(sl["Cq"], c * 128, t, bk)
                ACT(sq_s[:, c, :], bank(bk), AF.Silu, [bres(bk)], [f"sq_s{c}"])
                TT("dve", qt_[:, c, :], sq_s[:, c, :], Eq[:, c, :], ALU.mult, [f"sq_s{c}", "Eq"], ["qt_"])
            for c in range(2):
                bk = 6
                proj_fm(sl["Cf"], c * 128, t, bk)
                ACT(sq_s[:, c, :], bank(bk), AF.Exp, [bres(bk)], [f"sq_s{c}"])
                ACT(sq_s[:, c, :], sq_s[:, c, :], AF.Ln, [f"sq_s{c}"], [f"sq_s{c}"], bias=1.0)
                ACT(sq_s[:, c, :], sq_s[:, c, :], AF.Exp, [f"sq_s{c}"], [f"sq_s{c}"], scale=-1.0)
                STT("dve", kt_[:, c, :], sq_s[:, c, :], lbp[:, 2 + c:3 + c], Ek[:, c, :], ALU.mult, ALU.mult,
                    [f"sq_s{c}", "Ek", "lb"], ["kt_"])
            if CSTOP < 4:
                continue
            for i in range(4):
                proj_tm(sl["Ci"], 0, 256, 4 * t + i, P0[:, i * 256:(i + 1) * 256], bres(i // 2))
            COPY("dve", Vc[:].rearrange("p i n -> p (i n)"), P0[:, :], b01, ["Vc"])
            if CSTOP < 5:
                continue
            for hp in range(2):
                pb = hp * 64
                for ch in range(8):
                    i, half = ch // 2, ch % 2
                    for j in range(2):
                        slot = i * 2 + j
                        MM(P1[half * 64:(half + 1) * 64, hp * 512 + slot * 64:hp * 512 + (slot + 1) * 64],
                           kt_[pb:pb + 64, j, ch * 64:(ch + 1) * 64], qt_[pb:pb + 64, j, ch * 64:(ch + 1) * 64],
                           True, True, ["kt_", "qt_"], [bres(2 + hp)])
            TT("dve", scm[:], P1[:, :].rearrange("p (s n) -> p s n", n=64),
               cb[:, C_MASK64:C_MASK64 + 64].unsqueeze(1).to_broadcast([128, 16, 64]), ALU.mult, b23 + ["cst"], ["scm"])
            if CSTOP < 6:
                continue
            def uslot(ch, j):
                return (ch % 2) * 512 + ((ch // 2) * 2 + j) * 64
            for half in range(2):
                for ch in range(half, 8, 2):
                    i = ch // 2
                    for h in range(4):
                        j = h // 2
                        o0 = uslot(ch, j)
                        MM(P0[(h % 2) * 64:(h % 2) * 64 + 64, o0:o0 + 64],
                           khat[half * 64:(half + 1) * 64, i, h * 64:(h + 1) * 64],
                           Vc[half * 64:(half + 1) * 64, i, h * 64:(h + 1) * 64], True, True, ["khat", "Vc"], [bres(half)])
            if CSTOP < 7:
                continue
            for ch in range(8):
                for j in range(2):
                    o0 = uslot(ch, j)
                    STT("dve", Sall[:, ch + 1, j, :], Sall[:, ch, j, :], Eq[:, j, ch * 64 + 63:ch * 64 + 64],
                        P0[:, o0:o0 + 64], ALU.mult, ALU.add, [f"S{ch}", "Eq", bres(ch % 2)], [f"S{ch + 1}"])
            ACT(Sb[:].rearrange("p a b n -> p (a b n)"), Sall[:, 0:8].rearrange("p a b n -> p (a b n)"), AF.Copy,
                [f"S{k}" for k in range(8)], ["Sb"])
            COPY("dve", Sall[:, 0].rearrange("p b n -> p (b n)"), Sall[:, 8].rearrange("p b n -> p (b n)"), ["S8"], ["S0"])
            if CSTOP < 8:
                continue
            for hp in range(2):
                pb = hp * 64
                for j in range(2):
                    for ch in range(8):
                        MM(P2[pb:pb + 64, j * 512 + ch * 64:j * 512 + (ch + 1) * 64], Sb[pb:pb + 64, ch, j, :],
                           qt_[pb:pb + 64, j, ch * 64:(ch + 1) * 64], True, True, ["Sb", "qt_"], [bres(4 + j)])
            for half in range(2):
                for ch in range(half, 8, 2):
                    i = ch // 2
                    for h in range(4):
                        j, hp = h // 2, h % 2
                        o0 = half * 512 + (j * 4 + i) * 64
                        MM(P0[hp * 64:(hp + 1) * 64, o0:o0 + 64],
                           Vc[half * 64:(half + 1) * 64, i, h * 64:(h + 1) * 64],
                           scm[half * 64:(half + 1) * 64, hp * 8 + i * 2 + j, :], True, True, ["Vc", "scm"], [bres(half)])
            for half in range(2):
                COPY("dve", k_s[:].rearrange("p (j i h t) -> p j i h t", j=2, i=4, h=2)[:, :, :, half, :],
                     P0[:, half * 512:(half + 1) * 512].rearrange("p (j i t) -> p j i t", j=2, i=4), [bres(half)], ["k_s"])
            TT("dve", f_s[:], P2[:, :], k_s[:], ALU.add, b45 + ["k_s"], ["f_s"])
            if CSTOP < 9:
                continue
            ACT(sqo[:], f_s[:], AF.Square, ["f_s"], ["sqo"])
            for j in range(2):
                MM(bank(2 + j), cbm(C_BONES), sqo[:, j * 512:(j + 1) * 512], True, True, ["sqo", "cst"], [bres(2 + j)])
            ACT(e_s[:], P1[:, :], AF.Ln, b23, ["e_s"], bias=EPS, scale=1.0 / 64)
            ACT(e_s[:], e_s[:], AF.Exp, ["e_s"], ["e_s"], scale=-0.5)
            for j in range(2):
                STT("dve", f_s[:, j * 512:(j + 1) * 512], f_s[:, j * 512:(j + 1) * 512], hn[:, j:j + 1],
                    e_s[:, j * 512:(j + 1) * 512], ALU.mult, ALU.mult, ["f_s", "e_s", "hn"], ["f_s"])
                proj_fm(sl["Cg"], j * 128, t, 6)
                ACT(sgc[:], bank(6), AF.Silu, [bres(6)], ["sgc"])
                TT("dve", mixT[:, 4 + j, t * 512:(t + 1) * 512], f_s[:, j * 512:(j + 1) * 512], sgc[:], ALU.mult,
                   ["f_s", "sgc"], [f"mx{4 + j}_{t}"])
        for k in ("Cq", "Cf", "Ci", "Cg"):
            slab_release(sl[k])

    def group_D(b, l, sl):
        S.barrier()
        ar.off = 0
        PADW = 16 + T
        sgd = ar.alloc([128, 512], F32)
        for c in range(2):
            ar.off = 2048
            u = ar.alloc([128, PADW], F32)
            A1 = ar.alloc([128, PADW], F32)
            A2 = ar.alloc([128, PADW], F32)
            dfb = ar.alloc([128, T], BF16)
            tmp = ar.alloc([128, 16], F32)
            for a in (u, A1, A2):
                MEMSET("pool", a[:, 0:16], 0.0, ["dpad"])
            for t in range(NT):
                bk = nxt() % 2
                proj_fm(sl["Dv"], c * 128, t, bk)
                ACT(u[:, 16 + t * 512:16 + (t + 1) * 512], bank(bk), AF.Copy, [bres(bk)], ["u"])
            lo, hi = slice(0, 64), slice(64, 128)
            TT("pool", A1[:, 16:], u[:, 16:], u[:, 15:PADW - 1], ALU.add, ["u", "dpad"], ["A1"])
            if c == 0:
                TT("pool", A2[hi, 16:], A1[hi, 16:], A1[hi, 14:PADW - 2], ALU.add, ["A1", "dpad"], ["A2"])
                fins = ((lo, A1), (hi, A2))
            else:
                TT("pool", A2[:, 16:], A1[:, 16:], A1[:, 14:PADW - 2], ALU.add, ["A1", "dpad"], ["A2"])
                TT("pool", A1[:, 16:], A2[:, 16:], A2[:, 12:PADW - 4], ALU.add, ["A2", "dpad"], ["A1"])
                TT("pool", A2[hi, 16:], A1[hi, 16:], A1[hi, 8:PADW - 8], ALU.add, ["A1", "dpad"], ["A2"])
                fins = ((lo, A1), (hi, A2))
            for (hs, fin) in fins:
                STT("dve", dfb[hs, :], fin[hs, 16:], cf[hs, C_INVW + c:C_INVW + c + 1], u[hs, 16:], ALU.mult, ALU.subtract,
                    ["A1", "A2", "u", "cst"], ["dfb"])
                TT("dve", tmp[hs, :], fin[hs, 16:32], cf[hs, C_RC + c * 16:C_RC + (c + 1) * 16], ALU.mult,
                   ["A1", "A2", "cst"], ["dtmp"])
                TT("dve", dfb[hs, 0:16], tmp[hs, :], u[hs, 16:32], ALU.subtract, ["dtmp", "u", "dfb"], ["dfb"])
            for t in range(NT):
                bk = 2 + nxt() % 2
                MM(bank(bk), pwb[:, c, :], dfb[:, t * 512:(t + 1) * 512], True, True, ["dfb", "pwb"], [bres(bk)])
                proj_fm(sl["Dg"], c * 128, t, 6)
                ACT(sgd[:], bank(6), AF.Silu, [bres(6)], ["sgd"])
                STT("dve", mixT[:, 6 + c, t * 512:(t + 1) * 512], bank(bk), psc[:, c:c + 1], sgd[:], ALU.mult, ALU.mult,
                    [bres(bk), "sgd", "psc"], [f"mx{6 + c}_{t}"])
        slab_release(sl["Dv"])
        slab_release(sl["Dg"])

    def group_E(b, l, sl):
        nonlocal ar_rec
        S.barrier()
        ar.off = 0
        qm = ar.alloc([128, 2, T], BF16)
        hb = (ar.alloc([128, 2, 512], BF16), ar.alloc([128, 2, 512], F32))
        mt = ar.alloc([128, 2, D], F32)
        mnb = ar.alloc([128, D], BF16)
        mnT = ar.alloc([128, 8, 256], BF16)
        mss = ar.alloc([128, 4], F32)
        ptb = ar.alloc([128, 3, 512], BF16)
        ar_rec = ar.alloc([128, 512], F32)
        sge = ar.alloc([128, 512], F32)
        for mb in range(2):
            S.dma("sp", f"memld{mb}", mt[:, mb, :], mem[b, mb * 128:(mb + 1) * 128, :], writes=[f"mt{mb}"])
            sc, rc_ = mss[:, mb:mb + 1], mss[:, 2 + mb:3 + mb]
            ACT(sqj[:], mt[:, mb, :], AF.Square, [f"mt{mb}"], ["sqj", "mss"], accum_out=sc)
            ACT(rc_, sc, AF.Ln, ["mss"], ["mrs"], bias=EPS, scale=1.0 / D)
            ACT(rc_, rc_, AF.Exp, ["mrs"], ["mrs"], scale=-0.5)
            STT("dve", mnb[:], mt[:, mb, :], rc_, mg_b[:], ALU.mult, ALU.mult, [f"mt{mb}", "mrs", "mg_b"], ["mnb"])
            for c in range(8):
                S.pe(lambda e, c=c: e.transpose(out=PTt[:, c * 128:(c + 1) * 128], in_=mnb[:, c * 128:(c + 1) * 128],
                                                identity=cbm(C_IDENT)), ["mnb", "cst"], ["b7"])
            ACT(mnT[:, :, mb * 128:(mb + 1) * 128], PTt[:, :].rearrange("p (c n) -> p c n", c=8), AF.Copy, ["b7"], ["mnT"])
        for c in range(2):
            bk = nxt() % 2
            for kc in range(8):
                MM(bank(bk)[:, 0:256], sl["mK"].ap[:, kc, c * 128:(c + 1) * 128], mnT[:, kc, :], kc == 0, kc == 7,
                   [sl["mK"].res, "mnT"], [bres(bk)])
            headnorm(bk, 256, pv[:, 3:4], kmT[:, c, :], "kmT", 2 + bk, hb)
        slab_release(sl["mK"])
        for mb in range(2):
            bk = 4 + mb
            for kc in range(8):
                MM(bank(bk)[:, 0:256], mnT[:, kc, mb * 128:(mb + 1) * 128], sl["mV"].ap[:, kc, 0:256], kc == 0, kc == 7,
                   [sl["mV"].res, "mnT"], [bres(bk)])
            COPY("dve", Vm[:, mb, :], bank(bk)[:, 0:256], [bres(bk)], ["Vm"])
        slab_release(sl["mV"])
        units = []
        for c in range(2):
            for t in range(NT):
                units.append((lambda bk, c=c, t=t: proj_fm(sl["Eq"], c * 128, t, bk), 512, pv[:, 2:3],
                              qm[:, c, t * 512:(t + 1) * 512], f"Eq{c}_{t}"))
        hn_units(units, hb)
        slab_release(sl["Eq"])
        scale = 64 ** -0.5
        iters = [(c, t, hh, mb) for c in range(2) for t in range(NT) for hh in range(2) for mb in range(2)]

        def emit_SE(i):
            c, t, hh, mb = iters[i]
            hs = slice(hh * 64, hh * 64 + 64)
            MM(bank(i % 4), kmT[hs, c, mb * 128:(mb + 1) * 128], qm[hs, c, t * 512:(t + 1) * 512], True, True,
               ["kmT", f"Eq{c}_{t}"], [bres(i % 4)])

        emit_SE(0)
        emit_SE(1)
        for i, (c, t, hh, mb) in enumerate(iters):
            if i + 2 < len(iters):
                emit_SE(i + 2)
            h = 2 * c + hh
            hs = slice(hh * 64, hh * 64 + 64)
            k3 = i % 3
            ACT(ptb[:, k3, :], bank(i % 4), AF.Exp, [bres(i % 4)], [f"pt{k3}"], scale=scale)
            MM(bank(4)[hs, :], Vm[:, mb, h * 64:(h + 1) * 64], ptb[:, k3, :], mb == 0, mb == 1, ["Vm", f"pt{k3}"], [bres(4)])
            MM(bank(5)[hs, :], cbm(C_ONES, 64), ptb[:, k3, :], mb == 0, mb == 1, [f"pt{k3}", "cst"], [bres(5)])
            if hh == 1 and mb == 1:
                ACT(ar_rec[:], bank(5), AF.Ln, [bres(5)], ["rec"])
                ACT(ar_rec[:], ar_rec[:], AF.Exp, ["rec"], ["rec"], scale=-1.0)
                TT("dve", ar_rec[:], bank(4), ar_rec[:], ALU.mult, [bres(4), "rec"], ["rec"])
                proj_fm(sl["Eg"], c * 128, t, 6)
                ACT(sge[:], bank(6), AF.Silu, [bres(6)], ["sge"])
                TT("dve", mixT[:, 8 + c, t * 512:(t + 1) * 512], ar_rec[:], sge[:], ALU.mult, ["rec", "sge"], [f"mx{8 + c}_{t}"])
        slab_release(sl["Eg"])

    def wout_phase(b, l, fuse_rms=False):
        S.barrier()
        src = x[b] if l == 0 else xs
        if l < NLAYER - 1:
            dst = xs
        else:
            dst = y[b]

        def rms2(blk):
            hk = blk % 2
            for c in range(8):
                S.pe(lambda e, c=c, hk=hk: e.transpose(out=PTt[:, c * 128:(c + 1) * 128],
                                                       in_=hbuf[:, hk, c * 128:(c + 1) * 128], identity=cbm(C_IDENT)),
                     [f"hb{hk}", "cst"], ["b7"])
            ACT(hT[:, :, blk * 128:(blk + 1) * 128], PTt[:, :].rearrange("p (c n) -> p c n", c=8), AF.Copy,
                ["b7"], [f"hT{blk // 4}"])

        def xload(blk):
            k = blk % 2
            rd = [] if l == 0 else [f"xs{blk}"]
            S.dma("sp", f"xst{k}", xst[:, k, :], src[blk * 128:(blk + 1) * 128, :], reads=rd, writes=[f"xst{k}"])

        xload(0)
        for blk in range(NB):
            k = blk % 2
            xr = f"xst{k}"
            if blk + 1 < NB:
                xload(blk + 1)
            for dh in range(2):
                bk = nxt() % 4
                for kc in range(10):
                    MM(bank(bk), mixT[:, kc, blk * 128:(blk + 1) * 128], Wo[:, kc, dh * 512:(dh + 1) * 512], kc == 0, kc == 9,
                       [f"mx{kc}_{blk // 4}", "Wo"], [bres(bk)])
                TT("dve", xst[:, k, dh * 512:(dh + 1) * 512], xst[:, k, dh * 512:(dh + 1) * 512], bank(bk), ALU.add,
                   [xr, bres(bk)], [xr])
            if l < NLAYER - 1:
                S.dma("sp", "xout", dst[blk * 128:(blk + 1) * 128, :], xst[:, k, :], reads=[xr], writes=[f"xs{blk}"])
            else:
                S.dma("sp", "xout", dst[blk * 128:(blk + 1) * 128, :], xst[:, k, :], reads=[xr], writes=["yout"])
            if fuse_rms:
                sc = ssb[:, blk:blk + 1]
                rc_ = ssb[:, 16 + blk:17 + blk]
                ACT(sqj[:], xst[:, k, :], AF.Square, [xr], ["sqj", f"ss{blk}"], accum_out=sc)
                ACT(rc_, sc, AF.Ln, [f"ss{blk}"], [f"rs{blk}"], bias=EPS, scale=1.0 / D)
                ACT(rc_, rc_, AF.Exp, [f"rs{blk}"], [f"rs{blk}"], scale=-0.5)
                hk = blk % 2
                STT("dve", hbuf[:, hk, :], xst[:, k, :], rc_, g_b[:], ALU.mult, ALU.mult,
                    [xr, f"rs{blk}", "g_b"], [f"hb{hk}"])
                if blk > 0:
                    rms2(blk - 1)
        if fuse_rms:
            rms2(NB - 1)

    ar_rec = None
    if dbg:
        MEMSET("dve", mixT[:], 0.0, [f"mx{c}_{t}" for c in range(10) for t in range(NT)])
    for b in range(NSEQ):
        for l in range(NLAYER):
            if l == 0:
                load_params(l)
            sl = {}
            for (k, c0, n) in (("Aq", 0, 256), ("Ak", 256, 256), ("Av", 512, 256), ("Aff", 1024, 4), ("Ag", 768, 256)):
                sl[k] = slab_load(win_cols(l, c0, n), n)
            if l == 0:
                rms_phase(b, l)
            def load_B(l=l, sl=sl):
                if "Bq" in sl:
                    return
                for (k, c0, n) in (("Bq", 1028, 256), ("Bk", 1284, 256), ("Bv", 1540, 256), ("Bg", 1796, 256)):
                    sl[k] = slab_load(win_cols(l, c0, n), n)

            def load_C(l=l, sl=sl):
                if "Cq" in sl:
                    return
                for (k, c0, n) in (("Cq", 2052, 256), ("Cf", 2308, 256), ("Ci", 2564, 256), ("Cg", 2820, 256)):
                    sl[k] = slab_load(win_cols(l, c0, n), n)
                for kc in range(10):
                    S.dma("pool", "Wo", Wo[:, kc, :], w_out[l, kc * 128:(kc + 1) * 128, :], writes=["Wo"])

            if "A" in groups:
                group_A(b, l, sl, prefetch=load_B)
            else:
                for k in ("Aq", "Ak", "Av", "Aff", "Ag"):
                    slab_release(sl[k])
            load_B()
            if "B" in groups:
                group_B(b, l, sl, prefetch=load_C)
            else:
                for k in ("Bq", "Bk", "Bv", "Bg"):
                    slab_release(sl[k])
            load_C()
            if "C" in groups:
                group_C(b, l, sl)
            else:
                for k in ("Cq", "Cf", "Ci", "Cg"):
                    slab_release(sl[k])
            for (k, c0, n) in (("Dv", 3076, 256), ("Dg", 3332, 256)):
                sl[k] = slab_load(win_cols(l, c0, n), n)
            sl["mK"] = slab_load(wkv_cols(l, 0, 256), 256)
            sl["mV"] = slab_load(wkv_cols(l, 256, 256), 256)
            if "D" in groups:
                group_D(b, l, sl)
            else:
                slab_release(sl["Dv"])
                slab_release(sl["Dg"])
            for (k, c0, n) in (("Eq", 3588, 256), ("Eg", 3844, 256)):
                sl[k] = slab_load(win_cols(l, c0, n), n)
            if "E" in groups:
                group_E(b, l, sl)
            else:
                for k in ("mK", "mV", "Eq", "Eg"):
                    slab_release(sl[k])
            if l + 1 < NLAYER:
                S.barrier()
                load_params(l + 1)
                wout_phase(b, l, fuse_rms=True)
            else:
                wout_phase(b, l)
    fin_reads = ["yout"]
    if dbg:
        S.barrier()
        S.dma("sp", "dbg", dbg_out, mixT[:], reads=[f"mx{c}_{t}" for c in range(10) for t in range(NT)], writes=["dbgo"])
        fin_reads.append("dbgo")
    S.add("sp", None, reads=fin_reads)
    S.emit(st)
    st.close()
    nc._arena_peak = ar.peak
    return nc


_CACHE = {}


def kernel(x, mem, norm_g, w_in, fox_f_bias, fox_q_norm, fox_k_norm, hgrn_lb_logits, hgrn_out_norm, pool_w,
           pool_scale, mem_norm_g, mem_w_kv, mem_q_norm, mem_k_norm, w_out):
    ncores = 8
    f = lambda a: np.ascontiguousarray(np.asarray(a, dtype=np.float32))
    x = f(x)
    mem = f(mem)
    B = x.shape[0]
    per = B // ncores
    if "nc" not in _CACHE:
        _CACHE["nc"] = build_nc(per, 2)
    nc = _CACHE["nc"]
    shared = dict(norm_g=f(norm_g), w_in=f(w_in), fox_f_bias=f(fox_f_bias), fox_q_norm=f(fox_q_norm),
                  fox_k_norm=f(fox_k_norm), hgrn_lb_logits=f(hgrn_lb_logits), hgrn_out_norm=f(hgrn_out_norm),
                  pool_w=f(pool_w), pool_scale=f(pool_scale), mem_norm_g=f(mem_norm_g), mem_w_kv=f(mem_w_kv),
                  mem_q_norm=f(mem_q_norm), mem_k_norm=f(mem_k_norm), w_out=f(w_out), cpack=make_cpack())
    in_maps = []
    for i in range(ncores):
        d = dict(shared)
        d["x"] = np.ascontiguousarray(x[i * per:(i + 1) * per])
        d["mem"] = np.ascontiguousarray(mem[i * per:(i + 1) * per])
        in_maps.append(d)
    res = run_bass_kernel_spmd(nc, in_maps, core_ids=list(range(ncores)))
    return np.concatenate([np.asarray(r["y"], dtype=np.float32) for r in res.results], axis=0)
```
